# Optimizing a Trainium2 kernel written in Bass

```python
import jax, jax.numpy as jnp
from jax import lax
import numpy as np

D_MODEL = 2048
BATCH = 8
SEQ = 2048
DEPTH = 1

CHUNK = 64
N_MEM = 256
EPS = 1e-6
SB_HEAD_DIM = 128
SB_WIDTH = D_MODEL // 2
SB_HEADS = SB_WIDTH // SB_HEAD_DIM
SB_BLOCK = 128
SSM_INNER = D_MODEL // 2
SSM_HEAD_DIM = 64
SSM_HEADS = SSM_INNER // SSM_HEAD_DIM
SSM_GROUPS = 4
SSM_STATE = 128
SSM_CONV = 4
SSM_CHUNK = CHUNK
CONV_DIM = SSM_INNER + 2 * SSM_GROUPS * SSM_STATE
MIX_WIDTH = SB_WIDTH + SSM_INNER
IN_PROJ_DIM = 3 * SB_WIDTH + SSM_INNER + CONV_DIM + SSM_HEADS
MEM_HEADS = 4
MEM_HEAD_DIM = 128
MEM_WIDTH = MEM_HEADS * MEM_HEAD_DIM
D_FF = ((8 * D_MODEL // 3 + 127) // 128) * 128

kernel_name = "hybrid_sb_ssd_macaron_memory_layer"


def rms_norm(x, w):
    x32 = x.astype(jnp.float32)
    y = x32 * lax.rsqrt(jnp.mean(x32 * x32, axis=-1, keepdims=True) + EPS)
    return (y * w.astype(jnp.float32)).astype(x.dtype)


def swiglu(x, wg, wu, wd):
    return (jax.nn.silu(x @ wg) * (x @ wu)) @ wd


def stick_breaking(q, k, v):
    b, h, s, d = q.shape
    nblk = s // SB_BLOCK
    scale = d ** -0.5
    qb = q.reshape(b, h, nblk, SB_BLOCK, d).transpose(2, 0, 1, 3, 4)
    k32 = k.astype(jnp.float32)
    kpos = jnp.arange(s)

    def block(args):
        i, qi = args
        z = jnp.einsum('bhqd,bhkd->bhqk', qi.astype(jnp.float32), k32) * scale
        qpos = i * SB_BLOCK + jnp.arange(SB_BLOCK)
        mask = kpos[None, :] < qpos[:, None]
        log_keep = jnp.where(mask, jax.nn.log_sigmoid(-z), 0.0)
        tail = lax.cumsum(log_keep, axis=3, reverse=True)
        tail_excl = jnp.concatenate([tail[..., 1:], jnp.zeros_like(tail[..., :1])], axis=-1)
        log_a = jnp.where(mask, jax.nn.log_sigmoid(z) + tail_excl, -jnp.inf)
        a = jnp.exp(log_a)
        return jnp.einsum('bhqk,bhkd->bhqd', a.astype(v.dtype), v)

    out = lax.map(block, (jnp.arange(nblk), qb))
    return out.transpose(1, 2, 0, 3, 4).reshape(b, h, s, d)


def causal_dwconv(x, w, bias):
    kw = w.shape[0]
    y = lax.conv_general_dilated(x, w[:, None, :], window_strides=(1,), padding=[(kw - 1, 0)],
                                 dimension_numbers=('NWC', 'WIO', 'NWC'),
                                 feature_group_count=x.shape[-1])
    return y + bias


def segsum_exp(a):
    l = a.shape[-1]
    cs = jnp.cumsum(a, axis=-1)
    diff = cs[..., :, None] - cs[..., None, :]
    mask = jnp.tril(jnp.ones((l, l), dtype=bool))
    return jnp.exp(jnp.where(mask, diff, -jnp.inf))


def ssd(x, dt, a, bm, cm):
    b, s, h, p = x.shape
    g, n = bm.shape[2], bm.shape[3]
    r = h // g
    l = SSM_CHUNK
    c = s // l
    xd = (x * dt[..., None]).reshape(b, c, l, g, r, p)
    da = (dt * a).reshape(b, c, l, g, r).transpose(0, 1, 3, 4, 2)
    bm = bm.reshape(b, c, l, g, n)
    cm = cm.reshape(b, c, l, g, n)
    da_cs = jnp.cumsum(da, axis=-1)
    decay = segsum_exp(da)
    cb = jnp.einsum('bclgn,bcsgn->bcgls', cm, bm)
    y_diag = jnp.einsum('bcgrls,bcsgrp->bclgrp', cb[:, :, :, None] * decay, xd)
    decay_to_end = jnp.exp(da_cs[..., -1:] - da_cs)
    states = jnp.einsum('bclgn,bcgrl,bclgrp->bcgrpn', bm, decay_to_end, xd)
    chunk_decay = jnp.exp(da_cs[..., -1])

    def step(carry, inp):
        st, dec = inp
        return carry * dec[..., None, None] + st, carry

    init = jnp.zeros((b, g, r, p, n), jnp.float32)
    _, prev = lax.scan(step, init, (states.transpose(1, 0, 2, 3, 4, 5), chunk_decay.transpose(1, 0, 2, 3)))
    prev = prev.transpose(1, 0, 2, 3, 4, 5)
    y_off = jnp.einsum('bclgn,bcgrpn,bcgrl->bclgrp', cm, prev, jnp.exp(da_cs))
    return (y_diag + y_off).reshape(b, s, h, p)


def parallel_mixer(h, pre, w_in, conv_w, conv_b, dt_bias, a_log, d_skip, sb_norm, ssm_norm, w_out, post):
    b, s, _ = h.shape
    u = rms_norm(h, pre)
    proj = u @ w_in
    q, k, v, z, xbc, dt_raw = jnp.split(
        proj, np.cumsum([SB_WIDTH, SB_WIDTH, SB_WIDTH, SSM_INNER, CONV_DIM]).tolist(), axis=-1)
    heads = lambda t: t.reshape(b, s, SB_HEADS, SB_HEAD_DIM).transpose(0, 2, 1, 3)
    sb = stick_breaking(heads(q), heads(k), heads(v)).transpose(0, 2, 1, 3).reshape(b, s, SB_WIDTH)
    sb = rms_norm(sb, sb_norm)
    xbc = jax.nn.silu(causal_dwconv(xbc, conv_w, conv_b))
    xs, bm, cm = jnp.split(xbc, [SSM_INNER, SSM_INNER + SSM_GROUPS * SSM_STATE], axis=-1)
    xs = xs.reshape(b, s, SSM_HEADS, SSM_HEAD_DIM).astype(jnp.float32)
    bm = bm.reshape(b, s, SSM_GROUPS, SSM_STATE).astype(jnp.float32)
    cm = cm.reshape(b, s, SSM_GROUPS, SSM_STATE).astype(jnp.float32)
    dt = jax.nn.softplus(dt_raw.astype(jnp.float32) + dt_bias.astype(jnp.float32))
    a = -jnp.exp(a_log.astype(jnp.float32))
    y = ssd(xs, dt, a, bm, cm) + d_skip.astype(jnp.float32)[:, None] * xs
    y = y.reshape(b, s, SSM_INNER) * jax.nn.silu(z.astype(jnp.float32))
    y = rms_norm(y, ssm_norm).astype(h.dtype)
    out = jnp.concatenate([sb, y], axis=-1) @ w_out
    return rms_norm(out, post)


def memory_xattn(h, mem, pre, kv_norm, w_mq, w_mkv, w_mo, post):
    b, s, _ = h.shape
    u = rms_norm(h, pre)
    m = rms_norm(mem, kv_norm)
    q = (u @ w_mq).reshape(b, s, MEM_HEADS, MEM_HEAD_DIM)
    k, v = jnp.split((m @ w_mkv).reshape(b, -1, 2, MEM_HEADS, MEM_HEAD_DIM), 2, axis=2)
    k, v = k[:, :, 0], v[:, :, 0]
    scores = jnp.einsum('bshd,bmhd->bhsm', q, k).astype(jnp.float32) * (MEM_HEAD_DIM ** -0.5)
    p = jax.nn.softmax(scores, axis=-1)
    o = jnp.einsum('bhsm,bmhd->bshd', p.astype(v.dtype), v).reshape(b, s, MEM_WIDTH)
    return rms_norm(o @ w_mo, post)


def setup_inputs(seed: int = 0) -> dict:
    key = jax.random.key(seed)
    ks = iter(jax.random.split(key, 40))
    f32 = jnp.float32
    L = DEPTH

    def normal(shape, scale):
        return scale * jax.random.normal(next(ks), shape, f32)

    def gain(dim):
        return 1.0 + 0.05 * jax.random.normal(next(ks), (L, dim), f32)

    def ffn_params():
        return (gain(D_MODEL), normal((L, D_MODEL, D_FF), D_MODEL ** -0.5),
                normal((L, D_MODEL, D_FF), D_MODEL ** -0.5), normal((L, D_FF, D_MODEL), D_FF ** -0.5),
                gain(D_MODEL))

    x = normal((BATCH, SEQ, D_MODEL), 1.0)
    mem = normal((BATCH, N_MEM, D_MODEL), 1.0)
    f1 = ffn_params()
    mix_pre = gain(D_MODEL)
    w_in = normal((L, D_MODEL, IN_PROJ_DIM), D_MODEL ** -0.5)
    conv_w = normal((L, SSM_CONV, CONV_DIM), SSM_CONV ** -0.5)
    conv_b = normal((L, CONV_DIM), 0.01)
    dt0 = jnp.exp(jax.random.uniform(next(ks), (L, SSM_HEADS), f32, np.log(1e-3), np.log(1e-1)))
    dt_bias = dt0 + jnp.log(-jnp.expm1(-dt0))
    a_log = jnp.log(jax.random.uniform(next(ks), (L, SSM_HEADS), f32, 1.0, 16.0))
    d_skip = 1.0 + 0.05 * jax.random.normal(next(ks), (L, SSM_HEADS), f32)
    sb_norm = gain(SB_WIDTH)
    ssm_norm = gain(SSM_INNER)
    w_out = normal((L, MIX_WIDTH, D_MODEL), MIX_WIDTH ** -0.5)
    mix_post = gain(D_MODEL)
    mem_pre = gain(D_MODEL)
    mem_kv_norm = gain(D_MODEL)
    w_mq = normal((L, D_MODEL, MEM_WIDTH), D_MODEL ** -0.5)
    w_mkv = normal((L, D_MODEL, 2 * MEM_WIDTH), D_MODEL ** -0.5)
    w_mo = normal((L, MEM_WIDTH, D_MODEL), MEM_WIDTH ** -0.5)
    mem_post = gain(D_MODEL)
    f2 = ffn_params()
    return {"x": x, "mem": mem,
            "ffn1_pre": f1[0], "ffn1_wg": f1[1], "ffn1_wu": f1[2], "ffn1_wd": f1[3], "ffn1_post": f1[4],
            "mix_pre": mix_pre, "w_in": w_in, "conv_w": conv_w, "conv_b": conv_b, "dt_bias": dt_bias,
            "a_log": a_log, "d_skip": d_skip, "sb_norm": sb_norm, "ssm_norm": ssm_norm, "w_out": w_out,
            "mix_post": mix_post, "mem_pre": mem_pre, "mem_kv_norm": mem_kv_norm, "w_mq": w_mq,
            "w_mkv": w_mkv, "w_mo": w_mo, "mem_post": mem_post,
            "ffn2_pre": f2[0], "ffn2_wg": f2[1], "ffn2_wu": f2[2], "ffn2_wd": f2[3], "ffn2_post": f2[4]}


def reference(x, mem,
              ffn1_pre, ffn1_wg, ffn1_wu, ffn1_wd, ffn1_post,
              mix_pre, w_in, conv_w, conv_b, dt_bias, a_log, d_skip, sb_norm, ssm_norm, w_out, mix_post,
              mem_pre, mem_kv_norm, w_mq, w_mkv, w_mo, mem_post,
              ffn2_pre, ffn2_wg, ffn2_wu, ffn2_wd, ffn2_post):
    h = x
    for l in range(DEPTH):
        h = h + 0.5 * rms_norm(swiglu(rms_norm(h, ffn1_pre[l]), ffn1_wg[l], ffn1_wu[l], ffn1_wd[l]), ffn1_post[l])
        h = h + parallel_mixer(h, mix_pre[l], w_in[l], conv_w[l], conv_b[l], dt_bias[l], a_log[l], d_skip[l],
                               sb_norm[l], ssm_norm[l], w_out[l], mix_post[l])
        h = h + memory_xattn(h, mem, mem_pre[l], mem_kv_norm[l], w_mq[l], w_mkv[l], w_mo[l], mem_post[l])
        h = h + 0.5 * rms_norm(swiglu(rms_norm(h, ffn2_pre[l]), ffn2_wg[l], ffn2_wu[l], ffn2_wd[l]), ffn2_post[l])
    return h
```

```python
import numpy as np
from contextlib import ExitStack
import concourse.bass as bass
import concourse.mybir as mybir
from concourse.bass_utils import run_bass_kernel_spmd

F32 = mybir.dt.float32
BF16 = mybir.dt.bfloat16
AF = mybir.ActivationFunctionType
ALU = mybir.AluOpType
AX = mybir.AxisListType

S = 2048
D = 2048
DFF = 5504
NFC = 43
EPS = 1e-6
TT = 1024

ENGS = ("pe", "act", "dve", "pool", "sp")
NDMA_SEM = 8


class Op:
    __slots__ = ("eng", "fn", "reads", "writes", "dma", "signal", "sem", "val", "waits", "k", "bar")

    def __init__(self, eng, fn, reads, writes, dma, bar=False):
        self.eng, self.fn, self.reads, self.writes, self.dma = eng, fn, tuple(reads), tuple(writes), dma
        self.signal = dma
        self.sem = None
        self.val = 0
        self.waits = []
        self.k = 0
        self.bar = bar


class Prog:
    def __init__(self, nc):
        self.nc = nc
        self.ops = []

    def op(self, eng, fn, reads=(), writes=()):
        o = Op(eng, fn, reads, writes, False)
        self.ops.append(o)
        return o

    def dma(self, eng, out, in_, reads=(), writes=()):
        def fn(e, out=out, in_=in_):
            return e.dma_start(out=out, in_=in_)
        o = Op(eng, fn, reads, writes, True)
        self.ops.append(o)
        return o

    def barrier(self):
        for e in ENGS:
            self.ops.append(Op(e, None, (), (), False, bar=True))

    def emit(self, stack):
        nc = self.nc
        ops = self.ops
        esem = {e: stack.enter_context(nc.semaphore("s_" + e)) for e in ENGS}
        dsem = {e: [stack.enter_context(nc.semaphore("d_%s%d" % (e, i))) for i in range(NDMA_SEM)]
                for e in ("sp", "act", "pool")}
        dcnt = {e: 0 for e in dsem}
        for o in ops:
            if o.dma:
                o.k = dcnt[o.eng] % NDMA_SEM
                dcnt[o.eng] += 1
        last_w = {}
        readers = {}
        deps_of = []
        last_real = {e: None for e in ENGS}
        last_dma = {}
        for i, o in enumerate(ops):
            if o.bar:
                need = [j for e, j in last_real.items() if j is not None and e != o.eng]
                need += list(last_dma.values())
                for j in need:
                    ops[j].signal = True
                deps_of.append(need)
                continue
            deps = {}
            for r in o.reads:
                j = last_w.get(r)
                if j is not None:
                    deps[j] = "raw"
            for w in o.writes:
                j = last_w.get(w)
                if j is not None and j not in deps:
                    deps[j] = "waw"
                for j in readers.get(w, ()):
                    if j not in deps and j != i:
                        deps[j] = "war"
            need = []
            for j, kind in deps.items():
                p = ops[j]
                if p.dma or o.dma:
                    need.append(j)
                elif p.eng == o.eng:
                    if o.eng != "pe":
                        need.append(j)
                else:
                    need.append(j)
            for j in need:
                ops[j].signal = True
            deps_of.append(need)
            for r in o.reads:
                readers.setdefault(r, []).append(i)
            for w in o.writes:
                last_w[w] = i
                readers[w] = []
            if o.dma:
                last_dma[(o.eng, o.k)] = i
            elif o.fn is not None:
                last_real[o.eng] = i
        cnt = {e: 0 for e in ENGS}
        dval = {e: [0] * NDMA_SEM for e in dsem}
        for o in ops:
            if o.dma:
                o.sem = dsem[o.eng][o.k]
                prev = dval[o.eng][o.k]
                if prev:
                    o.waits.append((o.sem, prev))
                dval[o.eng][o.k] = prev + 16
                o.val = prev + 16
            elif o.signal:
                cnt[o.eng] += 1
                o.sem = esem[o.eng]
                o.val = cnt[o.eng]
        for i, o in enumerate(ops):
            for j in deps_of[i]:
                o.waits.append((ops[j].sem, ops[j].val))
        per_eng = {e: [o for o in ops if o.eng == e] for e in ENGS}
        self.stats = {e: len(per_eng[e]) for e in ENGS}

        def run(eng_obj, name):
            known = {}
            for o in per_eng[name]:
                best = {}
                for s, v in o.waits:
                    key = id(s)
                    if v > known.get(key, 0) and v > best.get(key, (None, 0))[1]:
                        best[key] = (s, v)
                for key, (s, v) in best.items():
                    eng_obj.wait_ge(s, v)
                    known[key] = v
                if o.fn is None:
                    continue
                ins = o.fn(eng_obj)
                if o.dma:
                    ins.then_inc(o.sem, 16)
                elif o.signal:
                    ins.then_inc(o.sem, 1)

        with nc.Block() as block:
            @block.tensor
            def _(e):
                run(e, "pe")

            @block.scalar
            def _(e):
                run(e, "act")

            @block.vector
            def _(e):
                run(e, "dve")

            @block.gpsimd
            def _(e):
                run(e, "pool")

            @block.sync
            def _(e):
                run(e, "sp")


WNAMES = ["ffn1_pre", "ffn1_wg", "ffn1_wu", "ffn1_wd", "ffn1_post",
          "mix_pre", "w_in", "conv_w", "conv_b", "dt_bias", "a_log", "d_skip", "sb_norm", "ssm_norm",
          "w_out", "mix_post", "mem_pre", "mem_kv_norm", "w_mq", "w_mkv", "w_mo", "mem_post",
          "ffn2_pre", "ffn2_wg", "ffn2_wu", "ffn2_wd", "ffn2_post"]
WSHAPES = {"ffn1_pre": [1, 2048], "ffn1_wg": [2048, 5504], "ffn1_wu": [2048, 5504], "ffn1_wd": [5504, 2048],
           "ffn1_post": [1, 2048], "mix_pre": [1, 2048], "w_in": [2048, 6160], "conv_w": [4, 2048],
           "conv_b": [1, 2048], "dt_bias": [1, 16], "a_log": [1, 16], "d_skip": [1, 16], "sb_norm": [1, 1024],
           "ssm_norm": [1, 1024], "w_out": [2048, 2048], "mix_post": [1, 2048], "mem_pre": [1, 2048],
           "mem_kv_norm": [1, 2048], "w_mq": [2048, 512], "w_mkv": [2048, 1024], "w_mo": [512, 2048],
           "mem_post": [1, 2048], "ffn2_pre": [1, 2048], "ffn2_wg": [2048, 5504], "ffn2_wu": [2048, 5504],
           "ffn2_wd": [5504, 2048], "ffn2_post": [1, 2048]}
NORMV = ["ffn1_pre", "ffn1_post", "mix_pre", "mix_post", "mem_pre", "mem_post", "ffn2_pre", "ffn2_post"]
NCONST = 2688
NSEL = 2048


def make_sel():
    sel = np.zeros((16, NSEL), np.float32)
    for h in range(16):
        sel[h, h * 128:(h + 1) * 128] = 1.0
    return sel


def make_consts():
    c = np.zeros((128, NCONST), np.float32)
    i = np.arange(128)
    c[:, 0:128] = np.eye(128, dtype=np.float32)
    c[:, 128:256] = (i[:, None] >= i[None, :])
    c[:, 256:384] = (i[:, None] <= i[None, :])
    c[:, 384:512] = 1.0
    t = np.arange(512)
    for k in range(4):
        c[:, 512 + 512 * k:1024 + 512 * k] = (t[None, :] - i[:, None] - 128 * k) <= 0
    c[:, 2560:2688] = -30000.0 * np.eye(128, dtype=np.float32)
    return c


class K:
    def __init__(self, stages, dbg=()):
        self.stages = stages
        nc = self.nc = bass.Bass("TRN2", target_bir_lowering=False)
        self.P = Prog(nc)
        self.dbg = set(dbg)
        self.x = nc.dram_tensor("x", [S, D], F32, kind="ExternalInput").ap()
        self.mem = nc.dram_tensor("mem", [256, D], F32, kind="ExternalInput").ap()
        self.consts = nc.dram_tensor("consts", [128, NCONST], F32, kind="ExternalInput").ap()
        self.selc = nc.dram_tensor("selc", [16, NSEL], F32, kind="ExternalInput").ap()
        self.w = {n: nc.dram_tensor(n, WSHAPES[n], F32, kind="ExternalInput").ap() for n in WNAMES}
        self.out = nc.dram_tensor("out", [S, D], F32, kind="ExternalOutput").ap()
        self.scr = {}
        self.wk = 0
        self.stgfree = [0, 1, 2, 3]
        self.sqk = 0

    def dram(self, name, shape, dt):
        kind = "ExternalOutput" if name in self.dbg else "Internal"
        t = self.nc.dram_tensor(name, shape, dt, kind=kind).ap()
        self.scr[name] = t
        return t

    def mm(self, out, lhsT, rhs, start, stop, reads, writes):
        self.P.op("pe", lambda e: e.matmul(out, lhsT=lhsT, rhs=rhs, start=start, stop=stop), reads, writes)

    def tr(self, out, in_, ident, reads, writes):
        self.P.op("pe", lambda e: e.transpose(out, in_, ident), reads, writes)

    def act(self, out, in_, func, reads, writes, bias=None, scale=None, accum=None):
        kw = {}
        if bias is not None:
            kw["bias"] = bias
        if scale is not None:
            kw["scale"] = scale
        if accum is not None:
            kw["accum_out"] = accum
        self.P.op("act", lambda e: e.activation(out=out, in_=in_, func=func, **kw), reads, writes)

    def tt(self, out, in0, in1, op, reads, writes, eng="dve"):
        self.P.op(eng, lambda e: e.tensor_tensor(out=out, in0=in0, in1=in1, op=op), reads, writes)

    def ts(self, out, in0, s1, s2, op0, op1, reads, writes, eng="dve"):
        if op1 is None:
            self.P.op(eng, lambda e: e.tensor_scalar(out=out, in0=in0, scalar1=s1, scalar2=None, op0=op0), reads, writes)
        else:
            self.P.op(eng, lambda e: e.tensor_scalar(out=out, in0=in0, scalar1=s1, scalar2=s2, op0=op0, op1=op1),
                      reads, writes)

    def stt(self, out, in0, scalar, in1, op0, op1, reads, writes, eng="dve"):
        self.P.op(eng, lambda e: e.scalar_tensor_tensor(out=out, in0=in0, scalar=scalar, in1=in1, op0=op0, op1=op1),
                  reads, writes)

    def cp(self, out, in_, reads, writes, eng="dve"):
        if eng == "act":
            self.P.op("act", lambda e: e.copy(out=out, in_=in_), reads, writes)
        else:
            self.P.op(eng, lambda e: e.tensor_copy(out=out, in_=in_), reads, writes)

    def dma(self, out, in_, reads, writes, q="sp"):
        self.P.dma(q, out, in_, reads, writes)

    def alloc(self, st):
        nc = self.nc
        sb = lambda n, shp, dt: st.enter_context(nc.sbuf_tensor(n, shp, dt))
        self.cf = sb("cf", [128, 512], F32)
        self.cb = sb("cb", [128, NCONST], BF16)
        self.ident_f = self.cf[:, 0:128]
        self.trile_f = self.cf[:, 256:384]
        self.ones_f = self.cf[:, 384:512]
        self.ident_b = self.cb[:, 0:128]
        self.trige_b = self.cb[:, 128:256]
        self.ones_b = self.cb[:, 384:512]
        self.sbmask = [self.cb[:, 512 + 512 * k:1024 + 512 * k] for k in range(4)]
        self.negid_b = self.cb[:, 2560:2688]
        self.nw = sb("nw", [128, 128], F32)
        self.pw2 = sb("pw2", [128, 88], F32)
        self.bc16 = sb("bc16", [128, 64], F32)
        self.dtraw = sb("dtraw", [128, 16, 16], F32)
        self.XT = sb("XT", [128, 16, TT], BF16)
        self.W = [sb("W%d" % i, [128, 16, 512], BF16) for i in range(3)]
        self.stg = [sb("stg%d" % i, [128, TT], F32) for i in range(4)]
        self.sq = [sb("sq%d" % i, [128, TT], BF16) for i in range(2)]
        self.rstd = sb("rstd", [128, TT], F32)
        self.rstd2 = sb("rstd2", [128, TT], F32)
        self.bg = []
        self.last_post = None
        self.sg = [sb("sg%d" % i, [128, 512], F32) for i in range(2)]
        self.BIG = sb("BIG", [128, NFC * TT], BF16)
        self.ps = [st.enter_context(nc.psum_tensor("ps%d" % i, [128, 512], F32)) for i in range(8)]
        self.coff = 0

    def carve_reset(self):
        self.coff = 0

    def carve(self, cols, dt):
        n = cols * (2 if dt == F32 else 1)
        n = (n + 15) // 16 * 16
        a = self.BIG[:, self.coff:self.coff + n]
        self.coff += n
        assert self.coff <= NFC * TT, self.coff
        if dt == F32:
            return a.bitcast(F32)[:, 0:cols]
        return a[:, 0:cols]

    def wslot(self):
        k = self.wk % 3
        self.wk += 1
        return k

    def stgslot(self):
        assert self.stgfree, "out of staging slots"
        return self.stgfree.pop(0)

    def stgrel(self, k):
        self.stgfree.append(k)

    def sqslot(self):
        k = self.sqk % 2
        self.sqk += 1
        return k

    def setup(self):
        w = self.w
        self.dma(self.cf[:], self.consts[:, 0:512], [], ["cf"])
        self.dma(self.cb[:], self.consts[:, :], [], ["cb"], q="pool")
        st1 = self.stg[0]
        for i, n in enumerate(NORMV):
            self.dma(st1[i * 16:(i + 1) * 16, 0:128], w[n].rearrange("o (c p) -> (o c) p", p=128), [], [("stg", 0)])
        self.tr(self.ps[7][:, 0:128], st1[:, 0:128], self.ident_f, [("stg", 0), "cf"], [("ps", 7)])
        self.cp(self.nw[:], self.ps[7][:, 0:128], [("ps", 7)], ["nw"])
        for i in (1, 7):
            self.P.op("act", lambda e, i=i: e.mul(out=self.nw[:, i * 16:(i + 1) * 16], in_=self.nw[:, i * 16:(i + 1) * 16],
                                                  mul=0.5), ["nw"], ["nw"])
        st2 = self.stg[1]
        self.dma(st2[0:8, 0:128], w["sb_norm"].rearrange("o (c p) -> (o c) p", p=128), [], [("stg", 1)])
        self.dma(st2[8:24, 0:128], w["conv_b"].rearrange("o (c p) -> (o c) p", p=128), [], [("stg", 1)])
        self.dma(st2[24:88, 0:128], w["conv_w"].rearrange("k (c p) -> (k c) p", p=128), [], [("stg", 1)])
        self.tr(self.ps[6][:, 0:88], st2[0:88, 0:128], self.ident_f[0:88, 0:88], [("stg", 1), "cf"], [("ps", 6)])
        self.cp(self.pw2[:], self.ps[6][:, 0:88], [("ps", 6)], ["pw2"])
        self.dma(self.bc16[:, 0:16], w["dt_bias"].to_broadcast([128, 16]), [], ["bc16"])
        self.dma(self.bc16[:, 16:32], w["a_log"].to_broadcast([128, 16]), [], ["bc16"])
        self.dma(self.bc16[:, 32:48], w["d_skip"].to_broadcast([128, 16]), [], ["bc16"])
        self.act(self.bc16[:, 16:32], self.bc16[:, 16:32], AF.Exp, ["bc16"], ["bc16"])
        self.P.op("act", lambda e: e.mul(out=self.bc16[:, 16:32], in_=self.bc16[:, 16:32], mul=-1.0), ["bc16"], ["bc16"])

    def phase0_gen(self, hT0):
        hid = id(hT0)
        nb = 0
        for tb in range(16):
            t0 = (tb // 8) * TT
            s = self.stgslot()
            s2 = self.stgslot()
            self.dma(self.stg[s][:], self.x[tb * 128:(tb + 1) * 128, 0:1024], [], [("stg", s)])
            self.dma(self.stg[s2][:], self.x[tb * 128:(tb + 1) * 128, 1024:2048], [], [("stg", s2)])
            for half, sl in ((0, s), (1, s2)):
                so = self.stgslot()
                for q4 in range(2):
                    b = 6 + nb % 2
                    nb += 1
                    for j in range(4):
                        dcl = q4 * 4 + j
                        self.tr(self.ps[b][:, j * 128:(j + 1) * 128], self.stg[sl][:, dcl * 128:(dcl + 1) * 128],
                                self.ident_f, [("stg", sl), "cf"], [("ps", b)])
                    self.cp(self.stg[so][:, q4 * 512:(q4 + 1) * 512], self.ps[b][:], [("ps", b)], [("stg", so)],
                            eng=("dve" if q4 == 0 else "act"))
                self.dma(hT0[half * 1024:(half + 1) * 1024, tb * 128:(tb + 1) * 128].rearrange("(c p) t -> p c t", p=128),
                         self.stg[so][:].rearrange("p (c t) -> p c t", t=128), [("stg", so)],
                         [("h", hid, dc, t0) for dc in range(half * 8, half * 8 + 8)])
                self.stgrel(so)
            self.stgrel(s)
            self.stgrel(s2)
            yield

    def stats_rstd(self, n, pb, rbuf, rkey):
        for t2 in range(2):
            self.act(rbuf[:, t2 * 512:(t2 + 1) * 512], self.ps[pb + t2][:], AF.Ln, [("ps", pb + t2)], [rkey],
                     bias=self.epsc, scale=1.0 / n)
        self.act(rbuf[:], rbuf[:], AF.Exp, [rkey], [rkey], scale=-0.5)

    def sq_stats(self, src, dc, reads, pb, ndc=16):
        q = self.sqslot()
        self.act(self.sq[q][:], src, AF.Square, reads, [("sq", q)])
        for t2 in range(2):
            self.mm(self.ps[pb + t2][:], self.ones_b, self.sq[q][:, t2 * 512:(t2 + 1) * 512], dc == 0, dc == ndc - 1,
                    [("sq", q), "cb"], [("ps", pb + t2)])

    def bg_add(self, g):
        self.bg.append(g)
        return g

    def bg_step(self, n=1):
        while n > 0 and self.bg:
            try:
                next(self.bg[0])
                n -= 1
            except StopIteration:
                self.bg.pop(0)

    def bg_finish(self, g):
        while any(x is g for x in self.bg):
            self.bg_step(1)

    def bg_drain(self):
        while self.bg:
            self.bg_step(1000)

    def prepass_gen(self, hT_in, t0, wi, XT, xk):
        hid = id(hT_in)
        LA = 2
        slots = {}

        def issue(dc):
            s = self.stgslot()
            self.dma(self.stg[s][:], hT_in[dc * 128:(dc + 1) * 128, t0:t0 + TT], [("h", hid, dc, t0)], [("stg", s)])
            slots[dc] = s

        for dc in range(LA):
            issue(dc)
        for dc in range(16):
            if dc + LA < 16:
                issue(dc + LA)
            s = slots.pop(dc)
            self.sq_stats(self.stg[s][:], dc, [("stg", s)], 6)
            self.stgrel(s)
            yield
        for dc in range(LA):
            issue(dc)
        self.stats_rstd(D, 6, self.rstd2, "rstd2")
        for dc in range(16):
            if dc + LA < 16:
                issue(dc + LA)
            s = slots.pop(dc)
            self.stt(XT[:, dc, :], self.stg[s][:], self.nw[:, wi * 16 + dc:wi * 16 + dc + 1], self.rstd2[:],
                     ALU.mult, ALU.mult, [("stg", s), "nw", "rstd2"], [(xk, dc)])
            self.stgrel(s)
            yield

    def post_evac(self, dc, t0, b0, yraw):
        q = self.sqslot()
        for t2 in range(2):
            self.cp(self.sg[t2][:], self.ps[b0 + t2][:], [("ps", b0 + t2)], [("sg", t2)], eng="act")
            self.act(self.sq[q][:, t2 * 512:(t2 + 1) * 512], self.sg[t2][:], AF.Square, [("sg", t2)], [("sq", q)])
            self.mm(self.ps[4 + t2][:], self.ones_b, self.sq[q][:, t2 * 512:(t2 + 1) * 512], dc == 0, dc == 15,
                    [("sq", q), "cb"], [("ps", 4 + t2)])
            self.dma(yraw[dc * 128:(dc + 1) * 128, t0 + t2 * 512:t0 + (t2 + 1) * 512], self.sg[t2][:], [("sg", t2)],
                     [("yraw", dc, t0, t2)])

    def add_post(self, *a, **kw):
        if self.last_post is not None:
            self.bg_finish(self.last_post)
        self.stats_rstd(D, 4, self.rstd, "rstd")
        self.last_post = self.bg_add(self.post_final_gen(*a, **kw))
        return self.last_post

    def post_final_gen(self, hT_in, hT_out, t0, wi, yraw, final=False):
        hid = id(hT_in)
        hod = id(hT_out)
        slots = {}

        def issue(dc):
            s1 = self.stgslot()
            s2 = self.stgslot()
            self.dma(self.stg[s1][:], yraw[dc * 128:(dc + 1) * 128, t0:t0 + TT],
                     [("yraw", dc, t0, 0), ("yraw", dc, t0, 1)], [("stg", s1)])
            self.dma(self.stg[s2][:], hT_in[dc * 128:(dc + 1) * 128, t0:t0 + TT], [("h", hid, dc, t0)], [("stg", s2)])
            slots[dc] = (s1, s2)

        issue(0)
        for dc in range(16):
            if dc + 1 < 16:
                issue(dc + 1)
            s1, s2 = slots.pop(dc)
            self.stt(self.stg[s1][:], self.stg[s1][:], self.nw[:, wi * 16 + dc:wi * 16 + dc + 1], self.rstd[:],
                     ALU.mult, ALU.mult, [("stg", s1), "nw", "rstd"], [("stg", s1)])
            self.tt(self.stg[s1][:], self.stg[s1][:], self.stg[s2][:], ALU.add, [("stg", s1), ("stg", s2)], [("stg", s1)])
            if not final:
                self.dma(hT_out[dc * 128:(dc + 1) * 128, t0:t0 + TT], self.stg[s1][:], [("stg", s1)], [("h", hod, dc, t0)])
            else:
                for q4 in range(2):
                    b = 6 + q4
                    for j in range(4):
                        jj = q4 * 4 + j
                        self.tr(self.ps[b][:, j * 128:(j + 1) * 128], self.stg[s1][:, jj * 128:(jj + 1) * 128],
                                self.ident_f, [("stg", s1), "cf"], [("ps", b)])
                    self.cp(self.stg[s2][:, q4 * 512:(q4 + 1) * 512], self.ps[b][:], [("ps", b)], [("stg", s2)],
                            eng=("dve" if q4 == 0 else "act"))
                self.dma(self.out[t0:t0 + TT, dc * 128:(dc + 1) * 128].rearrange("(j p) d -> p j d", p=128),
                         self.stg[s2][:].rearrange("p (j d) -> p j d", d=128), [("stg", s2)], [("out", dc, t0)])
            self.stgrel(s1)
            self.stgrel(s2)
            yield

    def ffn(self, hT_in, hT_out, wg, wu, wd, pre_i, post_i, yraw, pre0, next_pre=None, final=False):
        HT = self.BIG[:].rearrange("p (c t) -> p c t", t=TT)
        pre = pre0
        for t0 in (0, TT):
            self.bg_finish(pre)
            for g in range(11):
                ncols = 512 if g < 10 else 384
                c0 = g * 512
                ka = self.wslot()
                self.dma(self.W[ka][:, :, 0:ncols], wg[:, c0:c0 + ncols].rearrange("(c p) f -> p c f", p=128), [],
                         [("W", ka)], q="pool")
                kb = self.wslot()
                self.dma(self.W[kb][:, :, 0:ncols], wu[:, c0:c0 + ncols].rearrange("(c p) f -> p c f", p=128), [],
                         [("W", kb)], q="pool")
                for j in range(ncols // 128):
                    fc = g * 4 + j
                    for kc in range(16):
                        for t2 in range(2):
                            self.mm(self.ps[t2][:], self.W[ka][:, kc, j * 128:(j + 1) * 128],
                                    self.XT[:, kc, t2 * 512:(t2 + 1) * 512], kc == 0, kc == 15,
                                    [("W", ka), ("XT", kc)], [("ps", t2)])
                    for kc in range(16):
                        for t2 in range(2):
                            self.mm(self.ps[2 + t2][:], self.W[kb][:, kc, j * 128:(j + 1) * 128],
                                    self.XT[:, kc, t2 * 512:(t2 + 1) * 512], kc == 0, kc == 15,
                                    [("W", kb), ("XT", kc)], [("ps", 2 + t2)])
                    for t2 in range(2):
                        self.act(self.sg[t2][:], self.ps[t2][:], AF.Silu, [("ps", t2)], [("sg", t2)])
                        self.tt(HT[:, fc, t2 * 512:(t2 + 1) * 512], self.sg[t2][:], self.ps[2 + t2][:], ALU.mult,
                                [("sg", t2), ("ps", 2 + t2)], [("HT", fc, t2)])
                    self.bg_step(1)
            if t0 == 0:
                pre = self.bg_add(self.prepass_gen(hT_in, TT, pre_i, self.XT, "XT"))
            elif next_pre is not None:
                self.next_pre_gen = self.bg_add(next_pre())
            for dc in range(16):
                k = self.wslot()
                Wv = self.W[k][:].rearrange("p c f -> p (c f)")[:, 0:NFC * 128].rearrange("p (c m) -> p c m", m=128)
                self.dma(Wv, wd[:, dc * 128:(dc + 1) * 128].rearrange("(c p) m -> p c m", p=128), [], [("W", k)], q="pool")
                b0 = (dc % 2) * 2
                for fc in range(NFC):
                    for t2 in range(2):
                        self.mm(self.ps[b0 + t2][:], Wv[:, fc, :], HT[:, fc, t2 * 512:(t2 + 1) * 512], fc == 0,
                                fc == NFC - 1, [("W", k), ("HT", fc, t2)], [("ps", b0 + t2)])
                self.post_evac(dc, t0, b0, yraw)
                self.bg_step(3)
            self.add_post(hT_in, hT_out, t0, post_i, yraw, final=final)

    def inproj(self, hT_in, sc, pre0):
        w_in = self.w["w_in"]
        self.carve_reset()
        XTb = self.carve(16 * TT, BF16).rearrange("p (c t) -> p c t", t=TT)
        ev = [self.carve(1024, BF16) for _ in range(2)]
        evv = [self.carve(512, BF16) for _ in range(2)]
        evz = [self.carve(512, F32) for _ in range(2)]
        cbuf = [self.carve(1028, F32) for _ in range(2)]
        acc = [self.carve(1024, F32) for _ in range(2)]
        xtk = [self.carve(1024, BF16) for _ in range(2)]
        carry = self.carve(48, F32)
        wdt = self.carve(256, BF16).rearrange("p (c f) -> p c f", f=16)
        cnt = {"ev": 0, "evv": 0, "evz": 0, "cb": 0, "xtk": 0, "bank": 0, "tb": 0}

        def nxt(k, n=2):
            v = cnt[k] % n
            cnt[k] += 1
            return v

        self.dma(wdt, w_in[:, 6144:6160].rearrange("(c p) f -> p c f", p=128), [], ["wdt"], q="pool")
        for t0 in (0, TT):
            ti = t0 // TT
            if ti == 0:
                self.bg_finish(pre0)
                XT, xk = self.XT, "XT"
                pre1 = self.bg_add(self.prepass_gen(hT_in, TT, 2, XTb, "XTb"))
            else:
                self.bg_finish(pre1)
                XT, xk = XTb, "XTb"
            for g in range(4):
                ka = self.wslot()
                self.dma(self.W[ka][:], w_in[:, g * 512:(g + 1) * 512].rearrange("(c p) f -> p c f", p=128), [],
                         [("W", ka)], q="pool")
                for j in range(4):
                    ch = g * 4 + j
                    b0 = nxt("bank") * 2
                    for kc in range(16):
                        for t2 in range(2):
                            self.mm(self.ps[b0 + t2][:], self.W[ka][:, kc, j * 128:(j + 1) * 128],
                                    XT[:, kc, t2 * 512:(t2 + 1) * 512], kc == 0, kc == 15,
                                    [("W", ka), (xk, kc)], [("ps", b0 + t2)])
                    e = nxt("ev")
                    self.cp(ev[e][:, 0:512], self.ps[b0][:], [("ps", b0)], [("ev", e)], eng="act")
                    self.cp(ev[e][:, 512:1024], self.ps[b0 + 1][:], [("ps", b0 + 1)], [("ev", e)], eng="dve")
                    dst = sc["qT"] if ch < 8 else sc["kT"]
                    r0 = (ch % 8) * 128
                    self.dma(dst[r0:r0 + 128, t0:t0 + TT], ev[e], [("ev", e)], [("qk", ch, ti)])
                    self.bg_step(1)
            for which, base in (("v", 2048), ("z", 3072)):
                for ct in range(2):
                    ka = self.wslot()
                    self.dma(self.W[ka][:], w_in[:, base + ct * 512:base + (ct + 1) * 512].rearrange("(c p) f -> p c f", p=128),
                             [], [("W", ka)], q="pool")
                    for tb in range(8):
                        b = nxt("tb", 4)
                        for kc in range(16):
                            self.mm(self.ps[b][:], XT[:, kc, tb * 128:(tb + 1) * 128], self.W[ka][:, kc, :],
                                    kc == 0, kc == 15, [("W", ka), (xk, kc)], [("ps", b)])
                        r0 = t0 + tb * 128
                        self.bg_step(1)
                        if which == "v":
                            e = nxt("evv")
                            self.cp(evv[e], self.ps[b][:], [("ps", b)], [("evv", e)], eng=("dve" if tb % 2 else "act"))
                            self.dma(sc["v_tok"][r0:r0 + 128, ct * 512:(ct + 1) * 512], evv[e], [("evv", e)],
                                     [("vt", ct, tb, ti)])
                        else:
                            e = nxt("evz")
                            self.act(evz[e], self.ps[b][:], AF.Silu, [("ps", b)], [("evz", e)])
                            self.dma(sc["sz_tok"][r0:r0 + 128, ct * 512:(ct + 1) * 512], evz[e], [("evz", e)],
                                     [("zt", ct, tb, ti)])
            for g in range(4):
                ka = self.wslot()
                self.dma(self.W[ka][:], w_in[:, 4096 + g * 512:4096 + (g + 1) * 512].rearrange("(c p) f -> p c f", p=128),
                         [], [("W", ka)], q="pool")
                for j in range(4):
                    ch = g * 4 + j
                    b0 = nxt("bank") * 2
                    for kc in range(16):
                        for t2 in range(2):
                            self.mm(self.ps[b0 + t2][:], self.W[ka][:, kc, j * 128:(j + 1) * 128],
                                    XT[:, kc, t2 * 512:(t2 + 1) * 512], kc == 0, kc == 15,
                                    [("W", ka), (xk, kc)], [("ps", b0 + t2)])
                    c = nxt("cb")
                    if ti == 0:
                        self.P.op("dve", lambda e, c=c: e.memset(cbuf[c][:, 0:3], 0.0), [], [("cbuf", c)])
                    else:
                        self.cp(cbuf[c][:, 0:3], carry[:, ch * 3:ch * 3 + 3], ["carry"], [("cbuf", c)])
                    self.cp(cbuf[c][:, 3:515], self.ps[b0][:], [("ps", b0)], [("cbuf", c)], eng="act")
                    self.cp(cbuf[c][:, 515:1027], self.ps[b0 + 1][:], [("ps", b0 + 1)], [("cbuf", c)], eng="act")
                    if ti == 0:
                        self.cp(carry[:, ch * 3:ch * 3 + 3], cbuf[c][:, 1024:1027], [("cbuf", c)], ["carry"])
                    wcol = lambda k: self.pw2[:, 24 + k * 16 + ch:24 + k * 16 + ch + 1]
                    self.ts(acc[c], cbuf[c][:, 0:1024], wcol(0), 0.0, ALU.mult, ALU.add, [("cbuf", c), "pw2"], [("acc", c)],
                            eng="pool")
                    for k in range(1, 4):
                        self.stt(acc[c], cbuf[c][:, k:k + 1024], wcol(k), acc[c], ALU.mult, ALU.add,
                                 [("cbuf", c), ("acc", c), "pw2"], [("acc", c)])
                    e = nxt("ev")
                    self.act(ev[e], acc[c], AF.Silu, [("acc", c), "pw2"], [("ev", e)], bias=self.pw2[:, 8 + ch:9 + ch])
                    if ch >= 12:
                        self.dma(sc["cT"][(ch - 12) * 128:(ch - 11) * 128, t0:t0 + TT], ev[e], [("ev", e)], [("ct", ch, ti)])
                    elif ch >= 8:
                        self.dma(sc["bT"][(ch - 8) * 128:(ch - 7) * 128, t0:t0 + TT], ev[e], [("ev", e)], [("bt", ch, ti)])
                    if ch < 12:
                        xk_ = nxt("xtk")
                        pb = 4 + xk_
                        psT = self.ps[pb][:].bitcast(BF16)
                        for j8 in range(8):
                            self.tr(psT[:, j8 * 128:(j8 + 1) * 128], ev[e][:, j8 * 128:(j8 + 1) * 128], self.ident_b,
                                    [("ev", e), "cb"], [("ps", pb)])
                        self.cp(xtk[xk_], psT, [("ps", pb)], [("xtk", xk_)], eng="act")
                        if ch < 8:
                            dst = sc["x_tok"][t0:t0 + TT, ch * 128:(ch + 1) * 128]
                        else:
                            dst = sc["b_tok"][t0:t0 + TT, (ch - 8) * 128:(ch - 7) * 128]
                        self.dma(dst.rearrange("(j p) c -> p j c", p=128), xtk[xk_].rearrange("p (j c) -> p j c", c=128),
                                 [("xtk", xk_)], [("xo", ch, ti)])
            for tb in range(8):
                b = nxt("tb", 4)
                for kc in range(16):
                    self.mm(self.ps[b][:, 0:16], XT[:, kc, tb * 128:(tb + 1) * 128], wdt[:, kc, :], kc == 0, kc == 15,
                            ["wdt", (xk, kc)], [("ps", b)])
                self.cp(self.dtraw[:, ti * 8 + tb, :], self.ps[b][:, 0:16], [("ps", b)], ["dtraw"])

    def sbattn(self, sc):
        self.bg_drain()
        self.carve_reset()
        kth = [self.carve(2048, BF16) for _ in range(2)]
        qth = [self.carve(2048, BF16) for _ in range(2)]
        vh = [self.carve(2048, BF16) for _ in range(2)]
        eb = [self.carve(512, F32) for _ in range(2)]
        spm = [self.carve(512, BF16) for _ in range(2)]
        t1 = [self.carve(512, F32) for _ in range(2)]
        arg = [self.carve(512, F32) for _ in range(2)]
        ab = [self.carve(512, BF16) for _ in range(2)]
        R = self.carve(512, F32)
        ob = [self.carve(512, F32) for _ in range(2)]
        sqo = [self.carve(512, BF16) for _ in range(2)]
        ssacc = self.carve(2048, F32)
        scale = 128.0 ** -0.5
        self.P.op("dve", lambda e: e.memset(ssacc, 0.0), [], ["ssacc"])
        pairs = []
        for h in range(8):
            for tt in range(4):
                for sb in range(4 * tt + 3, -1, -1):
                    pairs.append((h, tt, sb))
        n = len(pairs)

        def load(h):
            hs = h % 2
            self.dma(kth[hs], sc["kT"][h * 128:(h + 1) * 128, :], [], [("kq", hs)])
            self.dma(qth[hs], sc["qT"][h * 128:(h + 1) * 128, :], [], [("kq", hs)])
            self.dma(vh[hs].rearrange("p (c d) -> p c d", d=128),
                     sc["v_tok"][:, h * 128:(h + 1) * 128].rearrange("(c p) d -> p c d", p=128), [], [("kq", hs)])

        def S1(p):
            h, tt, sb = pairs[p]
            if tt == 0 and sb == 3:
                if h == 0:
                    load(0)
                if h + 1 < 8:
                    load(h + 1)
            zb = (0, 1, 7)[p % 3]
            diag = sb >= 4 * tt
            self.mm(self.ps[zb][:], kth[h % 2][:, sb * 128:(sb + 1) * 128], qth[h % 2][:, tt * 512:(tt + 1) * 512],
                    True, not diag, [("kq", h % 2)], [("ps", zb)])
            if diag:
                self.mm(self.ps[zb][:], self.negid_b, self.sbmask[sb - 4 * tt], False, True, ["cb"], [("ps", zb)])

        def S2(p):
            h, tt, sb = pairs[p]
            i = p % 2
            zb = (0, 1, 7)[p % 3]
            self.act(eb[i], self.ps[zb][:], AF.Exp, [("ps", zb)], [("eb", i)], scale=scale)
            self.act(spm[i], eb[i], AF.Ln, [("eb", i)], [("spm", i)], bias=1.0)

        def S3(p):
            i = p % 2
            self.mm(self.ps[2 + i][:], self.trige_b, spm[i], True, True, [("spm", i), "cb"], [("ps", 2 + i)])
            self.mm(self.ps[4 + i][:], self.ones_b, spm[i], True, True, [("spm", i), "cb"], [("ps", 4 + i)])

        def S4(p):
            h, tt, sb = pairs[p]
            i = p % 2
            first = sb == 4 * tt + 3
            if first:
                self.cp(t1[i], self.ps[2 + i][:], [("ps", 2 + i)], [("t1", i)])
            else:
                self.tt(t1[i], self.ps[2 + i][:], R, ALU.add, [("ps", 2 + i), "R"], [("t1", i)])
            zb = (0, 1, 7)[p % 3]
            self.stt(arg[i], self.ps[zb][:], scale, t1[i], ALU.mult, ALU.subtract, [("ps", zb), ("t1", i)], [("arg", i)])
            if first:
                self.cp(R, self.ps[4 + i][:], [("ps", 4 + i)], ["R"])
            elif sb > 0:
                self.tt(R, self.ps[4 + i][:], R, ALU.add, [("ps", 4 + i), "R"], ["R"])

        def S5(p):
            h, tt, sb = pairs[p]
            i = p % 2
            self.act(ab[i], arg[i], AF.Exp, [("arg", i)], [("ab", i)])

        def S6(p):
            h, tt, sb = pairs[p]
            i = p % 2
            til = (h * 4 + tt) % 2
            self.mm(self.ps[6][:], vh[h % 2][:, sb * 128:(sb + 1) * 128], ab[i], sb == 4 * tt + 3, sb == 0,
                    [("kq", h % 2), ("ab", i)], [("ps", 6)])
            if sb == 0:
                self.cp(ob[til], self.ps[6][:], [("ps", 6)], [("ob", til)], eng="act")
                self.dma(sc["oT_raw"][h * 128:(h + 1) * 128, tt * 512:(tt + 1) * 512], ob[til], [("ob", til)],
                         [("oraw", h, tt)])
                self.act(sqo[til], ob[til], AF.Square, [("ob", til)], [("sqo", til)])
                self.mm(self.ps[2][:], self.ones_b, sqo[til], True, True, [("sqo", til), "cb"], [("ps", 2)])
                self.tt(ssacc[:, tt * 512:(tt + 1) * 512], self.ps[2][:], ssacc[:, tt * 512:(tt + 1) * 512], ALU.add,
                        [("ps", 2), "ssacc"], ["ssacc"])

        for it in range(-3, n):
            if 0 <= it + 3 < n:
                S1(it + 3)
                S2(it + 3)
            if 0 <= it + 2 < n:
                S3(it + 2)
                S4(it + 2)
            if 0 <= it + 1 < n:
                S5(it + 1)
            if 0 <= it < n:
                S6(it)
        self.act(ssacc, ssacc, AF.Ln, ["ssacc"], ["ssacc"], bias=self.epsc, scale=1.0 / 1024)
        self.act(ssacc, ssacc, AF.Exp, ["ssacc"], ["ssacc"], scale=-0.5)
        self.dma(sc["rsb"][:, :], ssacc, ["ssacc"], ["rsb"])

    def ssd(self, sc):
        self.carve_reset()
        c2 = lambda n, dt: [self.carve(n, dt) for _ in range(2)]
        ssmw = self.carve(1024, F32)
        xtk, btk, bTc, cTc = c2(1024, BF16), c2(512, BF16), c2(512, BF16), c2(512, BF16)
        szc = c2(1024, F32)
        tmp16, dtb, da, csb, wend, ecs, cdec = (c2(16, F32), c2(16, F32), c2(16, F32), c2(32, F32), c2(16, F32),
                                                c2(16, F32), c2(16, F32))
        ss1 = self.carve(4, F32)
        xd, xdw = c2(1024, BF16), c2(1024, BF16)
        cbm = [self.carve(128, F32) for _ in range(3)]
        sel = self.carve(NSEL, BF16)
        csT = c2(128, F32)
        csR = c2(128, F32)
        csT3 = c2(384, BF16)
        self.dma(sel[0:16, :], self.selc, [], ["sel"], q="pool")
        dec = [self.carve(128, F32) for _ in range(4)]
        Mh = [self.carve(128, BF16) for _ in range(4)]
        ytok = c2(1024, F32)
        gy = self.carve(1024, F32)
        junk = self.carve(1024, F32)
        yn = self.carve(1024, BF16)
        prevT = self.carve(1024, F32)
        prevb = self.carve(1024, BF16)
        mixst = c2(1024, BF16)
        bc = self.bc16
        self.dma(ssmw, self.w["ssm_norm"].to_broadcast([128, 1024]), [], ["ssmw"])
        g3 = lambda ap: ap.rearrange("p (g t) -> p g t", t=128)
        h3 = lambda ap: ap.rearrange("p (h d) -> p h d", d=64)

        def load(c):
            s = c % 2
            r = slice(c * 128, (c + 1) * 128)
            self.dma(xtk[s], sc["x_tok"][r, :], [], [("ld", s)])
            self.dma(btk[s], sc["b_tok"][r, :], [], [("ld", s)])
            self.dma(g3(bTc[s]), sc["bT"][:, r].rearrange("(g p) t -> p g t", p=128), [], [("ld", s)])
            self.dma(g3(cTc[s]), sc["cT"][:, r].rearrange("(g p) t -> p g t", p=128), [], [("ld", s)])
            self.dma(szc[s], sc["sz_tok"][r, :], [], [("ld", s)])

        def prologue(c):
            s = c % 2
            L = ("ld", s)
            K = lambda n: (n, s)
            self.tt(tmp16[s], self.dtraw[:, c, :], bc[:, 0:16], ALU.add, ["dtraw", "bc16"], [K("tmp16")])
            self.act(tmp16[s], tmp16[s], AF.Exp, [K("tmp16")], [K("tmp16")])
            self.act(dtb[s], tmp16[s], AF.Ln, [K("tmp16")], [K("dtb")], bias=1.0)
            self.tt(da[s], dtb[s], bc[:, 16:32], ALU.mult, [K("dtb"), "bc16"], [K("da")])
            self.mm(self.ps[4][:, 0:16], self.trile_f, da[s], True, True, [K("da"), "cf"], [("ps", 4)])
            self.mm(self.ps[4][:, 16:32], self.ones_f, da[s], True, True, [K("da"), "cf"], [("ps", 4)])
            self.cp(csb[s], self.ps[4][:, 0:32], [("ps", 4)], [K("csb")])
            self.tt(tmp16[s], csb[s][:, 16:32], csb[s][:, 0:16], ALU.subtract, [K("csb"), K("tmp16")], [K("tmp16")])
            self.act(wend[s], tmp16[s], AF.Exp, [K("tmp16")], [K("wend")])
            self.act(ecs[s], csb[s][:, 0:16], AF.Exp, [K("csb")], [K("ecs")])
            self.act(cdec[s], csb[s][:, 16:32], AF.Exp, [K("csb")], [K("cdec")])
            self.tr(self.ps[4][0:16, 256:384], csb[s][:, 0:16], self.ident_f, [K("csb"), "cf"], [("ps", 4)])
            self.cp(csT[s][0:16, :], self.ps[4][0:16, 256:384], [("ps", 4)], [K("csT")])
            for k3 in range(3):
                self.cp(csT3[s][0:16, k3 * 128:(k3 + 1) * 128], csT[s][0:16, :], [K("csT")], [K("csT3")])
                if k3 < 2:
                    self.tt(csT[s][0:16, :], csT[s][0:16, :], csT3[s][0:16, k3 * 128:(k3 + 1) * 128], ALU.subtract,
                            [K("csT"), K("csT3")], [K("csT")])
            self.tt(h3(xd[s]), h3(xtk[s]), dtb[s].rearrange("p (h o) -> p h o", o=1).to_broadcast([128, 16, 64]),
                    ALU.mult, [L, K("dtb")], [K("xd")])
            self.tt(h3(xdw[s]), h3(xd[s]), wend[s].rearrange("p (h o) -> p h o", o=1).to_broadcast([128, 16, 64]),
                    ALU.mult, [K("xd"), K("wend")], [K("xdw")])

        CSB = (0, 1, 5)

        def stG(c, g):
            s = c % 2
            g3_ = g % 3
            self.mm(self.ps[4][:, 128:256], g3(bTc[s])[:, g, :], g3(cTc[s])[:, g, :], True, True, [("ld", s)], [("ps", 4)])
            self.tt(cbm[g3_], self.ps[4][:, 128:256], self.trile_f, ALU.mult, [("ps", 4), "cf"], [("cbm", g3_)])

        def stA(c, h):
            s = c % 2
            g, hh = divmod(h, 4)
            g3_ = g % 3
            bk = CSB[h % 3]
            i3 = h % 4
            if hh == 0:
                if g == 0:
                    stG(c, 1)
                    stG(c, 2)
                elif g == 1:
                    stG(c, 3)
            for k3 in range(3):
                self.mm(self.ps[bk][:, 0:128], sel[0:16, h * 128:(h + 1) * 128], csT3[s][0:16, k3 * 128:(k3 + 1) * 128],
                        k3 == 0, k3 == 2, ["sel", ("csT3", s)], [("ps", bk)])
            self.ts(dec[i3], self.ps[bk][:, 0:128], csb[s][:, h:h + 1], 0.0, ALU.subtract, ALU.min,
                    [("ps", bk), ("csb", s)], [("dec", i3)])
            self.act(dec[i3], dec[i3], AF.Exp, [("dec", i3)], [("dec", i3)])
            self.tt(Mh[i3], dec[i3], cbm[g3_], ALU.mult, [("dec", i3), ("cbm", g3_)], [("Mh", i3)], eng="pool")

        def stB(c, h):
            s = c % 2
            L = ("ld", s)
            g, hh = divmod(h, 4)
            i3 = h % 4
            hs_ = slice(h * 64, (h + 1) * 64)
            yb = 2 + h % 2
            KB = ("ps", yb)
            YK = ("ytok", s, h)
            py, po, pst = self.ps[yb][:, 0:64], self.ps[yb][:, 64:128], self.ps[yb][:, 128:192]
            self.mm(py, Mh[i3], xd[s][:, hs_], True, True, [("Mh", i3), ("xd", s)], [KB])
            if c > 0:
                self.mm(po, g3(cTc[s])[:, g, :], prevb[:, hs_], True, True, [L, ("prevb", h)], [KB])
            if c < 15:
                self.mm(pst, btk[s][:, g * 128:(g + 1) * 128], xdw[s][:, hs_], True, True, [L, ("xdw", s)], [KB])
            self.stt(ytok[s][:, hs_], xtk[s][:, hs_], bc[:, 32 + h:33 + h], py, ALU.mult, ALU.add, [L, "bc16", KB], [YK])
            if c > 0:
                self.stt(ytok[s][:, hs_], po, ecs[s][:, h:h + 1], ytok[s][:, hs_], ALU.mult, ALU.add,
                         [KB, ("ecs", s), YK], [YK])
            if c < 15:
                if c == 0:
                    self.cp(prevT[:, hs_], pst, [KB], [("prevT", h)])
                else:
                    self.stt(prevT[:, hs_], prevT[:, hs_], cdec[s][:, h:h + 1], pst, ALU.mult, ALU.add,
                             [("prevT", h), ("cdec", s), KB], [("prevT", h)])
                self.cp(prevb[:, hs_], prevT[:, hs_], [("prevT", h)], [("prevb", h)], eng="act")

        def epilogue(c):
            s = c % 2
            yk = [("ytok", s, h) for h in range(16)]
            self.tt(gy, ytok[s], szc[s], ALU.mult, yk + [("ld", s)], ["gy"], eng="pool")
            self.act(junk, gy, AF.Square, ["gy"], ["junk"])
            self.P.op("dve", lambda e: e.reduce_sum(out=ss1[:, 0:1], in_=junk, axis=AX.X), ["junk"], ["ss1"])
            self.act(ss1[:, 0:1], ss1[:, 0:1], AF.Ln, ["ss1"], ["ss1"], bias=self.epsc, scale=1.0 / 1024)
            self.act(ss1[:, 0:1], ss1[:, 0:1], AF.Exp, ["ss1"], ["ss1"], scale=-0.5)
            self.stt(yn, gy, ss1[:, 0:1], ssmw, ALU.mult, ALU.mult, ["gy", "ss1", "ssmw"], ["yn"])
            pb = 6 + s
            psT = self.ps[pb][:].bitcast(BF16)
            for j in range(8):
                self.tr(psT[:, j * 128:(j + 1) * 128], yn[:, j * 128:(j + 1) * 128], self.ident_b, ["yn", "cb"], [("ps", pb)])
            self.cp(mixst[s], psT, [("ps", pb)], [("mixst", s)], eng="act")
            self.dma(sc["mixs"][:, c * 128:(c + 1) * 128].rearrange("(j p) t -> p j t", p=128),
                     mixst[s].rearrange("p (j t) -> p j t", t=128), [("mixst", s)], [("mixs", c)])

        load(0)
        load(1)
        prologue(0)
        stG(0, 0)
        for c in range(16):
            for it in range(-3, 16):
                if it + 3 < 16:
                    stA(c, it + 3)
                if it >= 0:
                    stB(c, it)
                if it == 2 and c > 0:
                    epilogue(c - 1)
                if it == 3 and c > 0 and c + 1 < 16:
                    load(c + 1)
                if it == 8 and c + 1 < 16:
                    prologue(c + 1)
                if it == 13 and c + 1 < 16:
                    stG(c + 1, 0)
        epilogue(15)

    def wout_pre_gen(self, sc, t0, XT, xk):
        self.dma(self.rstd2[:], sc["rsb"][:, t0:t0 + TT], [], ["rstd2"])
        slots = {}

        def issue(h):
            s = self.stgslot()
            self.dma(self.stg[s][:], sc["oT_raw"][h * 128:(h + 1) * 128, t0:t0 + TT], [], [("stg", s)])
            slots[h] = s

        issue(0)
        issue(1)
        for c in range(8):
            self.dma(XT[:, 8 + c, :], sc["mixs"][c * 128:(c + 1) * 128, t0:t0 + TT], [], [(xk, 8 + c)])
        for h in range(8):
            if h + 2 < 8:
                issue(h + 2)
            s = slots.pop(h)
            self.stt(XT[:, h, :], self.stg[s][:], self.pw2[:, h:h + 1], self.rstd2[:], ALU.mult, ALU.mult,
                     [("stg", s), "pw2", "rstd2"], [(xk, h)])
            self.stgrel(s)
            yield

    def wout(self, hT_in, hT_out, sc, yraw, next_pre):
        w_out = self.w["w_out"]
        self.carve_reset()
        XTb = self.carve(16 * TT, BF16).rearrange("p (c t) -> p c t", t=TT)
        XTc = self.carve(16 * TT, BF16).rearrange("p (c t) -> p c t", t=TT)
        g0 = self.bg_add(self.wout_pre_gen(sc, 0, self.XT, "XT"))
        self.bg_finish(g0)
        g1 = self.bg_add(self.wout_pre_gen(sc, TT, XTc, "XTc"))
        for t0, XT, xk in ((0, self.XT, "XT"), (TT, XTc, "XTc")):
            if t0:
                self.bg_finish(g1)
            for dc in range(16):
                if dc % 4 == 0:
                    k = self.wslot()
                    self.dma(self.W[k][:], w_out[:, dc * 128:(dc + 4) * 128].rearrange("(c p) m -> p c m", p=128), [],
                             [("W", k)], q="pool")
                j = dc % 4
                b0 = (dc % 2) * 2
                for fc in range(16):
                    for t2 in range(2):
                        self.mm(self.ps[b0 + t2][:], self.W[k][:, fc, j * 128:(j + 1) * 128],
                                XT[:, fc, t2 * 512:(t2 + 1) * 512], fc == 0, fc == 15, [("W", k), (xk, fc)],
                                [("ps", b0 + t2)])
                self.post_evac(dc, t0, b0, yraw)
                self.bg_step(3)
            self.add_post(hT_in, hT_out, t0, 3, yraw)
            if t0 == 0:
                self.next_pre_gen = self.bg_add(next_pre(XTb))

    def memattn(self, hT_in, hT_out, yraw, pre0, next_pre):
        w = self.w
        self.carve_reset()
        XTb = self.carve(16 * TT, BF16).rearrange("p (c t) -> p c t", t=TT)
        kvw = self.carve(2048, F32)
        mnb = self.carve(2048, BF16)
        mT = self.carve(4096, BF16)
        mT3 = mT.rearrange("p (c m) -> p c m", m=256)
        kTm = self.carve(1024, BF16)
        vm = self.carve(1024, BF16)
        qTm = self.carve(4096, BF16)
        qTm3 = qTm.rearrange("p (h t) -> p h t", t=TT)
        omT = self.carve(4096, BF16)
        omT3 = omT.rearrange("p (h t) -> p h t", t=TT)
        sc2 = [self.carve(4, F32) for _ in range(2)]
        pb_ = [self.carve(256, F32) for _ in range(2)]
        pn = [self.carve(256, BF16) for _ in range(2)]
        pT = [self.carve(256, BF16) for _ in range(2)]
        junk = omT.bitcast(F32)
        msc = self.carve(4, F32)
        scale = 128.0 ** -0.5
        self.dma(kvw, w["mem_kv_norm"].to_broadcast([128, 2048]), [], ["kvw"])
        ka = self.wslot()
        self.dma(self.W[ka][:], w["w_mkv"][:, 0:512].rearrange("(c p) f -> p c f", p=128), [], [("W", ka)], q="pool")
        kb = self.wslot()
        self.dma(self.W[kb][:], w["w_mkv"][:, 512:1024].rearrange("(c p) f -> p c f", p=128), [], [("W", kb)], q="pool")
        mstg = [self.carve(1024, F32) for _ in range(2)]

        class _M:
            def __getitem__(s_, k):
                return mstg[k]
        for mb in range(2):
            sa, sb_ = 0, 1
            self.dma(mstg[0], self.mem[mb * 128:(mb + 1) * 128, 0:1024], [], [("mstg", 0)])
            self.dma(mstg[1], self.mem[mb * 128:(mb + 1) * 128, 1024:2048], [], [("mstg", 1)])
            self.tt(junk[:, 0:1024], mstg[0], mstg[0], ALU.mult, [("mstg", 0)], ["junk"])
            self.tt(junk[:, 1024:2048], mstg[1], mstg[1], ALU.mult, [("mstg", 1)], ["junk"])
            self.P.op("dve", lambda e: e.reduce_sum(out=msc[:, 0:1], in_=junk, axis=AX.X), ["junk"], ["msc"])
            self.act(msc[:, 0:1], msc[:, 0:1], AF.Ln, ["msc"], ["msc"], bias=self.epsc, scale=1.0 / D)
            self.act(msc[:, 0:1], msc[:, 0:1], AF.Exp, ["msc"], ["msc"], scale=-0.5)
            self.stt(mnb[:, 0:1024], mstg[0], msc[:, 0:1], kvw[:, 0:1024], ALU.mult, ALU.mult,
                     [("mstg", 0), "msc", "kvw"], ["mnb"])
            self.stt(mnb[:, 1024:2048], mstg[1], msc[:, 0:1], kvw[:, 1024:2048], ALU.mult, ALU.mult,
                     [("mstg", 1), "msc", "kvw"], ["mnb"])
            for q8 in range(2):
                pbk = 4 + q8
                psT = self.ps[pbk][:].bitcast(BF16)
                for j in range(8):
                    dc = q8 * 8 + j
                    self.tr(psT[:, j * 128:(j + 1) * 128], mnb[:, dc * 128:(dc + 1) * 128], self.ident_b, ["mnb", "cb"],
                            [("ps", pbk)])
                self.cp(mT3[:, q8 * 8:(q8 + 1) * 8, mb * 128:(mb + 1) * 128], psT.rearrange("p (c m) -> p c m", m=128),
                        [("ps", pbk)], ["mT"], eng=("act" if q8 else "dve"))
        for h in range(4):
            b = h % 2
            for dc in range(16):
                self.mm(self.ps[b][:, 0:256], self.W[ka][:, dc, h * 128:(h + 1) * 128], mT3[:, dc, :], dc == 0, dc == 15,
                        [("W", ka), "mT"], [("ps", b)])
            self.cp(kTm[:, h * 256:(h + 1) * 256], self.ps[b][:, 0:256], [("ps", b)], ["kTm"], eng=("act" if b else "dve"))
            self.bg_step(2)
        for mb in range(2):
            for dc in range(16):
                self.mm(self.ps[2 + mb][:], mT3[:, dc, mb * 128:(mb + 1) * 128], self.W[kb][:, dc, :], dc == 0, dc == 15,
                        [("W", kb), "mT"], [("ps", 2 + mb)])
            self.cp(vm[:, mb * 512:(mb + 1) * 512], self.ps[2 + mb][:], [("ps", 2 + mb)], ["vm"], eng=("act" if mb else "dve"))
        for t0 in (0, TT):
            if t0 == 0:
                self.bg_finish(pre0)
                XT, xk = XTb, "XTb"
                pre1 = self.bg_add(self.prepass_gen(hT_in, TT, 4, self.XT, "XT"))
            else:
                self.bg_finish(pre1)
                XT, xk = self.XT, "XT"
            kq = self.wslot()
            self.dma(self.W[kq][:], w["w_mq"].rearrange("(c p) f -> p c f", p=128), [], [("W", kq)], q="pool")
            for h in range(4):
                b0 = (h % 2) * 2
                for kc in range(16):
                    for t2 in range(2):
                        self.mm(self.ps[b0 + t2][:], self.W[kq][:, kc, h * 128:(h + 1) * 128],
                                XT[:, kc, t2 * 512:(t2 + 1) * 512], kc == 0, kc == 15, [("W", kq), (xk, kc)],
                                [("ps", b0 + t2)])
                self.cp(qTm3[:, h, 0:512], self.ps[b0][:], [("ps", b0)], [("qTm", h)], eng="act")
                self.cp(qTm3[:, h, 512:1024], self.ps[b0 + 1][:], [("ps", b0 + 1)], [("qTm", h)], eng="dve")
                self.bg_step(2)
            ko = self.wslot()
            Wo = self.W[ko][:].rearrange("p c f -> p (c f)").rearrange("p (c m) -> p c m", m=2048)
            self.dma(Wo, w["w_mo"].rearrange("(c p) m -> p c m", p=128), [], [("W", ko)], q="pool")
            def mA(i):
                tb, h = divmod(i, 4)
                i2 = i % 2
                scp = self.ps[i2][:, 0:256]
                self.mm(scp, qTm3[:, h, tb * 128:(tb + 1) * 128], kTm[:, h * 256:(h + 1) * 256], True, True,
                        [("qTm", h), "kTm"], [("ps", i2)])
                s4 = sc2[i2]
                self.P.op("dve", lambda e, s4=s4, scp=scp: e.reduce_max(out=s4[:, 0:1], in_=scp, axis=AX.X),
                          [("ps", i2)], [("s4", i2)])
                self.ts(s4[:, 1:2], s4[:, 0:1], -scale, None, ALU.mult, None, [("s4", i2)], [("s4", i2)])
                self.act(pb_[i2], scp, AF.Exp, [("ps", i2), ("s4", i2)], [("pb", i2)], bias=s4[:, 1:2], scale=scale)
                self.P.op("dve", lambda e, s4=s4, i2=i2: e.reduce_sum(out=s4[:, 2:3], in_=pb_[i2], axis=AX.X),
                          [("pb", i2)], [("s4", i2)])
                self.P.op("dve", lambda e, s4=s4: e.reciprocal(out=s4[:, 3:4], in_=s4[:, 2:3]), [("s4", i2)],
                          [("s4", i2)])
                self.ts(pn[i2], pb_[i2], s4[:, 3:4], None, ALU.mult, None, [("pb", i2), ("s4", i2)], [("pn", i2)])

            def mB(i):
                tb, h = divmod(i, 4)
                i2 = i % 2
                pTb = self.ps[4 + i2][:].bitcast(BF16)
                for mb in range(2):
                    self.tr(pTb[:, mb * 128:(mb + 1) * 128], pn[i2][:, mb * 128:(mb + 1) * 128],
                            self.ident_b, [("pn", i2), "cb"], [("ps", 4 + i2)])
                self.cp(pT[i2], pTb[:, 0:256], [("ps", 4 + i2)], [("pT", i2)], eng="act")
                op_ = self.ps[2 + i2][:, 0:128]
                for mb in range(2):
                    self.mm(op_, vm[:, mb * 512 + h * 128:mb * 512 + (h + 1) * 128], pT[i2][:, mb * 128:(mb + 1) * 128],
                            mb == 0, mb == 1, ["vm", ("pT", i2)], [("ps", 2 + i2)])
                self.cp(omT3[:, h, tb * 128:(tb + 1) * 128], op_, [("ps", 2 + i2)], [("omT", h, tb)],
                        eng=("act" if i2 else "dve"))

            for it in range(-1, 32):
                if it + 1 < 32:
                    mA(it + 1)
                if it >= 0:
                    mB(it)
                self.bg_step(1)
            if t0 and next_pre is not None:
                self.next_pre_gen = self.bg_add(next_pre())
            for dc in range(16):
                b0 = (dc % 2) * 2
                for h in range(4):
                    for t2 in range(2):
                        self.mm(self.ps[b0 + t2][:], Wo[:, h, dc * 128:(dc + 1) * 128], omT3[:, h, t2 * 512:(t2 + 1) * 512],
                                h == 0, h == 3, [("W", ko)] + [("omT", h, tb) for tb in range(t2 * 4, t2 * 4 + 4)],
                                [("ps", b0 + t2)])
                self.post_evac(dc, t0, b0, yraw)
                self.bg_step(3)
            self.add_post(hT_in, hT_out, t0, 5, yraw)

    def build(self, st):
        nc = self.nc
        self.alloc(st)
        self.epsc = EPS
        self.setup()
        hT = [self.dram("hT%d" % i, [D, S], F32) for i in range(4)]
        yraw = self.dram("yraw", [D, S], F32)
        sc = {}
        for n, shp, dt in (("qT", [1024, S], BF16), ("kT", [1024, S], BF16), ("v_tok", [S, 1024], BF16),
                           ("sz_tok", [S, 1024], F32), ("x_tok", [S, 1024], BF16), ("b_tok", [S, 512], BF16),
                           ("bT", [512, S], BF16), ("cT", [512, S], BF16), ("oT_raw", [1024, S], F32),
                           ("rsb", [128, S], F32), ("mixs", [1024, S], BF16)):
            sc[n] = self.dram(n, shp, dt)
        g0 = self.phase0_gen(hT[0])
        for _ in range(8):
            next(g0)
        if self.stages[0] != "ffn1":
            for _ in g0:
                pass
            g0 = None
        w = self.w
        stages = self.stages
        PRE_I = {"ffn1": 0, "mix": 2, "mem": 4, "ffn2": 6}
        cur = 0
        pre0 = None
        for si, stg in enumerate(stages):
            last = si == len(stages) - 1
            nxt = stages[si + 1] if not last else None
            hin, hout = hT[cur], (hT[cur + 1] if cur + 1 < 4 else None)
            if nxt is None:
                next_pre = None
            elif stg == "mix":
                next_pre = (lambda XTb, hout=hout, nxt=nxt: self.prepass_gen(hout, 0, PRE_I[nxt], XTb, "XTb"))
            else:
                next_pre = (lambda hout=hout, nxt=nxt: self.prepass_gen(hout, 0, PRE_I[nxt], self.XT, "XT"))
            if pre0 is None and stg != "mem":
                pre0 = self.bg_add(self.prepass_gen(hin, 0, PRE_I[stg], self.XT, "XT"))
                if g0 is not None:
                    self.bg_add(g0)
                    g0 = None
            self.next_pre_gen = None
            if stg in ("ffn1", "ffn2"):
                self.P.barrier()
                p = "ffn1" if stg == "ffn1" else "ffn2"
                self.ffn(hin, hout, w[p + "_wg"], w[p + "_wu"], w[p + "_wd"], PRE_I[stg], PRE_I[stg] + 1, yraw, pre0,
                         next_pre=next_pre, final=last)
            elif stg == "mix":
                self.P.barrier()
                self.inproj(hin, sc, pre0)
                self.P.barrier()
                self.sbattn(sc)
                self.P.barrier()
                self.ssd(sc)
                self.P.barrier()
                self.wout(hin, hout, sc, yraw, next_pre if next_pre is not None else (lambda XTb: iter(())))
            elif stg == "mem":
                self.P.barrier()
                if pre0 is None:
                    self.carve_reset()
                    XTb0 = self.carve(16 * TT, BF16).rearrange("p (c t) -> p c t", t=TT)
                    pre0 = self.bg_add(self.prepass_gen(hin, 0, 4, XTb0, "XTb"))
                self.memattn(hin, hout, yraw, pre0, next_pre)
            pre0 = self.next_pre_gen
            cur += 1
            if last:
                self.bg_drain()
                self.P.barrier()
                if stg not in ("ffn1", "ffn2"):
                    self.to_out(hT[cur])
        self.bg_drain()
        self.P.barrier()
        self.P.emit(st)

    def to_out(self, hT_src):
        for t0 in (0, TT):
            for dc in range(16):
                s1 = self.stgslot()
                s2 = self.stgslot()
                self.dma(self.stg[s1][:], hT_src[dc * 128:(dc + 1) * 128, t0:t0 + TT], [], [("stg", s1)])
                for q4 in range(2):
                    b = q4
                    for j in range(4):
                        jj = q4 * 4 + j
                        self.tr(self.ps[b][:, j * 128:(j + 1) * 128], self.stg[s1][:, jj * 128:(jj + 1) * 128],
                                self.ident_f, [("stg", s1), "cf"], [("ps", b)])
                    self.cp(self.stg[s2][:, q4 * 512:(q4 + 1) * 512], self.ps[b][:], [("ps", b)], [("stg", s2)],
                            eng=("dve" if q4 == 0 else "act"))
                self.dma(self.out[t0:t0 + TT, dc * 128:(dc + 1) * 128].rearrange("(j p) d -> p j d", p=128),
                         self.stg[s2][:].rearrange("p (j d) -> p j d", d=128), [("stg", s2)], [("out", dc, t0)])
                self.stgrel(s1)
                self.stgrel(s2)


_CACHE = {}


def run(inputs, stages, dbg=(), ncores=8):
    key = (tuple(stages), tuple(dbg))
    if key not in _CACHE:
        kb = K(list(stages), dbg)
        with ExitStack() as st:
            kb.build(st)
        _CACHE[key] = kb
    kb = _CACHE[key]
    consts = make_consts()
    in_maps = []
    for c in range(ncores):
        m = {"x": np.ascontiguousarray(inputs["x"][c]), "mem": np.ascontiguousarray(inputs["mem"][c]),
             "consts": consts, "selc": make_sel()}
        for n in WNAMES:
            a = np.asarray(inputs[n])
            m[n] = np.ascontiguousarray(a.reshape(WSHAPES[n]))
        in_maps.append(m)
    res = run_bass_kernel_spmd(kb.nc, in_maps, core_ids=list(range(ncores)))
    return res


ALL_STAGES = ("ffn1", "mix", "mem", "ffn2")


def kernel(**inputs):
    inputs = {k: np.asarray(v) for k, v in inputs.items()}
    res = run(inputs, ALL_STAGES)
    return np.stack([r["out"] for r in res.results], axis=0).astype(np.float32)
```

```python
import numpy as np
from contextlib import ExitStack
import concourse.bass as bass
import concourse.mybir as mybir
from concourse.bass_utils import run_bass_kernel_spmd

F32 = mybir.dt.float32
BF16 = mybir.dt.bfloat16
AF = mybir.ActivationFunctionType
ALU = mybir.AluOpType
AX = mybir.AxisListType

S = 2048
D = 2048
DFF = 5504
NFC = 43
EPS = 1e-6
TT = 1024

ENGS = ("pe", "act", "dve", "pool", "sp")
NDMA_SEM = 8


class Op:
    __slots__ = ("eng", "fn", "reads", "writes", "dma", "signal", "sem", "val", "waits", "k", "bar")

    def __init__(self, eng, fn, reads, writes, dma, bar=False):
        self.eng, self.fn, self.reads, self.writes, self.dma = eng, fn, tuple(reads), tuple(writes), dma
        self.signal = dma
        self.sem = None
        self.val = 0
        self.waits = []
        self.k = 0
        self.bar = bar


class Prog:
    def __init__(self, nc):
        self.nc = nc
        self.ops = []

    def op(self, eng, fn, reads=(), writes=()):
        o = Op(eng, fn, reads, writes, False)
        self.ops.append(o)
        return o

    def dma(self, eng, out, in_, reads=(), writes=()):
        def fn(e, out=out, in_=in_):
            return e.dma_start(out=out, in_=in_)
        o = Op(eng, fn, reads, writes, True)
        self.ops.append(o)
        return o

    def barrier(self):
        for e in ENGS:
            self.ops.append(Op(e, None, (), (), False, bar=True))

    def emit(self, stack):
        nc = self.nc
        ops = self.ops
        esem = {e: stack.enter_context(nc.semaphore("s_" + e)) for e in ENGS}
        dsem = {e: [stack.enter_context(nc.semaphore("d_%s%d" % (e, i))) for i in range(NDMA_SEM)]
                for e in ("sp", "act", "pool")}
        dcnt = {e: 0 for e in dsem}
        for o in ops:
            if o.dma:
                o.k = dcnt[o.eng] % NDMA_SEM
                dcnt[o.eng] += 1
        last_w = {}
        readers = {}
        deps_of = []
        last_real = {e: None for e in ENGS}
        last_dma = {}
        for i, o in enumerate(ops):
            if o.bar:
                need = [j for e, j in last_real.items() if j is not None and e != o.eng]
                need += list(last_dma.values())
                for j in need:
                    ops[j].signal = True
                deps_of.append(need)
                continue
            deps = {}
            for r in o.reads:
                j = last_w.get(r)
                if j is not None:
                    deps[j] = "raw"
            for w in o.writes:
                j = last_w.get(w)
                if j is not None and j not in deps:
                    deps[j] = "waw"
                for j in readers.get(w, ()):
                    if j not in deps and j != i:
                        deps[j] = "war"
            need = []
            for j, kind in deps.items():
                p = ops[j]
                if p.dma or o.dma:
                    need.append(j)
                elif p.eng == o.eng:
                    if o.eng != "pe":
                        need.append(j)
                else:
                    need.append(j)
            for j in need:
                ops[j].signal = True
            deps_of.append(need)
            for r in o.reads:
                readers.setdefault(r, []).append(i)
            for w in o.writes:
                last_w[w] = i
                readers[w] = []
            if o.dma:
                last_dma[(o.eng, o.k)] = i
            elif o.fn is not None:
                last_real[o.eng] = i
        cnt = {e: 0 for e in ENGS}
        dval = {e: [0] * NDMA_SEM for e in dsem}
        for o in ops:
            if o.dma:
                o.sem = dsem[o.eng][o.k]
                prev = dval[o.eng][o.k]
                if prev:
                    o.waits.append((o.sem, prev))
                dval[o.eng][o.k] = prev + 16
                o.val = prev + 16
            elif o.signal:
                cnt[o.eng] += 1
                o.sem = esem[o.eng]
                o.val = cnt[o.eng]
        for i, o in enumerate(ops):
            for j in deps_of[i]:
                o.waits.append((ops[j].sem, ops[j].val))
        per_eng = {e: [o for o in ops if o.eng == e] for e in ENGS}
        self.stats = {e: len(per_eng[e]) for e in ENGS}

        def run(eng_obj, name):
            known = {}
            for o in per_eng[name]:
                best = {}
                for s, v in o.waits:
                    key = id(s)
                    if v > known.get(key, 0) and v > best.get(key, (None, 0))[1]:
                        best[key] = (s, v)
                for key, (s, v) in best.items():
                    eng_obj.wait_ge(s, v)
                    known[key] = v
                if o.fn is None:
                    continue
                ins = o.fn(eng_obj)
                if o.dma:
                    ins.then_inc(o.sem, 16)
                elif o.signal:
                    ins.then_inc(o.sem, 1)

        with nc.Block() as block:
            @block.tensor
            def _(e):
                run(e, "pe")

            @block.scalar
            def _(e):
                run(e, "act")

            @block.vector
            def _(e):
                run(e, "dve")

            @block.gpsimd
            def _(e):
                run(e, "pool")

            @block.sync
            def _(e):
                run(e, "sp")


WNAMES = ["ffn1_pre", "ffn1_wg", "ffn1_wu", "ffn1_wd", "ffn1_post",
          "mix_pre", "w_in", "conv_w", "conv_b", "dt_bias", "a_log", "d_skip", "sb_norm", "ssm_norm",
          "w_out", "mix_post", "mem_pre", "mem_kv_norm", "w_mq", "w_mkv", "w_mo", "mem_post",
          "ffn2_pre", "ffn2_wg", "ffn2_wu", "ffn2_wd", "ffn2_post"]
WSHAPES = {"ffn1_pre": [1, 2048], "ffn1_wg": [2048, 5504], "ffn1_wu": [2048, 5504], "ffn1_wd": [5504, 2048],
           "ffn1_post": [1, 2048], "mix_pre": [1, 2048], "w_in": [2048, 6160], "conv_w": [4, 2048],
           "conv_b": [1, 2048], "dt_bias": [1, 16], "a_log": [1, 16], "d_skip": [1, 16], "sb_norm": [1, 1024],
           "ssm_norm": [1, 1024], "w_out": [2048, 2048], "mix_post": [1, 2048], "mem_pre": [1, 2048],
           "mem_kv_norm": [1, 2048], "w_mq": [2048, 512], "w_mkv": [2048, 1024], "w_mo": [512, 2048],
           "mem_post": [1, 2048], "ffn2_pre": [1, 2048], "ffn2_wg": [2048, 5504], "ffn2_wu": [2048, 5504],
           "ffn2_wd": [5504, 2048], "ffn2_post": [1, 2048]}
NORMV = ["ffn1_pre", "ffn1_post", "mix_pre", "mix_post", "mem_pre", "mem_post", "ffn2_pre", "ffn2_post"]
NCONST = 2688
NSEL = 2048


def make_sel():
    sel = np.zeros((16, NSEL), np.float32)
    for h in range(16):
        sel[h, h * 128:(h + 1) * 128] = 1.0
    return sel


def make_consts():
    c = np.zeros((128, NCONST), np.float32)
    i = np.arange(128)
    c[:, 0:128] = np.eye(128, dtype=np.float32)
    c[:, 128:256] = (i[:, None] >= i[None, :])
    c[:, 256:384] = (i[:, None] <= i[None, :])
    c[:, 384:512] = 1.0
    t = np.arange(512)
    for k in range(4):
        c[:, 512 + 512 * k:1024 + 512 * k] = (t[None, :] - i[:, None] - 128 * k) <= 0
    c[:, 2560:2688] = -30000.0 * np.eye(128, dtype=np.float32)
    return c


class K:
    def __init__(self, stages, dbg=()):
        self.stages = stages
        nc = self.nc = bass.Bass("TRN2", target_bir_lowering=False)
        self.P = Prog(nc)
        self.dbg = set(dbg)
        self.x = nc.dram_tensor("x", [S, D], F32, kind="ExternalInput").ap()
        self.mem = nc.dram_tensor("mem", [256, D], F32, kind="ExternalInput").ap()
        self.consts = nc.dram_tensor("consts", [128, NCONST], F32, kind="ExternalInput").ap()
        self.selc = nc.dram_tensor("selc", [16, NSEL], F32, kind="ExternalInput").ap()
        self.w = {n: nc.dram_tensor(n, WSHAPES[n], F32, kind="ExternalInput").ap() for n in WNAMES}
        self.out = nc.dram_tensor("out", [S, D], F32, kind="ExternalOutput").ap()
        self.scr = {}
        self.wk = 0
        self.stgfree = [0, 1, 2, 3]
        self.sqk = 0

    def dram(self, name, shape, dt):
        kind = "ExternalOutput" if name in self.dbg else "Internal"
        t = self.nc.dram_tensor(name, shape, dt, kind=kind).ap()
        self.scr[name] = t
        return t

    def mm(self, out, lhsT, rhs, start, stop, reads, writes):
        self.P.op("pe", lambda e: e.matmul(out, lhsT=lhsT, rhs=rhs, start=start, stop=stop), reads, writes)

    def tr(self, out, in_, ident, reads, writes):
        self.P.op("pe", lambda e: e.transpose(out, in_, ident), reads, writes)

    def act(self, out, in_, func, reads, writes, bias=None, scale=None, accum=None):
        kw = {}
        if bias is not None:
            kw["bias"] = bias
        if scale is not None:
            kw["scale"] = scale
        if accum is not None:
            kw["accum_out"] = accum
        self.P.op("act", lambda e: e.activation(out=out, in_=in_, func=func, **kw), reads, writes)

    def tt(self, out, in0, in1, op, reads, writes, eng="dve"):
        self.P.op(eng, lambda e: e.tensor_tensor(out=out, in0=in0, in1=in1, op=op), reads, writes)

    def ts(self, out, in0, s1, s2, op0, op1, reads, writes, eng="dve"):
        if op1 is None:
            self.P.op(eng, lambda e: e.tensor_scalar(out=out, in0=in0, scalar1=s1, scalar2=None, op0=op0), reads, writes)
        else:
            self.P.op(eng, lambda e: e.tensor_scalar(out=out, in0=in0, scalar1=s1, scalar2=s2, op0=op0, op1=op1),
                      reads, writes)

    def stt(self, out, in0, scalar, in1, op0, op1, reads, writes, eng="dve"):
        self.P.op(eng, lambda e: e.scalar_tensor_tensor(out=out, in0=in0, scalar=scalar, in1=in1, op0=op0, op1=op1),
                  reads, writes)

    def cp(self, out, in_, reads, writes, eng="dve"):
        if eng == "act":
            self.P.op("act", lambda e: e.copy(out=out, in_=in_), reads, writes)
        else:
            self.P.op(eng, lambda e: e.tensor_copy(out=out, in_=in_), reads, writes)

    def dma(self, out, in_, reads, writes, q="sp"):
        self.P.dma(q, out, in_, reads, writes)

    def alloc(self, st):
        nc = self.nc
        sb = lambda n, shp, dt: st.enter_context(nc.sbuf_tensor(n, shp, dt))
        self.cf = sb("cf", [128, 512], F32)
        self.cb = sb("cb", [128, NCONST], BF16)
        self.ident_f = self.cf[:, 0:128]
        self.trile_f = self.cf[:, 256:384]
        self.ones_f = self.cf[:, 384:512]
        self.ident_b = self.cb[:, 0:128]
        self.trige_b = self.cb[:, 128:256]
        self.ones_b = self.cb[:, 384:512]
        self.sbmask = [self.cb[:, 512 + 512 * k:1024 + 512 * k] for k in range(4)]
        self.negid_b = self.cb[:, 2560:2688]
        self.nw = sb("nw", [128, 128], F32)
        self.pw2 = sb("pw2", [128, 88], F32)
        self.bc16 = sb("bc16", [128, 64], F32)
        self.dtraw = sb("dtraw", [128, 16, 16], F32)
        self.XT = sb("XT", [128, 16, TT], BF16)
        self.W = [sb("W%d" % i, [128, 16, 512], BF16) for i in range(3)]
        self.stg = [sb("stg%d" % i, [128, TT], F32) for i in range(4)]
        self.sq = [sb("sq%d" % i, [128, TT], BF16) for i in range(2)]
        self.rstd = sb("rstd", [128, TT], F32)
        self.rstd2 = sb("rstd2", [128, TT], F32)
        self.bg = []
        self.last_post = None
        self.sg = [sb("sg%d" % i, [128, 512], F32) for i in range(2)]
        self.BIG = sb("BIG", [128, NFC * TT], BF16)
        self.ps = [st.enter_context(nc.psum_tensor("ps%d" % i, [128, 512], F32)) for i in range(8)]
        self.coff = 0

    def carve_reset(self):
        self.coff = 0

    def carve(self, cols, dt):
        n = cols * (2 if dt == F32 else 1)
        n = (n + 15) // 16 * 16
        a = self.BIG[:, self.coff:self.coff + n]
        self.coff += n
        assert self.coff <= NFC * TT, self.coff
        if dt == F32:
            return a.bitcast(F32)[:, 0:cols]
        return a[:, 0:cols]

    def wslot(self):
        k = self.wk % 3
        self.wk += 1
        return k

    def stgslot(self):
        assert self.stgfree, "out of staging slots"
        return self.stgfree.pop(0)

    def stgrel(self, k):
        self.stgfree.append(k)

    def sqslot(self):
        k = self.sqk % 2
        self.sqk += 1
        return k

    def setup(self):
        w = self.w
        self.dma(self.cf[:], self.consts[:, 0:512], [], ["cf"])
        self.dma(self.cb[:], self.consts[:, :], [], ["cb"], q="pool")
        st1 = self.stg[0]
        for i, n in enumerate(NORMV):
            self.dma(st1[i * 16:(i + 1) * 16, 0:128], w[n].rearrange("o (c p) -> (o c) p", p=128), [], [("stg", 0)])
        self.tr(self.ps[7][:, 0:128], st1[:, 0:128], self.ident_f, [("stg", 0), "cf"], [("ps", 7)])
        self.cp(self.nw[:], self.ps[7][:, 0:128], [("ps", 7)], ["nw"])
        for i in (1, 7):
            self.P.op("act", lambda e, i=i: e.mul(out=self.nw[:, i * 16:(i + 1) * 16], in_=self.nw[:, i * 16:(i + 1) * 16],
                                                  mul=0.5), ["nw"], ["nw"])
        st2 = self.stg[1]
        self.dma(st2[0:8, 0:128], w["sb_norm"].rearrange("o (c p) -> (o c) p", p=128), [], [("stg", 1)])
        self.dma(st2[8:24, 0:128], w["conv_b"].rearrange("o (c p) -> (o c) p", p=128), [], [("stg", 1)])
        self.dma(st2[24:88, 0:128], w["conv_w"].rearrange("k (c p) -> (k c) p", p=128), [], [("stg", 1)])
        self.tr(self.ps[6][:, 0:88], st2[0:88, 0:128], self.ident_f[0:88, 0:88], [("stg", 1), "cf"], [("ps", 6)])
        self.cp(self.pw2[:], self.ps[6][:, 0:88], [("ps", 6)], ["pw2"])
        self.dma(self.bc16[:, 0:16], w["dt_bias"].to_broadcast([128, 16]), [], ["bc16"])
        self.dma(self.bc16[:, 16:32], w["a_log"].to_broadcast([128, 16]), [], ["bc16"])
        self.dma(self.bc16[:, 32:48], w["d_skip"].to_broadcast([128, 16]), [], ["bc16"])
        self.act(self.bc16[:, 16:32], self.bc16[:, 16:32], AF.Exp, ["bc16"], ["bc16"])
        self.P.op("act", lambda e: e.mul(out=self.bc16[:, 16:32], in_=self.bc16[:, 16:32], mul=-1.0), ["bc16"], ["bc16"])

    def phase0_gen(self, hT0):
        hid = id(hT0)
        nb = 0
        for tb in range(16):
            t0 = (tb // 8) * TT
            s = self.stgslot()
            s2 = self.stgslot()
            self.dma(self.stg[s][:], self.x[tb * 128:(tb + 1) * 128, 0:1024], [], [("stg", s)])
            self.dma(self.stg[s2][:], self.x[tb * 128:(tb + 1) * 128, 1024:2048], [], [("stg", s2)])
            for half, sl in ((0, s), (1, s2)):
                so = self.stgslot()
                for q4 in range(2):
                    b = 6 + nb % 2
                    nb += 1
                    for j in range(4):
                        dcl = q4 * 4 + j
                        self.tr(self.ps[b][:, j * 128:(j + 1) * 128], self.stg[sl][:, dcl * 128:(dcl + 1) * 128],
                                self.ident_f, [("stg", sl), "cf"], [("ps", b)])
                    self.cp(self.stg[so][:, q4 * 512:(q4 + 1) * 512], self.ps[b][:], [("ps", b)], [("stg", so)],
                            eng=("dve" if q4 == 0 else "act"))
                self.dma(hT0[half * 1024:(half + 1) * 1024, tb * 128:(tb + 1) * 128].rearrange("(c p) t -> p c t", p=128),
                         self.stg[so][:].rearrange("p (c t) -> p c t", t=128), [("stg", so)],
                         [("h", hid, dc, t0) for dc in range(half * 8, half * 8 + 8)])
                self.stgrel(so)
            self.stgrel(s)
            self.stgrel(s2)
            yield

    def stats_rstd(self, n, pb, rbuf, rkey):
        for t2 in range(2):
            self.act(rbuf[:, t2 * 512:(t2 + 1) * 512], self.ps[pb + t2][:], AF.Ln, [("ps", pb + t2)], [rkey],
                     bias=self.epsc, scale=1.0 / n)
        self.act(rbuf[:], rbuf[:], AF.Exp, [rkey], [rkey], scale=-0.5)

    def sq_stats(self, src, dc, reads, pb, ndc=16):
        q = self.sqslot()
        self.act(self.sq[q][:], src, AF.Square, reads, [("sq", q)])
        for t2 in range(2):
            self.mm(self.ps[pb + t2][:], self.ones_b, self.sq[q][:, t2 * 512:(t2 + 1) * 512], dc == 0, dc == ndc - 1,
                    [("sq", q), "cb"], [("ps", pb + t2)])

    def bg_add(self, g):
        self.bg.append(g)
        return g

    def bg_step(self, n=1):
        while n > 0 and self.bg:
            try:
                next(self.bg[0])
                n -= 1
            except StopIteration:
                self.bg.pop(0)

    def bg_finish(self, g):
        while any(x is g for x in self.bg):
            self.bg_step(1)

    def bg_drain(self):
        while self.bg:
            self.bg_step(1000)

    def prepass_gen(self, hT_in, t0, wi, XT, xk):
        hid = id(hT_in)
        LA = 2
        slots = {}

        def issue(dc):
            s = self.stgslot()
            self.dma(self.stg[s][:], hT_in[dc * 128:(dc + 1) * 128, t0:t0 + TT], [("h", hid, dc, t0)], [("stg", s)])
            slots[dc] = s

        for dc in range(LA):
            issue(dc)
        for dc in range(16):
            if dc + LA < 16:
                issue(dc + LA)
            s = slots.pop(dc)
            self.sq_stats(self.stg[s][:], dc, [("stg", s)], 6)
            self.stgrel(s)
            yield
        for dc in range(LA):
            issue(dc)
        self.stats_rstd(D, 6, self.rstd2, "rstd2")
        for dc in range(16):
            if dc + LA < 16:
                issue(dc + LA)
            s = slots.pop(dc)
            self.stt(XT[:, dc, :], self.stg[s][:], self.nw[:, wi * 16 + dc:wi * 16 + dc + 1], self.rstd2[:],
                     ALU.mult, ALU.mult, [("stg", s), "nw", "rstd2"], [(xk, dc)])
            self.stgrel(s)
            yield

    def post_evac(self, dc, t0, b0, yraw):
        q = self.sqslot()
        for t2 in range(2):
            self.cp(self.sg[t2][:], self.ps[b0 + t2][:], [("ps", b0 + t2)], [("sg", t2)], eng="act")
            self.act(self.sq[q][:, t2 * 512:(t2 + 1) * 512], self.sg[t2][:], AF.Square, [("sg", t2)], [("sq", q)])
            self.mm(self.ps[4 + t2][:], self.ones_b, self.sq[q][:, t2 * 512:(t2 + 1) * 512], dc == 0, dc == 15,
                    [("sq", q), "cb"], [("ps", 4 + t2)])
            self.dma(yraw[dc * 128:(dc + 1) * 128, t0 + t2 * 512:t0 + (t2 + 1) * 512], self.sg[t2][:], [("sg", t2)],
                     [("yraw", dc, t0, t2)])

    def add_post(self, *a, **kw):
        if self.last_post is not None:
            self.bg_finish(self.last_post)
        self.stats_rstd(D, 4, self.rstd, "rstd")
        self.last_post = self.bg_add(self.post_final_gen(*a, **kw))
        return self.last_post

    def post_final_gen(self, hT_in, hT_out, t0, wi, yraw, final=False):
        hid = id(hT_in)
        hod = id(hT_out)
        slots = {}

        def issue(dc):
            s1 = self.stgslot()
            s2 = self.stgslot()
            self.dma(self.stg[s1][:], yraw[dc * 128:(dc + 1) * 128, t0:t0 + TT],
                     [("yraw", dc, t0, 0), ("yraw", dc, t0, 1)], [("stg", s1)])
            self.dma(self.stg[s2][:], hT_in[dc * 128:(dc + 1) * 128, t0:t0 + TT], [("h", hid, dc, t0)], [("stg", s2)])
            slots[dc] = (s1, s2)

        issue(0)
        for dc in range(16):
            if dc + 1 < 16:
                issue(dc + 1)
            s1, s2 = slots.pop(dc)
            self.stt(self.stg[s1][:], self.stg[s1][:], self.nw[:, wi * 16 + dc:wi * 16 + dc + 1], self.rstd[:],
                     ALU.mult, ALU.mult, [("stg", s1), "nw", "rstd"], [("stg", s1)])
            self.tt(self.stg[s1][:], self.stg[s1][:], self.stg[s2][:], ALU.add, [("stg", s1), ("stg", s2)], [("stg", s1)])
            if not final:
                self.dma(hT_out[dc * 128:(dc + 1) * 128, t0:t0 + TT], self.stg[s1][:], [("stg", s1)], [("h", hod, dc, t0)])
            else:
                for q4 in range(2):
                    b = 6 + q4
                    for j in range(4):
                        jj = q4 * 4 + j
                        self.tr(self.ps[b][:, j * 128:(j + 1) * 128], self.stg[s1][:, jj * 128:(jj + 1) * 128],
                                self.ident_f, [("stg", s1), "cf"], [("ps", b)])
                    self.cp(self.stg[s2][:, q4 * 512:(q4 + 1) * 512], self.ps[b][:], [("ps", b)], [("stg", s2)],
                            eng=("dve" if q4 == 0 else "act"))
                self.dma(self.out[t0:t0 + TT, dc * 128:(dc + 1) * 128].rearrange("(j p) d -> p j d", p=128),
                         self.stg[s2][:].rearrange("p (j d) -> p j d", d=128), [("stg", s2)], [("out", dc, t0)])
            self.stgrel(s1)
            self.stgrel(s2)
            yield

    def ffn(self, hT_in, hT_out, wg, wu, wd, pre_i, post_i, yraw, pre0, next_pre=None, final=False):
        HT = self.BIG[:].rearrange("p (c t) -> p c t", t=TT)
        pre = pre0
        for t0 in (0, TT):
            self.bg_finish(pre)
            for g in range(11):
                ncols = 512 if g < 10 else 384
                c0 = g * 512
                ka = self.wslot()
                self.dma(self.W[ka][:, :, 0:ncols], wg[:, c0:c0 + ncols].rearrange("(c p) f -> p c f", p=128), [],
                         [("W", ka)], q="pool")
                kb = self.wslot()
                self.dma(self.W[kb][:, :, 0:ncols], wu[:, c0:c0 + ncols].rearrange("(c p) f -> p c f", p=128), [],
                         [("W", kb)], q="pool")
                for j in range(ncols // 128):
                    fc = g * 4 + j
                    for kc in range(16):
                        for t2 in range(2):
                            self.mm(self.ps[t2][:], self.W[ka][:, kc, j * 128:(j + 1) * 128],
                                    self.XT[:, kc, t2 * 512:(t2 + 1) * 512], kc == 0, kc == 15,
                                    [("W", ka), ("XT", kc)], [("ps", t2)])
                    for kc in range(16):
                        for t2 in range(2):
                            self.mm(self.ps[2 + t2][:], self.W[kb][:, kc, j * 128:(j + 1) * 128],
                                    self.XT[:, kc, t2 * 512:(t2 + 1) * 512], kc == 0, kc == 15,
                                    [("W", kb), ("XT", kc)], [("ps", 2 + t2)])
                    for t2 in range(2):
                        self.act(self.sg[t2][:], self.ps[t2][:], AF.Silu, [("ps", t2)], [("sg", t2)])
                        self.tt(HT[:, fc, t2 * 512:(t2 + 1) * 512], self.sg[t2][:], self.ps[2 + t2][:], ALU.mult,
                                [("sg", t2), ("ps", 2 + t2)], [("HT", fc, t2)])
                    self.bg_step(1)
            if t0 == 0:
                pre = self.bg_add(self.prepass_gen(hT_in, TT, pre_i, self.XT, "XT"))
            elif next_pre is not None:
                self.next_pre_gen = self.bg_add(next_pre())
            for dc in range(16):
                k = self.wslot()
                Wv = self.W[k][:].rearrange("p c f -> p (c f)")[:, 0:NFC * 128].rearrange("p (c m) -> p c m", m=128)
                self.dma(Wv, wd[:, dc * 128:(dc + 1) * 128].rearrange("(c p) m -> p c m", p=128), [], [("W", k)], q="pool")
                b0 = (dc % 2) * 2
                for fc in range(NFC):
                    for t2 in range(2):
                        self.mm(self.ps[b0 + t2][:], Wv[:, fc, :], HT[:, fc, t2 * 512:(t2 + 1) * 512], fc == 0,
                                fc == NFC - 1, [("W", k), ("HT", fc, t2)], [("ps", b0 + t2)])
                self.post_evac(dc, t0, b0, yraw)
                self.bg_step(3)
            self.add_post(hT_in, hT_out, t0, post_i, yraw, final=final)

    def inproj(self, hT_in, sc, pre0):
        w_in = self.w["w_in"]
        self.carve_reset()
        XTb = self.carve(16 * TT, BF16).rearrange("p (c t) -> p c t", t=TT)
        ev = [self.carve(1024, BF16) for _ in range(3)]
        evv = [self.carve(512, BF16) for _ in range(2)]
        evz = [self.carve(512, F32) for _ in range(2)]
        cbuf = [self.carve(1028, F32) for _ in range(2)]
        acc = [self.carve(1024, F32) for _ in range(2)]
        xtk = [self.carve(1024, BF16) for _ in range(2)]
        carry = self.carve(48, F32)
        wdt = self.carve(256, BF16).rearrange("p (c f) -> p c f", f=16)
        cnt = {"ev": 0, "evv": 0, "evz": 0, "cb": 0, "xtk": 0, "bank": 0, "tb": 0}
        deferred = []

        def nxt(k, n=2):
            v = cnt[k] % n
            cnt[k] += 1
            return v

        self.dma(wdt, w_in[:, 6144:6160].rearrange("(c p) f -> p c f", p=128), [], ["wdt"], q="pool")
        for t0 in (0, TT):
            ti = t0 // TT
            if ti == 0:
                self.bg_finish(pre0)
                XT, xk = self.XT, "XT"
                pre1 = self.bg_add(self.prepass_gen(hT_in, TT, 2, XTb, "XTb"))
            else:
                self.bg_finish(pre1)
                XT, xk = XTb, "XTb"
            for g in range(4):
                ka = self.wslot()
                self.dma(self.W[ka][:], w_in[:, g * 512:(g + 1) * 512].rearrange("(c p) f -> p c f", p=128), [],
                         [("W", ka)], q="pool")
                for j in range(4):
                    ch = g * 4 + j
                    b0 = nxt("bank") * 2
                    for kc in range(16):
                        for t2 in range(2):
                            self.mm(self.ps[b0 + t2][:], self.W[ka][:, kc, j * 128:(j + 1) * 128],
                                    XT[:, kc, t2 * 512:(t2 + 1) * 512], kc == 0, kc == 15,
                                    [("W", ka), (xk, kc)], [("ps", b0 + t2)])
                    e = nxt("ev")
                    self.cp(ev[e][:, 0:512], self.ps[b0][:], [("ps", b0)], [("ev", e)], eng="act")
                    self.cp(ev[e][:, 512:1024], self.ps[b0 + 1][:], [("ps", b0 + 1)], [("ev", e)], eng="dve")
                    dst = sc["qT"] if ch < 8 else sc["kT"]
                    r0 = (ch % 8) * 128
                    self.dma(dst[r0:r0 + 128, t0:t0 + TT], ev[e], [("ev", e)], [("qk", ch, ti)])
                    self.bg_step(1)
            for which, base in (("v", 2048), ("z", 3072)):
                for ct in range(2):
                    ka = self.wslot()
                    self.dma(self.W[ka][:], w_in[:, base + ct * 512:base + (ct + 1) * 512].rearrange("(c p) f -> p c f", p=128),
                             [], [("W", ka)], q="pool")
                    for tb in range(8):
                        b = nxt("tb", 4)
                        for kc in range(16):
                            self.mm(self.ps[b][:], XT[:, kc, tb * 128:(tb + 1) * 128], self.W[ka][:, kc, :],
                                    kc == 0, kc == 15, [("W", ka), (xk, kc)], [("ps", b)])
                        r0 = t0 + tb * 128
                        self.bg_step(1)
                        if which == "v":
                            e = nxt("evv")
                            self.cp(evv[e], self.ps[b][:], [("ps", b)], [("evv", e)], eng=("dve" if tb % 2 else "act"))
                            self.dma(sc["v_tok"][r0:r0 + 128, ct * 512:(ct + 1) * 512], evv[e], [("evv", e)],
                                     [("vt", ct, tb, ti)])
                        else:
                            e = nxt("evz")
                            self.act(evz[e], self.ps[b][:], AF.Silu, [("ps", b)], [("evz", e)])
                            self.dma(sc["sz_tok"][r0:r0 + 128, ct * 512:(ct + 1) * 512], evz[e], [("evz", e)],
                                     [("zt", ct, tb, ti)])
            for g in range(4):
                ka = self.wslot()
                self.dma(self.W[ka][:], w_in[:, 4096 + g * 512:4096 + (g + 1) * 512].rearrange("(c p) f -> p c f", p=128),
                         [], [("W", ka)], q="pool")
                for j in range(4):
                    ch = g * 4 + j
                    b0 = nxt("bank") * 2
                    for kc in range(16):
                        for t2 in range(2):
                            self.mm(self.ps[b0 + t2][:], self.W[ka][:, kc, j * 128:(j + 1) * 128],
                                    XT[:, kc, t2 * 512:(t2 + 1) * 512], kc == 0, kc == 15,
                                    [("W", ka), (xk, kc)], [("ps", b0 + t2)])
                    c = nxt("cb")
                    if ti == 0:
                        self.P.op("dve", lambda e, c=c: e.memset(cbuf[c][:, 0:3], 0.0), [], [("cbuf", c)])
                    else:
                        self.cp(cbuf[c][:, 0:3], carry[:, ch * 3:ch * 3 + 3], ["carry"], [("cbuf", c)])
                    self.cp(cbuf[c][:, 3:515], self.ps[b0][:], [("ps", b0)], [("cbuf", c)], eng="act")
                    self.cp(cbuf[c][:, 515:1027], self.ps[b0 + 1][:], [("ps", b0 + 1)], [("cbuf", c)], eng="act")
                    if ti == 0:
                        self.cp(carry[:, ch * 3:ch * 3 + 3], cbuf[c][:, 1024:1027], [("cbuf", c)], ["carry"])
                    wcol = lambda k: self.pw2[:, 24 + k * 16 + ch:24 + k * 16 + ch + 1]
                    self.ts(acc[c], cbuf[c][:, 0:1024], wcol(0), None, ALU.mult, None, [("cbuf", c), "pw2"], [("acc", c)])
                    for k in range(1, 4):
                        self.stt(acc[c], cbuf[c][:, k:k + 1024], wcol(k), acc[c], ALU.mult, ALU.add,
                                 [("cbuf", c), ("acc", c), "pw2"], [("acc", c)])
                    e = nxt("ev", 3)
                    self.act(ev[e], acc[c], AF.Silu, [("acc", c), "pw2"], [("ev", e)], bias=self.pw2[:, 8 + ch:9 + ch])
                    if ch >= 12:
                        self.dma(sc["cT"][(ch - 12) * 128:(ch - 11) * 128, t0:t0 + TT], ev[e], [("ev", e)], [("ct", ch, ti)])
                    elif ch >= 8:
                        self.dma(sc["bT"][(ch - 8) * 128:(ch - 7) * 128, t0:t0 + TT], ev[e], [("ev", e)], [("bt", ch, ti)])
                    if ch < 12:
                        def tr_out(ch=ch, e=e, t0=t0, ti=ti):
                            xk_ = nxt("xtk")
                            pb = 4 + xk_
                            psT = self.ps[pb][:].bitcast(BF16)
                            for j8 in range(8):
                                self.tr(psT[:, j8 * 128:(j8 + 1) * 128], ev[e][:, j8 * 128:(j8 + 1) * 128], self.ident_b,
                                        [("ev", e), "cb"], [("ps", pb)])
                            self.cp(xtk[xk_], psT, [("ps", pb)], [("xtk", xk_)], eng="act")
                            if ch < 8:
                                dst = sc["x_tok"][t0:t0 + TT, ch * 128:(ch + 1) * 128]
                            else:
                                dst = sc["b_tok"][t0:t0 + TT, (ch - 8) * 128:(ch - 7) * 128]
                            self.dma(dst.rearrange("(j p) c -> p j c", p=128), xtk[xk_].rearrange("p (j c) -> p j c", c=128),
                                     [("xtk", xk_)], [("xo", ch, ti)])
                        deferred.append(tr_out)
                    else:
                        deferred.append(None)
                    while len(deferred) > 2:
                        f = deferred.pop(0)
                        if f is not None:
                            f()
            while deferred:
                f = deferred.pop(0)
                if f is not None:
                    f()
            for tb in range(8):
                b = nxt("tb", 4)
                for kc in range(16):
                    self.mm(self.ps[b][:, 0:16], XT[:, kc, tb * 128:(tb + 1) * 128], wdt[:, kc, :], kc == 0, kc == 15,
                            ["wdt", (xk, kc)], [("ps", b)])
                self.cp(self.dtraw[:, ti * 8 + tb, :], self.ps[b][:, 0:16], [("ps", b)], ["dtraw"])

    def sbattn(self, sc):
        self.bg_drain()
        self.carve_reset()
        kth = [self.carve(2048, BF16) for _ in range(2)]
        qth = [self.carve(2048, BF16) for _ in range(2)]
        vh = [self.carve(2048, BF16) for _ in range(2)]
        eb = [self.carve(512, F32) for _ in range(2)]
        spm = [self.carve(512, BF16) for _ in range(2)]
        t1 = [self.carve(512, F32) for _ in range(2)]
        arg = [self.carve(512, F32) for _ in range(2)]
        ab = [self.carve(512, BF16) for _ in range(2)]
        R = self.carve(512, F32)
        ob = [self.carve(512, F32) for _ in range(2)]
        sqo = [self.carve(512, BF16) for _ in range(2)]
        ssacc = self.carve(2048, F32)
        scale = 128.0 ** -0.5
        self.P.op("dve", lambda e: e.memset(ssacc, 0.0), [], ["ssacc"])
        pairs = []
        for h in range(8):
            for tt in range(4):
                for sb in range(4 * tt + 3, -1, -1):
                    pairs.append((h, tt, sb))
        n = len(pairs)

        def load(h):
            hs = h % 2
            self.dma(kth[hs], sc["kT"][h * 128:(h + 1) * 128, :], [], [("kq", hs)])
            self.dma(qth[hs], sc["qT"][h * 128:(h + 1) * 128, :], [], [("kq", hs)])
            self.dma(vh[hs].rearrange("p (c d) -> p c d", d=128),
                     sc["v_tok"][:, h * 128:(h + 1) * 128].rearrange("(c p) d -> p c d", p=128), [], [("kq", hs)])

        def S1(p):
            h, tt, sb = pairs[p]
            if tt == 0 and sb == 3:
                if h == 0:
                    load(0)
                if h + 1 < 8:
                    load(h + 1)
            zb = (0, 1, 7)[p % 3]
            diag = sb >= 4 * tt
            self.mm(self.ps[zb][:], kth[h % 2][:, sb * 128:(sb + 1) * 128], qth[h % 2][:, tt * 512:(tt + 1) * 512],
                    True, not diag, [("kq", h % 2)], [("ps", zb)])
            if diag:
                self.mm(self.ps[zb][:], self.negid_b, self.sbmask[sb - 4 * tt], False, True, ["cb"], [("ps", zb)])

        def S2(p):
            h, tt, sb = pairs[p]
            i = p % 2
            zb = (0, 1, 7)[p % 3]
            self.act(eb[i], self.ps[zb][:], AF.Exp, [("ps", zb)], [("eb", i)], scale=scale)
            self.act(spm[i], eb[i], AF.Ln, [("eb", i)], [("spm", i)], bias=1.0)

        def S3(p):
            i = p % 2
            self.mm(self.ps[2 + i][:], self.trige_b, spm[i], True, True, [("spm", i), "cb"], [("ps", 2 + i)])
            self.mm(self.ps[4 + i][:], self.ones_b, spm[i], True, True, [("spm", i), "cb"], [("ps", 4 + i)])

        def S4(p):
            h, tt, sb = pairs[p]
            i = p % 2
            first = sb == 4 * tt + 3
            if first:
                self.cp(t1[i], self.ps[2 + i][:], [("ps", 2 + i)], [("t1", i)])
            else:
                self.tt(t1[i], self.ps[2 + i][:], R, ALU.add, [("ps", 2 + i), "R"], [("t1", i)])
            zb = (0, 1, 7)[p % 3]
            self.stt(arg[i], self.ps[zb][:], scale, t1[i], ALU.mult, ALU.subtract, [("ps", zb), ("t1", i)], [("arg", i)])
            if first:
                self.cp(R, self.ps[4 + i][:], [("ps", 4 + i)], ["R"])
            elif sb > 0:
                self.tt(R, self.ps[4 + i][:], R, ALU.add, [("ps", 4 + i), "R"], ["R"])

        def S5(p):
            h, tt, sb = pairs[p]
            i = p % 2
            self.act(ab[i], arg[i], AF.Exp, [("arg", i)], [("ab", i)])

        def S6(p):
            h, tt, sb = pairs[p]
            i = p % 2
            til = (h * 4 + tt) % 2
            self.mm(self.ps[6][:], vh[h % 2][:, sb * 128:(sb + 1) * 128], ab[i], sb == 4 * tt + 3, sb == 0,
                    [("kq", h % 2), ("ab", i)], [("ps", 6)])
            if sb == 0:
                self.cp(ob[til], self.ps[6][:], [("ps", 6)], [("ob", til)], eng="act")
                self.dma(sc["oT_raw"][h * 128:(h + 1) * 128, tt * 512:(tt + 1) * 512], ob[til], [("ob", til)],
                         [("oraw", h, tt)])
                self.act(sqo[til], ob[til], AF.Square, [("ob", til)], [("sqo", til)])
                self.mm(self.ps[2][:], self.ones_b, sqo[til], True, True, [("sqo", til), "cb"], [("ps", 2)])
                self.tt(ssacc[:, tt * 512:(tt + 1) * 512], self.ps[2][:], ssacc[:, tt * 512:(tt + 1) * 512], ALU.add,
                        [("ps", 2), "ssacc"], ["ssacc"])

        for it in range(-3, n):
            if 0 <= it + 3 < n:
                S1(it + 3)
                S2(it + 3)
            if 0 <= it + 2 < n:
                S3(it + 2)
                S4(it + 2)
            if 0 <= it + 1 < n:
                S5(it + 1)
            if 0 <= it < n:
                S6(it)
        self.act(ssacc, ssacc, AF.Ln, ["ssacc"], ["ssacc"], bias=self.epsc, scale=1.0 / 1024)
        self.act(ssacc, ssacc, AF.Exp, ["ssacc"], ["ssacc"], scale=-0.5)
        self.dma(sc["rsb"][:, :], ssacc, ["ssacc"], ["rsb"])

    def ssd(self, sc):
        self.carve_reset()
        c2 = lambda n, dt: [self.carve(n, dt) for _ in range(2)]
        ssmw = self.carve(1024, F32)
        xtk, btk, bTc, cTc = c2(1024, BF16), c2(512, BF16), c2(512, BF16), c2(512, BF16)
        szc = c2(1024, F32)
        tmp16, dtb, da, csb, wend, ecs, cdec = (c2(16, F32), c2(16, F32), c2(16, F32), c2(32, F32), c2(16, F32),
                                                c2(16, F32), c2(16, F32))
        ss1 = self.carve(4, F32)
        xd, xdw = c2(1024, BF16), c2(1024, BF16)
        cbm = [self.carve(128, F32) for _ in range(3)]
        sel = self.carve(NSEL, BF16)
        csT = c2(128, F32)
        csR = c2(128, F32)
        csT3 = c2(384, BF16)
        self.dma(sel[0:16, :], self.selc, [], ["sel"], q="pool")
        dec = [self.carve(128, F32) for _ in range(4)]
        Mh = [self.carve(128, BF16) for _ in range(4)]
        ytok = c2(1024, F32)
        gy = self.carve(1024, F32)
        junk = self.carve(1024, F32)
        yn = self.carve(1024, BF16)
        prevT = self.carve(1024, F32)
        prevb = self.carve(1024, BF16)
        mixst = c2(1024, BF16)
        bc = self.bc16
        self.dma(ssmw, self.w["ssm_norm"].to_broadcast([128, 1024]), [], ["ssmw"])
        g3 = lambda ap: ap.rearrange("p (g t) -> p g t", t=128)
        h3 = lambda ap: ap.rearrange("p (h d) -> p h d", d=64)

        def load(c):
            s = c % 2
            r = slice(c * 128, (c + 1) * 128)
            self.dma(xtk[s], sc["x_tok"][r, :], [], [("ld", s)])
            self.dma(btk[s], sc["b_tok"][r, :], [], [("ld", s)])
            self.dma(g3(bTc[s]), sc["bT"][:, r].rearrange("(g p) t -> p g t", p=128), [], [("ld", s)])
            self.dma(g3(cTc[s]), sc["cT"][:, r].rearrange("(g p) t -> p g t", p=128), [], [("ld", s)])
            self.dma(szc[s], sc["sz_tok"][r, :], [], [("ld", s)])

        def prologue(c):
            s = c % 2
            L = ("ld", s)
            K = lambda n: (n, s)
            self.tt(tmp16[s], self.dtraw[:, c, :], bc[:, 0:16], ALU.add, ["dtraw", "bc16"], [K("tmp16")])
            self.act(tmp16[s], tmp16[s], AF.Exp, [K("tmp16")], [K("tmp16")])
            self.act(dtb[s], tmp16[s], AF.Ln, [K("tmp16")], [K("dtb")], bias=1.0)
            self.tt(da[s], dtb[s], bc[:, 16:32], ALU.mult, [K("dtb"), "bc16"], [K("da")])
            self.mm(self.ps[4][:, 0:16], self.trile_f, da[s], True, True, [K("da"), "cf"], [("ps", 4)])
            self.mm(self.ps[4][:, 16:32], self.ones_f, da[s], True, True, [K("da"), "cf"], [("ps", 4)])
            self.cp(csb[s], self.ps[4][:, 0:32], [("ps", 4)], [K("csb")])
            self.tt(tmp16[s], csb[s][:, 16:32], csb[s][:, 0:16], ALU.subtract, [K("csb"), K("tmp16")], [K("tmp16")])
            self.act(wend[s], tmp16[s], AF.Exp, [K("tmp16")], [K("wend")])
            self.act(ecs[s], csb[s][:, 0:16], AF.Exp, [K("csb")], [K("ecs")])
            self.act(cdec[s], csb[s][:, 16:32], AF.Exp, [K("csb")], [K("cdec")])
            self.tr(self.ps[4][0:16, 256:384], csb[s][:, 0:16], self.ident_f, [K("csb"), "cf"], [("ps", 4)])
            self.cp(csT[s][0:16, :], self.ps[4][0:16, 256:384], [("ps", 4)], [K("csT")])
            for k3 in range(3):
                self.cp(csT3[s][0:16, k3 * 128:(k3 + 1) * 128], csT[s][0:16, :], [K("csT")], [K("csT3")])
                if k3 < 2:
                    self.tt(csT[s][0:16, :], csT[s][0:16, :], csT3[s][0:16, k3 * 128:(k3 + 1) * 128], ALU.subtract,
                            [K("csT"), K("csT3")], [K("csT")])
            self.tt(h3(xd[s]), h3(xtk[s]), dtb[s].rearrange("p (h o) -> p h o", o=1).to_broadcast([128, 16, 64]),
                    ALU.mult, [L, K("dtb")], [K("xd")])
            self.tt(h3(xdw[s]), h3(xd[s]), wend[s].rearrange("p (h o) -> p h o", o=1).to_broadcast([128, 16, 64]),
                    ALU.mult, [K("xd"), K("wend")], [K("xdw")])

        CSB = (0, 1, 5)

        def stG(c, g):
            s = c % 2
            g3_ = g % 3
            self.mm(self.ps[4][:, 128:256], g3(bTc[s])[:, g, :], g3(cTc[s])[:, g, :], True, True, [("ld", s)], [("ps", 4)])
            self.tt(cbm[g3_], self.ps[4][:, 128:256], self.trile_f, ALU.mult, [("ps", 4), "cf"], [("cbm", g3_)])

        def stA(c, h):
            s = c % 2
            g, hh = divmod(h, 4)
            g3_ = g % 3
            bk = CSB[h % 3]
            i3 = h % 4
            if hh == 0:
                if g == 0:
                    stG(c, 1)
                    stG(c, 2)
                elif g == 1:
                    stG(c, 3)
            for k3 in range(3):
                self.mm(self.ps[bk][:, 0:128], sel[0:16, h * 128:(h + 1) * 128], csT3[s][0:16, k3 * 128:(k3 + 1) * 128],
                        k3 == 0, k3 == 2, ["sel", ("csT3", s)], [("ps", bk)])
            self.ts(dec[i3], self.ps[bk][:, 0:128], csb[s][:, h:h + 1], 0.0, ALU.subtract, ALU.min,
                    [("ps", bk), ("csb", s)], [("dec", i3)])
            self.act(dec[i3], dec[i3], AF.Exp, [("dec", i3)], [("dec", i3)])
            self.tt(Mh[i3], dec[i3], cbm[g3_], ALU.mult, [("dec", i3), ("cbm", g3_)], [("Mh", i3)], eng="pool")

        def stB(c, h):
            s = c % 2
            L = ("ld", s)
            g, hh = divmod(h, 4)
            i3 = h % 4
            hs_ = slice(h * 64, (h + 1) * 64)
            yb = 2 + h % 2
            KB = ("ps", yb)
            YK = ("ytok", s, h)
            py, po, pst = self.ps[yb][:, 0:64], self.ps[yb][:, 64:128], self.ps[yb][:, 128:192]
            self.mm(py, Mh[i3], xd[s][:, hs_], True, True, [("Mh", i3), ("xd", s)], [KB])
            if c > 0:
                self.mm(po, g3(cTc[s])[:, g, :], prevb[:, hs_], True, True, [L, ("prevb", h)], [KB])
            if c < 15:
                self.mm(pst, btk[s][:, g * 128:(g + 1) * 128], xdw[s][:, hs_], True, True, [L, ("xdw", s)], [KB])
            self.stt(ytok[s][:, hs_], xtk[s][:, hs_], bc[:, 32 + h:33 + h], py, ALU.mult, ALU.add, [L, "bc16", KB], [YK])
            if c > 0:
                self.stt(ytok[s][:, hs_], po, ecs[s][:, h:h + 1], ytok[s][:, hs_], ALU.mult, ALU.add,
                         [KB, ("ecs", s), YK], [YK])
            if c < 15:
                if c == 0:
                    self.cp(prevT[:, hs_], pst, [KB], [("prevT", h)])
                else:
                    self.stt(prevT[:, hs_], prevT[:, hs_], cdec[s][:, h:h + 1], pst, ALU.mult, ALU.add,
                             [("prevT", h), ("cdec", s), KB], [("prevT", h)])
                self.cp(prevb[:, hs_], prevT[:, hs_], [("prevT", h)], [("prevb", h)], eng="act")

        def epilogue(c):
            s = c % 2
            yk = [("ytok", s, h) for h in range(16)]
            self.tt(gy, ytok[s], szc[s], ALU.mult, yk + [("ld", s)], ["gy"], eng="pool")
            self.act(junk, gy, AF.Square, ["gy"], ["junk"])
            self.P.op("dve", lambda e: e.reduce_sum(out=ss1[:, 0:1], in_=junk, axis=AX.X), ["junk"], ["ss1"])
            self.act(ss1[:, 0:1], ss1[:, 0:1], AF.Ln, ["ss1"], ["ss1"], bias=self.epsc, scale=1.0 / 1024)
            self.act(ss1[:, 0:1], ss1[:, 0:1], AF.Exp, ["ss1"], ["ss1"], scale=-0.5)
            self.stt(yn, gy, ss1[:, 0:1], ssmw, ALU.mult, ALU.mult, ["gy", "ss1", "ssmw"], ["yn"])
            pb = 6 + s
            psT = self.ps[pb][:].bitcast(BF16)
            for j in range(8):
                self.tr(psT[:, j * 128:(j + 1) * 128], yn[:, j * 128:(j + 1) * 128], self.ident_b, ["yn", "cb"], [("ps", pb)])
            self.cp(mixst[s], psT, [("ps", pb)], [("mixst", s)], eng="act")
            self.dma(sc["mixs"][:, c * 128:(c + 1) * 128].rearrange("(j p) t -> p j t", p=128),
                     mixst[s].rearrange("p (j t) -> p j t", t=128), [("mixst", s)], [("mixs", c)])

        load(0)
        load(1)
        prologue(0)
        stG(0, 0)
        for c in range(16):
            for it in range(-3, 16):
                if it + 3 < 16:
                    stA(c, it + 3)
                if it >= 0:
                    stB(c, it)
                if it == 2 and c > 0:
                    epilogue(c - 1)
                if it == 3 and c > 0 and c + 1 < 16:
                    load(c + 1)
                if it == 8 and c + 1 < 16:
                    prologue(c + 1)
                if it == 13 and c + 1 < 16:
                    stG(c + 1, 0)
        epilogue(15)

    def wout_pre_gen(self, sc, t0, XT, xk):
        self.dma(self.rstd2[:], sc["rsb"][:, t0:t0 + TT], [], ["rstd2"])
        slots = {}

        def issue(h):
            s = self.stgslot()
            self.dma(self.stg[s][:], sc["oT_raw"][h * 128:(h + 1) * 128, t0:t0 + TT], [], [("stg", s)])
            slots[h] = s

        issue(0)
        issue(1)
        for c in range(8):
            self.dma(XT[:, 8 + c, :], sc["mixs"][c * 128:(c + 1) * 128, t0:t0 + TT], [], [(xk, 8 + c)])
        for h in range(8):
            if h + 2 < 8:
                issue(h + 2)
            s = slots.pop(h)
            self.stt(XT[:, h, :], self.stg[s][:], self.pw2[:, h:h + 1], self.rstd2[:], ALU.mult, ALU.mult,
                     [("stg", s), "pw2", "rstd2"], [(xk, h)])
            self.stgrel(s)
            yield

    def wout(self, hT_in, hT_out, sc, yraw, next_pre):
        w_out = self.w["w_out"]
        self.carve_reset()
        XTb = self.carve(16 * TT, BF16).rearrange("p (c t) -> p c t", t=TT)
        XTc = self.carve(16 * TT, BF16).rearrange("p (c t) -> p c t", t=TT)
        g0 = self.bg_add(self.wout_pre_gen(sc, 0, self.XT, "XT"))
        self.bg_finish(g0)
        g1 = self.bg_add(self.wout_pre_gen(sc, TT, XTc, "XTc"))
        for t0, XT, xk in ((0, self.XT, "XT"), (TT, XTc, "XTc")):
            if t0:
                self.bg_finish(g1)
            for dc in range(16):
                if dc % 4 == 0:
                    k = self.wslot()
                    self.dma(self.W[k][:], w_out[:, dc * 128:(dc + 4) * 128].rearrange("(c p) m -> p c m", p=128), [],
                             [("W", k)], q="pool")
                j = dc % 4
                b0 = (dc % 2) * 2
                for fc in range(16):
                    for t2 in range(2):
                        self.mm(self.ps[b0 + t2][:], self.W[k][:, fc, j * 128:(j + 1) * 128],
                                XT[:, fc, t2 * 512:(t2 + 1) * 512], fc == 0, fc == 15, [("W", k), (xk, fc)],
                                [("ps", b0 + t2)])
                self.post_evac(dc, t0, b0, yraw)
                self.bg_step(3)
            self.add_post(hT_in, hT_out, t0, 3, yraw)
            if t0 == 0:
                self.next_pre_gen = self.bg_add(next_pre(XTb))

    def memattn(self, hT_in, hT_out, yraw, pre0, next_pre):
        w = self.w
        self.carve_reset()
        XTb = self.carve(16 * TT, BF16).rearrange("p (c t) -> p c t", t=TT)
        kvw = self.carve(2048, F32)
        mnb = self.carve(2048, BF16)
        mT = self.carve(4096, BF16)
        mT3 = mT.rearrange("p (c m) -> p c m", m=256)
        kTm = self.carve(1024, BF16)
        vm = self.carve(1024, BF16)
        qTm = self.carve(4096, BF16)
        qTm3 = qTm.rearrange("p (h t) -> p h t", t=TT)
        omT = self.carve(4096, BF16)
        omT3 = omT.rearrange("p (h t) -> p h t", t=TT)
        sc2 = [self.carve(4, F32) for _ in range(2)]
        pb_ = [self.carve(256, F32) for _ in range(2)]
        pn = [self.carve(256, BF16) for _ in range(2)]
        pT = [self.carve(256, BF16) for _ in range(2)]
        junk = omT.bitcast(F32)
        msc = self.carve(4, F32)
        scale = 128.0 ** -0.5
        self.dma(kvw, w["mem_kv_norm"].to_broadcast([128, 2048]), [], ["kvw"])
        ka = self.wslot()
        self.dma(self.W[ka][:], w["w_mkv"][:, 0:512].rearrange("(c p) f -> p c f", p=128), [], [("W", ka)], q="pool")
        kb = self.wslot()
        self.dma(self.W[kb][:], w["w_mkv"][:, 512:1024].rearrange("(c p) f -> p c f", p=128), [], [("W", kb)], q="pool")
        mstg = [self.carve(1024, F32) for _ in range(2)]

        class _M:
            def __getitem__(s_, k):
                return mstg[k]
        for mb in range(2):
            sa, sb_ = 0, 1
            self.dma(mstg[0], self.mem[mb * 128:(mb + 1) * 128, 0:1024], [], [("mstg", 0)])
            self.dma(mstg[1], self.mem[mb * 128:(mb + 1) * 128, 1024:2048], [], [("mstg", 1)])
            self.tt(junk[:, 0:1024], mstg[0], mstg[0], ALU.mult, [("mstg", 0)], ["junk"])
            self.tt(junk[:, 1024:2048], mstg[1], mstg[1], ALU.mult, [("mstg", 1)], ["junk"])
            self.P.op("dve", lambda e: e.reduce_sum(out=msc[:, 0:1], in_=junk, axis=AX.X), ["junk"], ["msc"])
            self.act(msc[:, 0:1], msc[:, 0:1], AF.Ln, ["msc"], ["msc"], bias=self.epsc, scale=1.0 / D)
            self.act(msc[:, 0:1], msc[:, 0:1], AF.Exp, ["msc"], ["msc"], scale=-0.5)
            self.stt(mnb[:, 0:1024], mstg[0], msc[:, 0:1], kvw[:, 0:1024], ALU.mult, ALU.mult,
                     [("mstg", 0), "msc", "kvw"], ["mnb"])
            self.stt(mnb[:, 1024:2048], mstg[1], msc[:, 0:1], kvw[:, 1024:2048], ALU.mult, ALU.mult,
                     [("mstg", 1), "msc", "kvw"], ["mnb"])
            for q8 in range(2):
                pbk = 4 + q8
                psT = self.ps[pbk][:].bitcast(BF16)
                for j in range(8):
                    dc = q8 * 8 + j
                    self.tr(psT[:, j * 128:(j + 1) * 128], mnb[:, dc * 128:(dc + 1) * 128], self.ident_b, ["mnb", "cb"],
                            [("ps", pbk)])
                self.cp(mT3[:, q8 * 8:(q8 + 1) * 8, mb * 128:(mb + 1) * 128], psT.rearrange("p (c m) -> p c m", m=128),
                        [("ps", pbk)], ["mT"], eng=("act" if q8 else "dve"))
        for h in range(4):
            b = h % 2
            for dc in range(16):
                self.mm(self.ps[b][:, 0:256], self.W[ka][:, dc, h * 128:(h + 1) * 128], mT3[:, dc, :], dc == 0, dc == 15,
                        [("W", ka), "mT"], [("ps", b)])
            self.cp(kTm[:, h * 256:(h + 1) * 256], self.ps[b][:, 0:256], [("ps", b)], ["kTm"], eng=("act" if b else "dve"))
            self.bg_step(2)
        for mb in range(2):
            for dc in range(16):
                self.mm(self.ps[2 + mb][:], mT3[:, dc, mb * 128:(mb + 1) * 128], self.W[kb][:, dc, :], dc == 0, dc == 15,
                        [("W", kb), "mT"], [("ps", 2 + mb)])
            self.cp(vm[:, mb * 512:(mb + 1) * 512], self.ps[2 + mb][:], [("ps", 2 + mb)], ["vm"], eng=("act" if mb else "dve"))
        for t0 in (0, TT):
            if t0 == 0:
                self.bg_finish(pre0)
                XT, xk = XTb, "XTb"
                pre1 = self.bg_add(self.prepass_gen(hT_in, TT, 4, self.XT, "XT"))
            else:
                self.bg_finish(pre1)
                XT, xk = self.XT, "XT"
            kq = self.wslot()
            self.dma(self.W[kq][:], w["w_mq"].rearrange("(c p) f -> p c f", p=128), [], [("W", kq)], q="pool")
            for h in range(4):
                b0 = (h % 2) * 2
                for kc in range(16):
                    for t2 in range(2):
                        self.mm(self.ps[b0 + t2][:], self.W[kq][:, kc, h * 128:(h + 1) * 128],
                                XT[:, kc, t2 * 512:(t2 + 1) * 512], kc == 0, kc == 15, [("W", kq), (xk, kc)],
                                [("ps", b0 + t2)])
                self.cp(qTm3[:, h, 0:512], self.ps[b0][:], [("ps", b0)], [("qTm", h)], eng="act")
                self.cp(qTm3[:, h, 512:1024], self.ps[b0 + 1][:], [("ps", b0 + 1)], [("qTm", h)], eng="dve")
                self.bg_step(2)
            ko = self.wslot()
            Wo = self.W[ko][:].rearrange("p c f -> p (c f)").rearrange("p (c m) -> p c m", m=2048)
            self.dma(Wo, w["w_mo"].rearrange("(c p) m -> p c m", p=128), [], [("W", ko)], q="pool")
            def mA(i):
                tb, h = divmod(i, 4)
                i2 = i % 2
                scp = self.ps[i2][:, 0:256]
                self.mm(scp, qTm3[:, h, tb * 128:(tb + 1) * 128], kTm[:, h * 256:(h + 1) * 256], True, True,
                        [("qTm", h), "kTm"], [("ps", i2)])
                s4 = sc2[i2]
                self.P.op("dve", lambda e, s4=s4, scp=scp: e.reduce_max(out=s4[:, 0:1], in_=scp, axis=AX.X),
                          [("ps", i2)], [("s4", i2)])
                self.ts(s4[:, 1:2], s4[:, 0:1], -scale, None, ALU.mult, None, [("s4", i2)], [("s4", i2)])
                self.act(pb_[i2], scp, AF.Exp, [("ps", i2), ("s4", i2)], [("pb", i2)], bias=s4[:, 1:2], scale=scale)
                self.P.op("dve", lambda e, s4=s4, i2=i2: e.reduce_sum(out=s4[:, 2:3], in_=pb_[i2], axis=AX.X),
                          [("pb", i2)], [("s4", i2)])
                self.P.op("dve", lambda e, s4=s4: e.reciprocal(out=s4[:, 3:4], in_=s4[:, 2:3]), [("s4", i2)],
                          [("s4", i2)])
                self.ts(pn[i2], pb_[i2], s4[:, 3:4], None, ALU.mult, None, [("pb", i2), ("s4", i2)], [("pn", i2)])

            def mB(i):
                tb, h = divmod(i, 4)
                i2 = i % 2
                pTb = self.ps[4 + i2][:].bitcast(BF16)
                for mb in range(2):
                    self.tr(pTb[:, mb * 128:(mb + 1) * 128], pn[i2][:, mb * 128:(mb + 1) * 128],
                            self.ident_b, [("pn", i2), "cb"], [("ps", 4 + i2)])
                self.cp(pT[i2], pTb[:, 0:256], [("ps", 4 + i2)], [("pT", i2)], eng="act")
                op_ = self.ps[2 + i2][:, 0:128]
                for mb in range(2):
                    self.mm(op_, vm[:, mb * 512 + h * 128:mb * 512 + (h + 1) * 128], pT[i2][:, mb * 128:(mb + 1) * 128],
                            mb == 0, mb == 1, ["vm", ("pT", i2)], [("ps", 2 + i2)])
                self.cp(omT3[:, h, tb * 128:(tb + 1) * 128], op_, [("ps", 2 + i2)], [("omT", h, tb)],
                        eng=("act" if i2 else "dve"))

            for it in range(-1, 32):
                if it + 1 < 32:
                    mA(it + 1)
                if it >= 0:
                    mB(it)
                self.bg_step(1)
            if t0 and next_pre is not None:
                self.next_pre_gen = self.bg_add(next_pre())
            for dc in range(16):
                b0 = (dc % 2) * 2
                for h in range(4):
                    for t2 in range(2):
                        self.mm(self.ps[b0 + t2][:], Wo[:, h, dc * 128:(dc + 1) * 128], omT3[:, h, t2 * 512:(t2 + 1) * 512],
                                h == 0, h == 3, [("W", ko)] + [("omT", h, tb) for tb in range(t2 * 4, t2 * 4 + 4)],
                                [("ps", b0 + t2)])
                self.post_evac(dc, t0, b0, yraw)
                self.bg_step(3)
            self.add_post(hT_in, hT_out, t0, 5, yraw)

    def build(self, st):
        nc = self.nc
        self.alloc(st)
        self.epsc = EPS
        self.setup()
        hT = [self.dram("hT%d" % i, [D, S], F32) for i in range(4)]
        yraw = self.dram("yraw", [D, S], F32)
        sc = {}
        for n, shp, dt in (("qT", [1024, S], BF16), ("kT", [1024, S], BF16), ("v_tok", [S, 1024], BF16),
                           ("sz_tok", [S, 1024], F32), ("x_tok", [S, 1024], BF16), ("b_tok", [S, 512], BF16),
                           ("bT", [512, S], BF16), ("cT", [512, S], BF16), ("oT_raw", [1024, S], F32),
                           ("rsb", [128, S], F32), ("mixs", [1024, S], BF16)):
            sc[n] = self.dram(n, shp, dt)
        g0 = self.phase0_gen(hT[0])
        for _ in range(8):
            next(g0)
        if self.stages[0] != "ffn1":
            for _ in g0:
                pass
            g0 = None
        w = self.w
        stages = self.stages
        PRE_I = {"ffn1": 0, "mix": 2, "mem": 4, "ffn2": 6}
        cur = 0
        pre0 = None
        for si, stg in enumerate(stages):
            last = si == len(stages) - 1
            nxt = stages[si + 1] if not last else None
            hin, hout = hT[cur], (hT[cur + 1] if cur + 1 < 4 else None)
            if nxt is None:
                next_pre = None
            elif stg == "mix":
                next_pre = (lambda XTb, hout=hout, nxt=nxt: self.prepass_gen(hout, 0, PRE_I[nxt], XTb, "XTb"))
            else:
                next_pre = (lambda hout=hout, nxt=nxt: self.prepass_gen(hout, 0, PRE_I[nxt], self.XT, "XT"))
            if pre0 is None and stg != "mem":
                pre0 = self.bg_add(self.prepass_gen(hin, 0, PRE_I[stg], self.XT, "XT"))
                if g0 is not None:
                    self.bg_add(g0)
                    g0 = None
            self.next_pre_gen = None
            if stg in ("ffn1", "ffn2"):
                self.P.barrier()
                p = "ffn1" if stg == "ffn1" else "ffn2"
                self.ffn(hin, hout, w[p + "_wg"], w[p + "_wu"], w[p + "_wd"], PRE_I[stg], PRE_I[stg] + 1, yraw, pre0,
                         next_pre=next_pre, final=last)
            elif stg == "mix":
                self.P.barrier()
                self.inproj(hin, sc, pre0)
                self.P.barrier()
                self.sbattn(sc)
                self.P.barrier()
                self.ssd(sc)
                self.P.barrier()
                self.wout(hin, hout, sc, yraw, next_pre if next_pre is not None else (lambda XTb: iter(())))
            elif stg == "mem":
                self.P.barrier()
                if pre0 is None:
                    self.carve_reset()
                    XTb0 = self.carve(16 * TT, BF16).rearrange("p (c t) -> p c t", t=TT)
                    pre0 = self.bg_add(self.prepass_gen(hin, 0, 4, XTb0, "XTb"))
                self.memattn(hin, hout, yraw, pre0, next_pre)
            pre0 = self.next_pre_gen
            cur += 1
            if last:
                self.bg_drain()
                self.P.barrier()
                if stg not in ("ffn1", "ffn2"):
                    self.to_out(hT[cur])
        self.bg_drain()
        self.P.barrier()
        self.P.emit(st)

    def to_out(self, hT_src):
        for t0 in (0, TT):
            for dc in range(16):
                s1 = self.stgslot()
                s2 = self.stgslot()
                self.dma(self.stg[s1][:], hT_src[dc * 128:(dc + 1) * 128, t0:t0 + TT], [], [("stg", s1)])
                for q4 in range(2):
                    b = q4
                    for j in range(4):
                        jj = q4 * 4 + j
                        self.tr(self.ps[b][:, j * 128:(j + 1) * 128], self.stg[s1][:, jj * 128:(jj + 1) * 128],
                                self.ident_f, [("stg", s1), "cf"], [("ps", b)])
                    self.cp(self.stg[s2][:, q4 * 512:(q4 + 1) * 512], self.ps[b][:], [("ps", b)], [("stg", s2)],
                            eng=("dve" if q4 == 0 else "act"))
                self.dma(self.out[t0:t0 + TT, dc * 128:(dc + 1) * 128].rearrange("(j p) d -> p j d", p=128),
                         self.stg[s2][:].rearrange("p (j d) -> p j d", d=128), [("stg", s2)], [("out", dc, t0)])
                self.stgrel(s1)
                self.stgrel(s2)


_CACHE = {}


def run(inputs, stages, dbg=(), ncores=8):
    key = (tuple(stages), tuple(dbg))
    if key not in _CACHE:
        kb = K(list(stages), dbg)
        with ExitStack() as st:
            kb.build(st)
        _CACHE[key] = kb
    kb = _CACHE[key]
    consts = make_consts()
    in_maps = []
    for c in range(ncores):
        m = {"x": np.ascontiguousarray(inputs["x"][c]), "mem": np.ascontiguousarray(inputs["mem"][c]),
             "consts": consts, "selc": make_sel()}
        for n in WNAMES:
            a = np.asarray(inputs[n])
            m[n] = np.ascontiguousarray(a.reshape(WSHAPES[n]))
        in_maps.append(m)
    res = run_bass_kernel_spmd(kb.nc, in_maps, core_ids=list(range(ncores)))
    return res


ALL_STAGES = ("ffn1", "mix", "mem", "ffn2")


def kernel(**inputs):
    inputs = {k: np.asarray(v) for k, v in inputs.items()}
    res = run(inputs, ALL_STAGES)
    return np.stack([r["out"] for r in res.results], axis=0).astype(np.float32)
```

```python
import numpy as np
from contextlib import ExitStack
import concourse.bass as bass
import concourse.mybir as mybir
from concourse.bass_utils import run_bass_kernel_spmd

F32 = mybir.dt.float32
BF16 = mybir.dt.bfloat16
AF = mybir.ActivationFunctionType
ALU = mybir.AluOpType
AX = mybir.AxisListType

S = 2048
D = 2048
DFF = 5504
NFC = 43
EPS = 1e-6
TT = 1024

ENGS = ("pe", "act", "dve", "pool", "sp")
NDMA_SEM = 8


class Op:
    __slots__ = ("eng", "fn", "reads", "writes", "dma", "signal", "sem", "val", "waits", "k", "bar")

    def __init__(self, eng, fn, reads, writes, dma, bar=False):
        self.eng, self.fn, self.reads, self.writes, self.dma = eng, fn, tuple(reads), tuple(writes), dma
        self.signal = dma
        self.sem = None
        self.val = 0
        self.waits = []
        self.k = 0
        self.bar = bar


class Prog:
    def __init__(self, nc):
        self.nc = nc
        self.ops = []

    def op(self, eng, fn, reads=(), writes=()):
        o = Op(eng, fn, reads, writes, False)
        self.ops.append(o)
        return o

    def dma(self, eng, out, in_, reads=(), writes=()):
        def fn(e, out=out, in_=in_):
            return e.dma_start(out=out, in_=in_)
        o = Op(eng, fn, reads, writes, True)
        self.ops.append(o)
        return o

    def barrier(self):
        for e in ENGS:
            self.ops.append(Op(e, None, (), (), False, bar=True))

    def emit(self, stack):
        nc = self.nc
        ops = self.ops
        esem = {e: stack.enter_context(nc.semaphore("s_" + e)) for e in ENGS}
        dsem = {e: [stack.enter_context(nc.semaphore("d_%s%d" % (e, i))) for i in range(NDMA_SEM)]
                for e in ("sp", "act", "pool")}
        dcnt = {e: 0 for e in dsem}
        for o in ops:
            if o.dma:
                o.k = dcnt[o.eng] % NDMA_SEM
                dcnt[o.eng] += 1
        last_w = {}
        readers = {}
        deps_of = []
        last_real = {e: None for e in ENGS}
        last_dma = {}
        for i, o in enumerate(ops):
            if o.bar:
                need = [j for e, j in last_real.items() if j is not None and e != o.eng]
                need += list(last_dma.values())
                for j in need:
                    ops[j].signal = True
                deps_of.append(need)
                continue
            deps = {}
            for r in o.reads:
                j = last_w.get(r)
                if j is not None:
                    deps[j] = "raw"
            for w in o.writes:
                j = last_w.get(w)
                if j is not None and j not in deps:
                    deps[j] = "waw"
                for j in readers.get(w, ()):
                    if j not in deps and j != i:
                        deps[j] = "war"
            need = []
            for j, kind in deps.items():
                p = ops[j]
                if p.dma or o.dma:
                    need.append(j)
                elif p.eng == o.eng:
                    if o.eng != "pe":
                        need.append(j)
                else:
                    need.append(j)
            for j in need:
                ops[j].signal = True
            deps_of.append(need)
            for r in o.reads:
                readers.setdefault(r, []).append(i)
            for w in o.writes:
                last_w[w] = i
                readers[w] = []
            if o.dma:
                last_dma[(o.eng, o.k)] = i
            elif o.fn is not None:
                last_real[o.eng] = i
        cnt = {e: 0 for e in ENGS}
        dval = {e: [0] * NDMA_SEM for e in dsem}
        for o in ops:
            if o.dma:
                o.sem = dsem[o.eng][o.k]
                prev = dval[o.eng][o.k]
                if prev:
                    o.waits.append((o.sem, prev))
                dval[o.eng][o.k] = prev + 16
                o.val = prev + 16
            elif o.signal:
                cnt[o.eng] += 1
                o.sem = esem[o.eng]
                o.val = cnt[o.eng]
        for i, o in enumerate(ops):
            for j in deps_of[i]:
                o.waits.append((ops[j].sem, ops[j].val))
        per_eng = {e: [o for o in ops if o.eng == e] for e in ENGS}
        self.stats = {e: len(per_eng[e]) for e in ENGS}

        def run(eng_obj, name):
            known = {}
            for o in per_eng[name]:
                best = {}
                for s, v in o.waits:
                    key = id(s)
                    if v > known.get(key, 0) and v > best.get(key, (None, 0))[1]:
                        best[key] = (s, v)
                for key, (s, v) in best.items():
                    eng_obj.wait_ge(s, v)
                    known[key] = v
                if o.fn is None:
                    continue
                ins = o.fn(eng_obj)
                if o.dma:
                    ins.then_inc(o.sem, 16)
                elif o.signal:
                    ins.then_inc(o.sem, 1)

        with nc.Block() as block:
            @block.tensor
            def _(e):
                run(e, "pe")

            @block.scalar
            def _(e):
                run(e, "act")

            @block.vector
            def _(e):
                run(e, "dve")

            @block.gpsimd
            def _(e):
                run(e, "pool")

            @block.sync
            def _(e):
                run(e, "sp")


WNAMES = ["ffn1_pre", "ffn1_wg", "ffn1_wu", "ffn1_wd", "ffn1_post",
          "mix_pre", "w_in", "conv_w", "conv_b", "dt_bias", "a_log", "d_skip", "sb_norm", "ssm_norm",
          "w_out", "mix_post", "mem_pre", "mem_kv_norm", "w_mq", "w_mkv", "w_mo", "mem_post",
          "ffn2_pre", "ffn2_wg", "ffn2_wu", "ffn2_wd", "ffn2_post"]
WSHAPES = {"ffn1_pre": [1, 2048], "ffn1_wg": [2048, 5504], "ffn1_wu": [2048, 5504], "ffn1_wd": [5504, 2048],
           "ffn1_post": [1, 2048], "mix_pre": [1, 2048], "w_in": [2048, 6160], "conv_w": [4, 2048],
           "conv_b": [1, 2048], "dt_bias": [1, 16], "a_log": [1, 16], "d_skip": [1, 16], "sb_norm": [1, 1024],
           "ssm_norm": [1, 1024], "w_out": [2048, 2048], "mix_post": [1, 2048], "mem_pre": [1, 2048],
           "mem_kv_norm": [1, 2048], "w_mq": [2048, 512], "w_mkv": [2048, 1024], "w_mo": [512, 2048],
           "mem_post": [1, 2048], "ffn2_pre": [1, 2048], "ffn2_wg": [2048, 5504], "ffn2_wu": [2048, 5504],
           "ffn2_wd": [5504, 2048], "ffn2_post": [1, 2048]}
NORMV = ["ffn1_pre", "ffn1_post", "mix_pre", "mix_post", "mem_pre", "mem_post", "ffn2_pre", "ffn2_post"]
NCONST = 2688
NSEL = 2048


def make_sel():
    sel = np.zeros((16, NSEL), np.float32)
    for h in range(16):
        sel[h, h * 128:(h + 1) * 128] = 1.0
    return sel


def make_consts():
    c = np.zeros((128, NCONST), np.float32)
    i = np.arange(128)
    c[:, 0:128] = np.eye(128, dtype=np.float32)
    c[:, 128:256] = (i[:, None] >= i[None, :])
    c[:, 256:384] = (i[:, None] <= i[None, :])
    c[:, 384:512] = 1.0
    t = np.arange(512)
    for k in range(4):
        c[:, 512 + 512 * k:1024 + 512 * k] = (t[None, :] - i[:, None] - 128 * k) <= 0
    c[:, 2560:2688] = -30000.0 * np.eye(128, dtype=np.float32)
    return c


class K:
    def __init__(self, stages, dbg=()):
        self.stages = stages
        nc = self.nc = bass.Bass("TRN2", target_bir_lowering=False)
        self.P = Prog(nc)
        self.dbg = set(dbg)
        self.x = nc.dram_tensor("x", [S, D], F32, kind="ExternalInput").ap()
        self.mem = nc.dram_tensor("mem", [256, D], F32, kind="ExternalInput").ap()
        self.consts = nc.dram_tensor("consts", [128, NCONST], F32, kind="ExternalInput").ap()
        self.selc = nc.dram_tensor("selc", [16, NSEL], F32, kind="ExternalInput").ap()
        self.w = {n: nc.dram_tensor(n, WSHAPES[n], F32, kind="ExternalInput").ap() for n in WNAMES}
        self.out = nc.dram_tensor("out", [S, D], F32, kind="ExternalOutput").ap()
        self.scr = {}
        self.wk = 0
        self.stgfree = [0, 1, 2, 3]
        self.sqk = 0

    def dram(self, name, shape, dt):
        kind = "ExternalOutput" if name in self.dbg else "Internal"
        t = self.nc.dram_tensor(name, shape, dt, kind=kind).ap()
        self.scr[name] = t
        return t

    def mm(self, out, lhsT, rhs, start, stop, reads, writes):
        self.P.op("pe", lambda e: e.matmul(out, lhsT=lhsT, rhs=rhs, start=start, stop=stop), reads, writes)

    def tr(self, out, in_, ident, reads, writes):
        self.P.op("pe", lambda e: e.transpose(out, in_, ident), reads, writes)

    def act(self, out, in_, func, reads, writes, bias=None, scale=None, accum=None):
        kw = {}
        if bias is not None:
            kw["bias"] = bias
        if scale is not None:
            kw["scale"] = scale
        if accum is not None:
            kw["accum_out"] = accum
        self.P.op("act", lambda e: e.activation(out=out, in_=in_, func=func, **kw), reads, writes)

    def tt(self, out, in0, in1, op, reads, writes, eng="dve"):
        self.P.op(eng, lambda e: e.tensor_tensor(out=out, in0=in0, in1=in1, op=op), reads, writes)

    def ts(self, out, in0, s1, s2, op0, op1, reads, writes, eng="dve"):
        if op1 is None:
            self.P.op(eng, lambda e: e.tensor_scalar(out=out, in0=in0, scalar1=s1, scalar2=None, op0=op0), reads, writes)
        else:
            self.P.op(eng, lambda e: e.tensor_scalar(out=out, in0=in0, scalar1=s1, scalar2=s2, op0=op0, op1=op1),
                      reads, writes)

    def stt(self, out, in0, scalar, in1, op0, op1, reads, writes, eng="dve"):
        self.P.op(eng, lambda e: e.scalar_tensor_tensor(out=out, in0=in0, scalar=scalar, in1=in1, op0=op0, op1=op1),
                  reads, writes)

    def cp(self, out, in_, reads, writes, eng="dve"):
        if eng == "act":
            self.P.op("act", lambda e: e.copy(out=out, in_=in_), reads, writes)
        else:
            self.P.op(eng, lambda e: e.tensor_copy(out=out, in_=in_), reads, writes)

    def dma(self, out, in_, reads, writes, q="sp"):
        self.P.dma(q, out, in_, reads, writes)

    def alloc(self, st):
        nc = self.nc
        sb = lambda n, shp, dt: st.enter_context(nc.sbuf_tensor(n, shp, dt))
        self.cf = sb("cf", [128, 512], F32)
        self.cb = sb("cb", [128, NCONST], BF16)
        self.ident_f = self.cf[:, 0:128]
        self.trile_f = self.cf[:, 256:384]
        self.ones_f = self.cf[:, 384:512]
        self.ident_b = self.cb[:, 0:128]
        self.trige_b = self.cb[:, 128:256]
        self.ones_b = self.cb[:, 384:512]
        self.sbmask = [self.cb[:, 512 + 512 * k:1024 + 512 * k] for k in range(4)]
        self.negid_b = self.cb[:, 2560:2688]
        self.nw = sb("nw", [128, 128], F32)
        self.pw2 = sb("pw2", [128, 88], F32)
        self.bc16 = sb("bc16", [128, 64], F32)
        self.dtraw = sb("dtraw", [128, 16, 16], F32)
        self.XT = sb("XT", [128, 16, TT], BF16)
        self.W = [sb("W%d" % i, [128, 16, 512], BF16) for i in range(3)]
        self.stg = [sb("stg%d" % i, [128, TT], F32) for i in range(4)]
        self.sq = [sb("sq%d" % i, [128, TT], BF16) for i in range(2)]
        self.rstd = sb("rstd", [128, TT], F32)
        self.rstd2 = sb("rstd2", [128, TT], F32)
        self.bg = []
        self.last_post = None
        self.sg = [sb("sg%d" % i, [128, 512], F32) for i in range(2)]
        self.BIG = sb("BIG", [128, NFC * TT], BF16)
        self.ps = [st.enter_context(nc.psum_tensor("ps%d" % i, [128, 512], F32)) for i in range(8)]
        self.coff = 0

    def carve_reset(self):
        self.coff = 0

    def carve(self, cols, dt):
        n = cols * (2 if dt == F32 else 1)
        n = (n + 15) // 16 * 16
        a = self.BIG[:, self.coff:self.coff + n]
        self.coff += n
        assert self.coff <= NFC * TT, self.coff
        if dt == F32:
            return a.bitcast(F32)[:, 0:cols]
        return a[:, 0:cols]

    def wslot(self):
        k = self.wk % 3
        self.wk += 1
        return k

    def stgslot(self):
        assert self.stgfree, "out of staging slots"
        return self.stgfree.pop(0)

    def stgrel(self, k):
        self.stgfree.append(k)

    def sqslot(self):
        k = self.sqk % 2
        self.sqk += 1
        return k

    def setup(self):
        w = self.w
        self.dma(self.cf[:], self.consts[:, 0:512], [], ["cf"])
        self.dma(self.cb[:], self.consts[:, :], [], ["cb"], q="pool")
        st1 = self.stg[0]
        for i, n in enumerate(NORMV):
            self.dma(st1[i * 16:(i + 1) * 16, 0:128], w[n].rearrange("o (c p) -> (o c) p", p=128), [], [("stg", 0)])
        self.tr(self.ps[7][:, 0:128], st1[:, 0:128], self.ident_f, [("stg", 0), "cf"], [("ps", 7)])
        self.cp(self.nw[:], self.ps[7][:, 0:128], [("ps", 7)], ["nw"])
        for i in (1, 7):
            self.P.op("act", lambda e, i=i: e.mul(out=self.nw[:, i * 16:(i + 1) * 16], in_=self.nw[:, i * 16:(i + 1) * 16],
                                                  mul=0.5), ["nw"], ["nw"])
        st2 = self.stg[1]
        self.dma(st2[0:8, 0:128], w["sb_norm"].rearrange("o (c p) -> (o c) p", p=128), [], [("stg", 1)])
        self.dma(st2[8:24, 0:128], w["conv_b"].rearrange("o (c p) -> (o c) p", p=128), [], [("stg", 1)])
        self.dma(st2[24:88, 0:128], w["conv_w"].rearrange("k (c p) -> (k c) p", p=128), [], [("stg", 1)])
        self.tr(self.ps[6][:, 0:88], st2[0:88, 0:128], self.ident_f[0:88, 0:88], [("stg", 1), "cf"], [("ps", 6)])
        self.cp(self.pw2[:], self.ps[6][:, 0:88], [("ps", 6)], ["pw2"])
        self.dma(self.bc16[:, 0:16], w["dt_bias"].to_broadcast([128, 16]), [], ["bc16"])
        self.dma(self.bc16[:, 16:32], w["a_log"].to_broadcast([128, 16]), [], ["bc16"])
        self.dma(self.bc16[:, 32:48], w["d_skip"].to_broadcast([128, 16]), [], ["bc16"])
        self.act(self.bc16[:, 16:32], self.bc16[:, 16:32], AF.Exp, ["bc16"], ["bc16"])
        self.P.op("act", lambda e: e.mul(out=self.bc16[:, 16:32], in_=self.bc16[:, 16:32], mul=-1.0), ["bc16"], ["bc16"])

    def phase0_gen(self, hT0):
        hid = id(hT0)
        nb = 0
        for tb in range(16):
            t0 = (tb // 8) * TT
            s = self.stgslot()
            s2 = self.stgslot()
            self.dma(self.stg[s][:], self.x[tb * 128:(tb + 1) * 128, 0:1024], [], [("stg", s)])
            self.dma(self.stg[s2][:], self.x[tb * 128:(tb + 1) * 128, 1024:2048], [], [("stg", s2)])
            for half, sl in ((0, s), (1, s2)):
                so = self.stgslot()
                for q4 in range(2):
                    b = 6 + nb % 2
                    nb += 1
                    for j in range(4):
                        dcl = q4 * 4 + j
                        self.tr(self.ps[b][:, j * 128:(j + 1) * 128], self.stg[sl][:, dcl * 128:(dcl + 1) * 128],
                                self.ident_f, [("stg", sl), "cf"], [("ps", b)])
                    self.cp(self.stg[so][:, q4 * 512:(q4 + 1) * 512], self.ps[b][:], [("ps", b)], [("stg", so)],
                            eng=("dve" if q4 == 0 else "act"))
                self.dma(hT0[half * 1024:(half + 1) * 1024, tb * 128:(tb + 1) * 128].rearrange("(c p) t -> p c t", p=128),
                         self.stg[so][:].rearrange("p (c t) -> p c t", t=128), [("stg", so)],
                         [("h", hid, dc, t0) for dc in range(half * 8, half * 8 + 8)])
                self.stgrel(so)
            self.stgrel(s)
            self.stgrel(s2)
            yield

    def stats_rstd(self, n, pb, rbuf, rkey):
        for t2 in range(2):
            self.act(rbuf[:, t2 * 512:(t2 + 1) * 512], self.ps[pb + t2][:], AF.Ln, [("ps", pb + t2)], [rkey],
                     bias=self.epsc, scale=1.0 / n)
        self.act(rbuf[:], rbuf[:], AF.Exp, [rkey], [rkey], scale=-0.5)

    def sq_stats(self, src, dc, reads, pb, ndc=16):
        q = self.sqslot()
        self.act(self.sq[q][:], src, AF.Square, reads, [("sq", q)])
        for t2 in range(2):
            self.mm(self.ps[pb + t2][:], self.ones_b, self.sq[q][:, t2 * 512:(t2 + 1) * 512], dc == 0, dc == ndc - 1,
                    [("sq", q), "cb"], [("ps", pb + t2)])

    def bg_add(self, g):
        self.bg.append(g)
        return g

    def bg_step(self, n=1):
        while n > 0 and self.bg:
            try:
                next(self.bg[0])
                n -= 1
            except StopIteration:
                self.bg.pop(0)

    def bg_finish(self, g):
        while any(x is g for x in self.bg):
            self.bg_step(1)

    def bg_drain(self):
        while self.bg:
            self.bg_step(1000)

    def prepass_gen(self, hT_in, t0, wi, XT, xk):
        hid = id(hT_in)
        LA = 2
        slots = {}

        def issue(dc):
            s = self.stgslot()
            self.dma(self.stg[s][:], hT_in[dc * 128:(dc + 1) * 128, t0:t0 + TT], [("h", hid, dc, t0)], [("stg", s)])
            slots[dc] = s

        for dc in range(LA):
            issue(dc)
        for dc in range(16):
            if dc + LA < 16:
                issue(dc + LA)
            s = slots.pop(dc)
            self.sq_stats(self.stg[s][:], dc, [("stg", s)], 6)
            self.stgrel(s)
            yield
        for dc in range(LA):
            issue(dc)
        self.stats_rstd(D, 6, self.rstd2, "rstd2")
        for dc in range(16):
            if dc + LA < 16:
                issue(dc + LA)
            s = slots.pop(dc)
            self.stt(XT[:, dc, :], self.stg[s][:], self.nw[:, wi * 16 + dc:wi * 16 + dc + 1], self.rstd2[:],
                     ALU.mult, ALU.mult, [("stg", s), "nw", "rstd2"], [(xk, dc)])
            self.stgrel(s)
            yield

    def post_evac(self, dc, t0, b0, yraw):
        q = self.sqslot()
        for t2 in range(2):
            self.cp(self.sg[t2][:], self.ps[b0 + t2][:], [("ps", b0 + t2)], [("sg", t2)], eng="act")
            self.act(self.sq[q][:, t2 * 512:(t2 + 1) * 512], self.sg[t2][:], AF.Square, [("sg", t2)], [("sq", q)])
            self.mm(self.ps[4 + t2][:], self.ones_b, self.sq[q][:, t2 * 512:(t2 + 1) * 512], dc == 0, dc == 15,
                    [("sq", q), "cb"], [("ps", 4 + t2)])
            self.dma(yraw[dc * 128:(dc + 1) * 128, t0 + t2 * 512:t0 + (t2 + 1) * 512], self.sg[t2][:], [("sg", t2)],
                     [("yraw", dc, t0, t2)])

    def add_post(self, *a, **kw):
        if self.last_post is not None:
            self.bg_finish(self.last_post)
        self.stats_rstd(D, 4, self.rstd, "rstd")
        self.last_post = self.bg_add(self.post_final_gen(*a, **kw))
        return self.last_post

    def post_final_gen(self, hT_in, hT_out, t0, wi, yraw, final=False):
        hid = id(hT_in)
        hod = id(hT_out)
        slots = {}

        def issue(dc):
            s1 = self.stgslot()
            s2 = self.stgslot()
            self.dma(self.stg[s1][:], yraw[dc * 128:(dc + 1) * 128, t0:t0 + TT],
                     [("yraw", dc, t0, 0), ("yraw", dc, t0, 1)], [("stg", s1)])
            self.dma(self.stg[s2][:], hT_in[dc * 128:(dc + 1) * 128, t0:t0 + TT], [("h", hid, dc, t0)], [("stg", s2)])
            slots[dc] = (s1, s2)

        issue(0)
        for dc in range(16):
            if dc + 1 < 16:
                issue(dc + 1)
            s1, s2 = slots.pop(dc)
            self.stt(self.stg[s1][:], self.stg[s1][:], self.nw[:, wi * 16 + dc:wi * 16 + dc + 1], self.rstd[:],
                     ALU.mult, ALU.mult, [("stg", s1), "nw", "rstd"], [("stg", s1)])
            self.tt(self.stg[s1][:], self.stg[s1][:], self.stg[s2][:], ALU.add, [("stg", s1), ("stg", s2)], [("stg", s1)])
            if not final:
                self.dma(hT_out[dc * 128:(dc + 1) * 128, t0:t0 + TT], self.stg[s1][:], [("stg", s1)], [("h", hod, dc, t0)])
            else:
                for q4 in range(2):
                    b = 6 + q4
                    for j in range(4):
                        jj = q4 * 4 + j
                        self.tr(self.ps[b][:, j * 128:(j + 1) * 128], self.stg[s1][:, jj * 128:(jj + 1) * 128],
                                self.ident_f, [("stg", s1), "cf"], [("ps", b)])
                    self.cp(self.stg[s2][:, q4 * 512:(q4 + 1) * 512], self.ps[b][:], [("ps", b)], [("stg", s2)],
                            eng=("dve" if q4 == 0 else "act"))
                self.dma(self.out[t0:t0 + TT, dc * 128:(dc + 1) * 128].rearrange("(j p) d -> p j d", p=128),
                         self.stg[s2][:].rearrange("p (j d) -> p j d", d=128), [("stg", s2)], [("out", dc, t0)])
            self.stgrel(s1)
            self.stgrel(s2)
            yield

    def ffn(self, hT_in, hT_out, wg, wu, wd, pre_i, post_i, yraw, pre0, next_pre=None, final=False):
        HT = self.BIG[:].rearrange("p (c t) -> p c t", t=TT)
        pre = pre0
        for t0 in (0, TT):
            self.bg_finish(pre)
            for g in range(11):
                ncols = 512 if g < 10 else 384
                c0 = g * 512
                ka = self.wslot()
                self.dma(self.W[ka][:, :, 0:ncols], wg[:, c0:c0 + ncols].rearrange("(c p) f -> p c f", p=128), [],
                         [("W", ka)], q="pool")
                kb = self.wslot()
                self.dma(self.W[kb][:, :, 0:ncols], wu[:, c0:c0 + ncols].rearrange("(c p) f -> p c f", p=128), [],
                         [("W", kb)], q="pool")
                for j in range(ncols // 128):
                    fc = g * 4 + j
                    for kc in range(16):
                        for t2 in range(2):
                            self.mm(self.ps[t2][:], self.W[ka][:, kc, j * 128:(j + 1) * 128],
                                    self.XT[:, kc, t2 * 512:(t2 + 1) * 512], kc == 0, kc == 15,
                                    [("W", ka), ("XT", kc)], [("ps", t2)])
                    for kc in range(16):
                        for t2 in range(2):
                            self.mm(self.ps[2 + t2][:], self.W[kb][:, kc, j * 128:(j + 1) * 128],
                                    self.XT[:, kc, t2 * 512:(t2 + 1) * 512], kc == 0, kc == 15,
                                    [("W", kb), ("XT", kc)], [("ps", 2 + t2)])
                    for t2 in range(2):
                        self.act(self.sg[t2][:], self.ps[t2][:], AF.Silu, [("ps", t2)], [("sg", t2)])
                        self.tt(HT[:, fc, t2 * 512:(t2 + 1) * 512], self.sg[t2][:], self.ps[2 + t2][:], ALU.mult,
                                [("sg", t2), ("ps", 2 + t2)], [("HT", fc, t2)])
                    self.bg_step(1)
            if t0 == 0:
                pre = self.bg_add(self.prepass_gen(hT_in, TT, pre_i, self.XT, "XT"))
            elif next_pre is not None:
                self.next_pre_gen = self.bg_add(next_pre())
            for dc in range(16):
                k = self.wslot()
                Wv = self.W[k][:].rearrange("p c f -> p (c f)")[:, 0:NFC * 128].rearrange("p (c m) -> p c m", m=128)
                self.dma(Wv, wd[:, dc * 128:(dc + 1) * 128].rearrange("(c p) m -> p c m", p=128), [], [("W", k)], q="pool")
                b0 = (dc % 2) * 2
                for fc in range(NFC):
                    for t2 in range(2):
                        self.mm(self.ps[b0 + t2][:], Wv[:, fc, :], HT[:, fc, t2 * 512:(t2 + 1) * 512], fc == 0,
                                fc == NFC - 1, [("W", k), ("HT", fc, t2)], [("ps", b0 + t2)])
                self.post_evac(dc, t0, b0, yraw)
                self.bg_step(3)
            self.add_post(hT_in, hT_out, t0, post_i, yraw, final=final)

    def inproj(self, hT_in, sc, pre0):
        w_in = self.w["w_in"]
        self.carve_reset()
        XTb = self.carve(16 * TT, BF16).rearrange("p (c t) -> p c t", t=TT)
        ev = [self.carve(1024, BF16) for _ in range(3)]
        evv = [self.carve(512, BF16) for _ in range(2)]
        evz = [self.carve(512, F32) for _ in range(2)]
        cbuf = [self.carve(1028, F32) for _ in range(2)]
        acc = [self.carve(1024, F32) for _ in range(2)]
        xtk = [self.carve(1024, BF16) for _ in range(2)]
        carry = self.carve(48, F32)
        wdt = self.carve(256, BF16).rearrange("p (c f) -> p c f", f=16)
        cnt = {"ev": 0, "evv": 0, "evz": 0, "cb": 0, "xtk": 0, "bank": 0, "tb": 0}
        deferred = []

        def nxt(k, n=2):
            v = cnt[k] % n
            cnt[k] += 1
            return v

        self.dma(wdt, w_in[:, 6144:6160].rearrange("(c p) f -> p c f", p=128), [], ["wdt"], q="pool")
        for t0 in (0, TT):
            ti = t0 // TT
            if ti == 0:
                self.bg_finish(pre0)
                XT, xk = self.XT, "XT"
                pre1 = self.bg_add(self.prepass_gen(hT_in, TT, 2, XTb, "XTb"))
            else:
                self.bg_finish(pre1)
                XT, xk = XTb, "XTb"
            for g in range(4):
                ka = self.wslot()
                self.dma(self.W[ka][:], w_in[:, g * 512:(g + 1) * 512].rearrange("(c p) f -> p c f", p=128), [],
                         [("W", ka)], q="pool")
                for j in range(4):
                    ch = g * 4 + j
                    b0 = nxt("bank") * 2
                    for kc in range(16):
                        for t2 in range(2):
                            self.mm(self.ps[b0 + t2][:], self.W[ka][:, kc, j * 128:(j + 1) * 128],
                                    XT[:, kc, t2 * 512:(t2 + 1) * 512], kc == 0, kc == 15,
                                    [("W", ka), (xk, kc)], [("ps", b0 + t2)])
                    e = nxt("ev")
                    self.cp(ev[e][:, 0:512], self.ps[b0][:], [("ps", b0)], [("ev", e)], eng="act")
                    self.cp(ev[e][:, 512:1024], self.ps[b0 + 1][:], [("ps", b0 + 1)], [("ev", e)], eng="dve")
                    dst = sc["qT"] if ch < 8 else sc["kT"]
                    r0 = (ch % 8) * 128
                    self.dma(dst[r0:r0 + 128, t0:t0 + TT], ev[e], [("ev", e)], [("qk", ch, ti)])
                    self.bg_step(1)
            for which, base in (("v", 2048), ("z", 3072)):
                for ct in range(2):
                    ka = self.wslot()
                    self.dma(self.W[ka][:], w_in[:, base + ct * 512:base + (ct + 1) * 512].rearrange("(c p) f -> p c f", p=128),
                             [], [("W", ka)], q="pool")
                    for tb in range(8):
                        b = nxt("tb", 4)
                        for kc in range(16):
                            self.mm(self.ps[b][:], XT[:, kc, tb * 128:(tb + 1) * 128], self.W[ka][:, kc, :],
                                    kc == 0, kc == 15, [("W", ka), (xk, kc)], [("ps", b)])
                        r0 = t0 + tb * 128
                        self.bg_step(1)
                        if which == "v":
                            e = nxt("evv")
                            self.cp(evv[e], self.ps[b][:], [("ps", b)], [("evv", e)], eng=("dve" if tb % 2 else "act"))
                            self.dma(sc["v_tok"][r0:r0 + 128, ct * 512:(ct + 1) * 512], evv[e], [("evv", e)],
                                     [("vt", ct, tb, ti)])
                        else:
                            e = nxt("evz")
                            self.act(evz[e], self.ps[b][:], AF.Silu, [("ps", b)], [("evz", e)])
                            self.dma(sc["sz_tok"][r0:r0 + 128, ct * 512:(ct + 1) * 512], evz[e], [("evz", e)],
                                     [("zt", ct, tb, ti)])
            for g in range(4):
                ka = self.wslot()
                self.dma(self.W[ka][:], w_in[:, 4096 + g * 512:4096 + (g + 1) * 512].rearrange("(c p) f -> p c f", p=128),
                         [], [("W", ka)], q="pool")
                for j in range(4):
                    ch = g * 4 + j
                    b0 = nxt("bank") * 2
                    for kc in range(16):
                        for t2 in range(2):
                            self.mm(self.ps[b0 + t2][:], self.W[ka][:, kc, j * 128:(j + 1) * 128],
                                    XT[:, kc, t2 * 512:(t2 + 1) * 512], kc == 0, kc == 15,
                                    [("W", ka), (xk, kc)], [("ps", b0 + t2)])
                    c = nxt("cb")
                    if ti == 0:
                        self.P.op("dve", lambda e, c=c: e.memset(cbuf[c][:, 0:3], 0.0), [], [("cbuf", c)])
                    else:
                        self.cp(cbuf[c][:, 0:3], carry[:, ch * 3:ch * 3 + 3], ["carry"], [("cbuf", c)])
                    self.cp(cbuf[c][:, 3:515], self.ps[b0][:], [("ps", b0)], [("cbuf", c)], eng="act")
                    self.cp(cbuf[c][:, 515:1027], self.ps[b0 + 1][:], [("ps", b0 + 1)], [("cbuf", c)], eng="act")
                    if ti == 0:
                        self.cp(carry[:, ch * 3:ch * 3 + 3], cbuf[c][:, 1024:1027], [("cbuf", c)], ["carry"])
                    wcol = lambda k: self.pw2[:, 24 + k * 16 + ch:24 + k * 16 + ch + 1]
                    self.ts(acc[c], cbuf[c][:, 0:1024], wcol(0), None, ALU.mult, None, [("cbuf", c), "pw2"], [("acc", c)])
                    for k in range(1, 4):
                        self.stt(acc[c], cbuf[c][:, k:k + 1024], wcol(k), acc[c], ALU.mult, ALU.add,
                                 [("cbuf", c), ("acc", c), "pw2"], [("acc", c)])
                    e = nxt("ev", 3)
                    self.act(ev[e], acc[c], AF.Silu, [("acc", c), "pw2"], [("ev", e)], bias=self.pw2[:, 8 + ch:9 + ch])
                    if ch >= 12:
                        self.dma(sc["cT"][(ch - 12) * 128:(ch - 11) * 128, t0:t0 + TT], ev[e], [("ev", e)], [("ct", ch, ti)])
                    elif ch >= 8:
                        self.dma(sc["bT"][(ch - 8) * 128:(ch - 7) * 128, t0:t0 + TT], ev[e], [("ev", e)], [("bt", ch, ti)])
                    if ch < 12:
                        def tr_out(ch=ch, e=e, t0=t0, ti=ti):
                            xk_ = nxt("xtk")
                            pb = 4 + xk_
                            psT = self.ps[pb][:].bitcast(BF16)
                            for j8 in range(8):
                                self.tr(psT[:, j8 * 128:(j8 + 1) * 128], ev[e][:, j8 * 128:(j8 + 1) * 128], self.ident_b,
                                        [("ev", e), "cb"], [("ps", pb)])
                            self.cp(xtk[xk_], psT, [("ps", pb)], [("xtk", xk_)], eng="act")
                            if ch < 8:
                                dst = sc["x_tok"][t0:t0 + TT, ch * 128:(ch + 1) * 128]
                            else:
                                dst = sc["b_tok"][t0:t0 + TT, (ch - 8) * 128:(ch - 7) * 128]
                            self.dma(dst.rearrange("(j p) c -> p j c", p=128), xtk[xk_].rearrange("p (j c) -> p j c", c=128),
                                     [("xtk", xk_)], [("xo", ch, ti)])
                        deferred.append(tr_out)
                    else:
                        deferred.append(None)
                    while len(deferred) > 2:
                        f = deferred.pop(0)
                        if f is not None:
                            f()
            while deferred:
                f = deferred.pop(0)
                if f is not None:
                    f()
            for tb in range(8):
                b = nxt("tb", 4)
                for kc in range(16):
                    self.mm(self.ps[b][:, 0:16], XT[:, kc, tb * 128:(tb + 1) * 128], wdt[:, kc, :], kc == 0, kc == 15,
                            ["wdt", (xk, kc)], [("ps", b)])
                self.cp(self.dtraw[:, ti * 8 + tb, :], self.ps[b][:, 0:16], [("ps", b)], ["dtraw"])

    def sbattn(self, sc):
        self.bg_drain()
        self.carve_reset()
        kth = [self.carve(2048, BF16) for _ in range(2)]
        qth = [self.carve(2048, BF16) for _ in range(2)]
        vh = [self.carve(2048, BF16) for _ in range(2)]
        eb = [self.carve(512, F32) for _ in range(2)]
        spm = [self.carve(512, BF16) for _ in range(2)]
        t1 = [self.carve(512, F32) for _ in range(2)]
        arg = [self.carve(512, F32) for _ in range(2)]
        ab = [self.carve(512, BF16) for _ in range(2)]
        R = self.carve(512, F32)
        ob = [self.carve(512, F32) for _ in range(2)]
        sqo = [self.carve(512, BF16) for _ in range(2)]
        ssacc = self.carve(2048, F32)
        scale = 128.0 ** -0.5
        self.P.op("dve", lambda e: e.memset(ssacc, 0.0), [], ["ssacc"])
        pend = []
        pairs = []
        for h in range(8):
            for tt in range(4):
                for sb in range(4 * tt + 3, -1, -1):
                    pairs.append((h, tt, sb))
        n = len(pairs)

        def load(h):
            hs = h % 2
            self.dma(kth[hs], sc["kT"][h * 128:(h + 1) * 128, :], [], [("kq", hs)])
            self.dma(qth[hs], sc["qT"][h * 128:(h + 1) * 128, :], [], [("kq", hs)])
            self.dma(vh[hs].rearrange("p (c d) -> p c d", d=128),
                     sc["v_tok"][:, h * 128:(h + 1) * 128].rearrange("(c p) d -> p c d", p=128), [], [("kq", hs)])

        def S1(p):
            h, tt, sb = pairs[p]
            if tt == 0 and sb == 3:
                if h == 0:
                    load(0)
                if h + 1 < 8:
                    load(h + 1)
            zb = (0, 1, 7)[p % 3]
            diag = sb >= 4 * tt
            self.mm(self.ps[zb][:], kth[h % 2][:, sb * 128:(sb + 1) * 128], qth[h % 2][:, tt * 512:(tt + 1) * 512],
                    True, not diag, [("kq", h % 2)], [("ps", zb)])
            if diag:
                self.mm(self.ps[zb][:], self.negid_b, self.sbmask[sb - 4 * tt], False, True, ["cb"], [("ps", zb)])

        def S2(p):
            h, tt, sb = pairs[p]
            i = p % 2
            zb = (0, 1, 7)[p % 3]
            self.act(eb[i], self.ps[zb][:], AF.Exp, [("ps", zb)], [("eb", i)], scale=scale)
            self.act(spm[i], eb[i], AF.Ln, [("eb", i)], [("spm", i)], bias=1.0)

        def S3(p):
            i = p % 2
            self.mm(self.ps[2 + i][:], self.trige_b, spm[i], True, True, [("spm", i), "cb"], [("ps", 2 + i)])
            self.mm(self.ps[4 + i][:], self.ones_b, spm[i], True, True, [("spm", i), "cb"], [("ps", 4 + i)])

        def S4(p):
            h, tt, sb = pairs[p]
            i = p % 2
            first = sb == 4 * tt + 3
            if first:
                self.cp(t1[i], self.ps[2 + i][:], [("ps", 2 + i)], [("t1", i)])
            else:
                self.tt(t1[i], self.ps[2 + i][:], R, ALU.add, [("ps", 2 + i), "R"], [("t1", i)])
            zb = (0, 1, 7)[p % 3]
            self.stt(arg[i], self.ps[zb][:], scale, t1[i], ALU.mult, ALU.subtract, [("ps", zb), ("t1", i)], [("arg", i)])
            if first:
                self.cp(R, self.ps[4 + i][:], [("ps", 4 + i)], ["R"])
            elif sb > 0:
                self.tt(R, self.ps[4 + i][:], R, ALU.add, [("ps", 4 + i), "R"], ["R"])

        def S5(p):
            h, tt, sb = pairs[p]
            i = p % 2
            self.act(ab[i], arg[i], AF.Exp, [("arg", i)], [("ab", i)])

        def S6(p):
            h, tt, sb = pairs[p]
            i = p % 2
            til = (h * 4 + tt) % 2
            self.mm(self.ps[6][:], vh[h % 2][:, sb * 128:(sb + 1) * 128], ab[i], sb == 4 * tt + 3, sb == 0,
                    [("kq", h % 2), ("ab", i)], [("ps", 6)])
            if sb == 0:
                self.cp(ob[til], self.ps[6][:], [("ps", 6)], [("ob", til)], eng="act")
                self.dma(sc["oT_raw"][h * 128:(h + 1) * 128, tt * 512:(tt + 1) * 512], ob[til], [("ob", til)],
                         [("oraw", h, tt)])
                self.act(sqo[til], ob[til], AF.Square, [("ob", til)], [("sqo", til)])

                def stat(til=til, tt=tt):
                    self.mm(self.ps[2][:], self.ones_b, sqo[til], True, True, [("sqo", til), "cb"], [("ps", 2)])
                    self.tt(ssacc[:, tt * 512:(tt + 1) * 512], self.ps[2][:], ssacc[:, tt * 512:(tt + 1) * 512], ALU.add,
                            [("ps", 2), "ssacc"], ["ssacc"])
                pend.append([2, stat])

        for it in range(-3, n):
            if 0 <= it + 3 < n:
                S1(it + 3)
                S2(it + 3)
            if 0 <= it + 2 < n:
                S3(it + 2)
                S4(it + 2)
            if 0 <= it + 1 < n:
                S5(it + 1)
            if 0 <= it < n:
                S6(it)
            for pe_ in list(pend):
                pe_[0] -= 1
                if pe_[0] < 0:
                    pend.remove(pe_)
                    pe_[1]()
        for pe_ in pend:
            pe_[1]()
        self.act(ssacc, ssacc, AF.Ln, ["ssacc"], ["ssacc"], bias=self.epsc, scale=1.0 / 1024)
        self.act(ssacc, ssacc, AF.Exp, ["ssacc"], ["ssacc"], scale=-0.5)
        self.dma(sc["rsb"][:, :], ssacc, ["ssacc"], ["rsb"])

    def ssd(self, sc):
        self.carve_reset()
        c2 = lambda n, dt: [self.carve(n, dt) for _ in range(2)]
        ssmw = self.carve(1024, F32)
        xtk, btk, bTc, cTc = c2(1024, BF16), c2(512, BF16), c2(512, BF16), c2(512, BF16)
        szc = c2(1024, F32)
        tmp16, dtb, da, csb, wend, ecs, cdec = (c2(16, F32), c2(16, F32), c2(16, F32), c2(32, F32), c2(16, F32),
                                                c2(16, F32), c2(16, F32))
        ss1 = self.carve(4, F32)
        xd, xdw = c2(1024, BF16), c2(1024, BF16)
        cbm = [self.carve(128, F32) for _ in range(3)]
        sel = self.carve(NSEL, BF16)
        csT = c2(128, F32)
        csR = c2(128, F32)
        csT3 = c2(384, BF16)
        self.dma(sel[0:16, :], self.selc, [], ["sel"], q="pool")
        dec = [self.carve(128, F32) for _ in range(4)]
        Mh = [self.carve(128, BF16) for _ in range(4)]
        ytok = c2(1024, F32)
        gy = self.carve(1024, F32)
        junk = self.carve(1024, F32)
        yn = self.carve(1024, BF16)
        prevT = self.carve(1024, F32)
        prevb = self.carve(1024, BF16)
        mixst = c2(1024, BF16)
        bc = self.bc16
        self.dma(ssmw, self.w["ssm_norm"].to_broadcast([128, 1024]), [], ["ssmw"])
        g3 = lambda ap: ap.rearrange("p (g t) -> p g t", t=128)
        h3 = lambda ap: ap.rearrange("p (h d) -> p h d", d=64)

        def load(c):
            s = c % 2
            r = slice(c * 128, (c + 1) * 128)
            self.dma(xtk[s], sc["x_tok"][r, :], [], [("ld", s)])
            self.dma(btk[s], sc["b_tok"][r, :], [], [("ld", s)])
            self.dma(g3(bTc[s]), sc["bT"][:, r].rearrange("(g p) t -> p g t", p=128), [], [("ld", s)])
            self.dma(g3(cTc[s]), sc["cT"][:, r].rearrange("(g p) t -> p g t", p=128), [], [("ld", s)])
            self.dma(szc[s], sc["sz_tok"][r, :], [], [("ld", s)])

        def prologue1(c):
            s = c % 2
            K = lambda n: (n, s)
            self.tt(tmp16[s], self.dtraw[:, c, :], bc[:, 0:16], ALU.add, ["dtraw", "bc16"], [K("tmp16")])
            self.act(tmp16[s], tmp16[s], AF.Exp, [K("tmp16")], [K("tmp16")])
            self.act(dtb[s], tmp16[s], AF.Ln, [K("tmp16")], [K("dtb")], bias=1.0)
            self.tt(da[s], dtb[s], bc[:, 16:32], ALU.mult, [K("dtb"), "bc16"], [K("da")])

        def prologue2(c):
            s = c % 2
            K = lambda n: (n, s)
            self.mm(self.ps[4][:, 0:16], self.trile_f, da[s], True, True, [K("da"), "cf"], [("ps", 4)])
            self.mm(self.ps[4][:, 16:32], self.ones_f, da[s], True, True, [K("da"), "cf"], [("ps", 4)])
            self.cp(csb[s], self.ps[4][:, 0:32], [("ps", 4)], [K("csb")])
            self.tt(tmp16[s], csb[s][:, 16:32], csb[s][:, 0:16], ALU.subtract, [K("csb"), K("tmp16")], [K("tmp16")])
            self.act(wend[s], tmp16[s], AF.Exp, [K("tmp16")], [K("wend")])
            self.act(ecs[s], csb[s][:, 0:16], AF.Exp, [K("csb")], [K("ecs")])
            self.act(cdec[s], csb[s][:, 16:32], AF.Exp, [K("csb")], [K("cdec")])

        def prologue3(c):
            s = c % 2
            L = ("ld", s)
            K = lambda n: (n, s)
            self.tr(self.ps[4][0:16, 256:384], csb[s][:, 0:16], self.ident_f, [K("csb"), "cf"], [("ps", 4)])
            self.cp(csT[s][0:16, :], self.ps[4][0:16, 256:384], [("ps", 4)], [K("csT")])
            for k3 in range(3):
                self.cp(csT3[s][0:16, k3 * 128:(k3 + 1) * 128], csT[s][0:16, :], [K("csT")], [K("csT3")])
                if k3 < 2:
                    self.tt(csT[s][0:16, :], csT[s][0:16, :], csT3[s][0:16, k3 * 128:(k3 + 1) * 128], ALU.subtract,
                            [K("csT"), K("csT3")], [K("csT")])
            self.tt(h3(xd[s]), h3(xtk[s]), dtb[s].rearrange("p (h o) -> p h o", o=1).to_broadcast([128, 16, 64]),
                    ALU.mult, [L, K("dtb")], [K("xd")])
            self.tt(h3(xdw[s]), h3(xd[s]), wend[s].rearrange("p (h o) -> p h o", o=1).to_broadcast([128, 16, 64]),
                    ALU.mult, [K("xd"), K("wend")], [K("xdw")])

        def prologue(c):
            prologue1(c)
            prologue2(c)
            prologue3(c)

        CSB = (0, 1, 5)

        def stG(c, g):
            s = c % 2
            g3_ = g % 3
            self.mm(self.ps[4][:, 128:256], g3(bTc[s])[:, g, :], g3(cTc[s])[:, g, :], True, True, [("ld", s)], [("ps", 4)])
            self.tt(cbm[g3_], self.ps[4][:, 128:256], self.trile_f, ALU.mult, [("ps", 4), "cf"], [("cbm", g3_)])

        def stA(c, h):
            s = c % 2
            g, hh = divmod(h, 4)
            g3_ = g % 3
            bk = CSB[h % 3]
            i3 = h % 4
            if hh == 0:
                if g == 0:
                    stG(c, 1)
                    stG(c, 2)
                elif g == 1:
                    stG(c, 3)
            for k3 in range(3):
                self.mm(self.ps[bk][:, 0:128], sel[0:16, h * 128:(h + 1) * 128], csT3[s][0:16, k3 * 128:(k3 + 1) * 128],
                        k3 == 0, k3 == 2, ["sel", ("csT3", s)], [("ps", bk)])
            self.ts(dec[i3], self.ps[bk][:, 0:128], csb[s][:, h:h + 1], 0.0, ALU.subtract, ALU.min,
                    [("ps", bk), ("csb", s)], [("dec", i3)])
            self.act(dec[i3], dec[i3], AF.Exp, [("dec", i3)], [("dec", i3)])
            self.tt(Mh[i3], dec[i3], cbm[g3_], ALU.mult, [("dec", i3), ("cbm", g3_)], [("Mh", i3)], eng="pool")

        def stB(c, h):
            s = c % 2
            L = ("ld", s)
            g, hh = divmod(h, 4)
            i3 = h % 4
            hs_ = slice(h * 64, (h + 1) * 64)
            yb = 2 + h % 2
            KB = ("ps", yb)
            YK = ("ytok", s, h)
            py, po, pst = self.ps[yb][:, 0:64], self.ps[yb][:, 64:128], self.ps[yb][:, 128:192]
            self.mm(py, Mh[i3], xd[s][:, hs_], True, True, [("Mh", i3), ("xd", s)], [KB])
            if c > 0:
                self.mm(po, g3(cTc[s])[:, g, :], prevb[:, hs_], True, True, [L, ("prevb", h)], [KB])
            if c < 15:
                self.mm(pst, btk[s][:, g * 128:(g + 1) * 128], xdw[s][:, hs_], True, True, [L, ("xdw", s)], [KB])
            self.stt(ytok[s][:, hs_], xtk[s][:, hs_], bc[:, 32 + h:33 + h], py, ALU.mult, ALU.add, [L, "bc16", KB], [YK])
            if c > 0:
                self.stt(ytok[s][:, hs_], po, ecs[s][:, h:h + 1], ytok[s][:, hs_], ALU.mult, ALU.add,
                         [KB, ("ecs", s), YK], [YK])
            if c < 15:
                if c == 0:
                    self.cp(prevT[:, hs_], pst, [KB], [("prevT", h)])
                else:
                    self.stt(prevT[:, hs_], prevT[:, hs_], cdec[s][:, h:h + 1], pst, ALU.mult, ALU.add,
                             [("prevT", h), ("cdec", s), KB], [("prevT", h)])
                self.cp(prevb[:, hs_], prevT[:, hs_], [("prevT", h)], [("prevb", h)], eng="act")

        def epilogue1(c):
            s = c % 2
            yk = [("ytok", s, h) for h in range(16)]
            self.tt(gy, ytok[s], szc[s], ALU.mult, yk + [("ld", s)], ["gy"], eng="pool")
            self.act(junk, gy, AF.Square, ["gy"], ["junk"])
            self.P.op("dve", lambda e: e.reduce_sum(out=ss1[:, 0:1], in_=junk, axis=AX.X), ["junk"], ["ss1"])
            self.act(ss1[:, 0:1], ss1[:, 0:1], AF.Ln, ["ss1"], ["ss1"], bias=self.epsc, scale=1.0 / 1024)
            self.act(ss1[:, 0:1], ss1[:, 0:1], AF.Exp, ["ss1"], ["ss1"], scale=-0.5)
            self.stt(yn, gy, ss1[:, 0:1], ssmw, ALU.mult, ALU.mult, ["gy", "ss1", "ssmw"], ["yn"])

        def epilogue2(c):
            s = c % 2
            pb = 6 + s
            psT = self.ps[pb][:].bitcast(BF16)
            for j in range(8):
                self.tr(psT[:, j * 128:(j + 1) * 128], yn[:, j * 128:(j + 1) * 128], self.ident_b, ["yn", "cb"], [("ps", pb)])
            self.cp(mixst[s], psT, [("ps", pb)], [("mixst", s)], eng="act")
            self.dma(sc["mixs"][:, c * 128:(c + 1) * 128].rearrange("(j p) t -> p j t", p=128),
                     mixst[s].rearrange("p (j t) -> p j t", t=128), [("mixst", s)], [("mixs", c)])

        load(0)
        load(1)
        prologue(0)
        stG(0, 0)
        for c in range(16):
            for it in range(-3, 16):
                if it + 3 < 16:
                    stA(c, it + 3)
                if it >= 0:
                    stB(c, it)
                if it == 2 and c > 0:
                    epilogue1(c - 1)
                if it == 3 and c > 0 and c + 1 < 16:
                    load(c + 1)
                if it == 4 and c + 1 < 16:
                    prologue1(c + 1)
                if it == 7 and c + 1 < 16:
                    prologue2(c + 1)
                if it == 9 and c > 0:
                    epilogue2(c - 1)
                if it == 10 and c + 1 < 16:
                    prologue3(c + 1)
                if it == 13 and c + 1 < 16:
                    stG(c + 1, 0)
        epilogue1(15)
        epilogue2(15)

    def wout_pre_gen(self, sc, t0, XT, xk):
        self.dma(self.rstd2[:], sc["rsb"][:, t0:t0 + TT], [], ["rstd2"])
        slots = {}

        def issue(h):
            s = self.stgslot()
            self.dma(self.stg[s][:], sc["oT_raw"][h * 128:(h + 1) * 128, t0:t0 + TT], [], [("stg", s)])
            slots[h] = s

        issue(0)
        issue(1)
        for c in range(8):
            self.dma(XT[:, 8 + c, :], sc["mixs"][c * 128:(c + 1) * 128, t0:t0 + TT], [], [(xk, 8 + c)])
        for h in range(8):
            if h + 2 < 8:
                issue(h + 2)
            s = slots.pop(h)
            self.stt(XT[:, h, :], self.stg[s][:], self.pw2[:, h:h + 1], self.rstd2[:], ALU.mult, ALU.mult,
                     [("stg", s), "pw2", "rstd2"], [(xk, h)])
            self.stgrel(s)
            yield

    def wout(self, hT_in, hT_out, sc, yraw, next_pre):
        w_out = self.w["w_out"]
        self.carve_reset()
        XTb = self.carve(16 * TT, BF16).rearrange("p (c t) -> p c t", t=TT)
        XTc = self.carve(16 * TT, BF16).rearrange("p (c t) -> p c t", t=TT)
        g0 = self.bg_add(self.wout_pre_gen(sc, 0, self.XT, "XT"))
        self.bg_finish(g0)
        g1 = self.bg_add(self.wout_pre_gen(sc, TT, XTc, "XTc"))
        for t0, XT, xk in ((0, self.XT, "XT"), (TT, XTc, "XTc")):
            if t0:
                self.bg_finish(g1)
            for dc in range(16):
                if dc % 4 == 0:
                    k = self.wslot()
                    self.dma(self.W[k][:], w_out[:, dc * 128:(dc + 4) * 128].rearrange("(c p) m -> p c m", p=128), [],
                             [("W", k)], q="pool")
                j = dc % 4
                b0 = (dc % 2) * 2
                for fc in range(16):
                    for t2 in range(2):
                        self.mm(self.ps[b0 + t2][:], self.W[k][:, fc, j * 128:(j + 1) * 128],
                                XT[:, fc, t2 * 512:(t2 + 1) * 512], fc == 0, fc == 15, [("W", k), (xk, fc)],
                                [("ps", b0 + t2)])
                self.post_evac(dc, t0, b0, yraw)
                self.bg_step(3)
            self.add_post(hT_in, hT_out, t0, 3, yraw)
            if t0 == 0:
                self.next_pre_gen = self.bg_add(next_pre(XTb))

    def memattn(self, hT_in, hT_out, yraw, pre0, next_pre):
        w = self.w
        self.carve_reset()
        XTb = self.carve(16 * TT, BF16).rearrange("p (c t) -> p c t", t=TT)
        kvw = self.carve(2048, F32)
        mnb = self.carve(2048, BF16)
        mT = self.carve(4096, BF16)
        mT3 = mT.rearrange("p (c m) -> p c m", m=256)
        kTm = self.carve(1024, BF16)
        vm = self.carve(1024, BF16)
        qTm = self.carve(4096, BF16)
        qTm3 = qTm.rearrange("p (h t) -> p h t", t=TT)
        omT = self.carve(4096, BF16)
        omT3 = omT.rearrange("p (h t) -> p h t", t=TT)
        sc2 = [self.carve(4, F32) for _ in range(2)]
        pb_ = [self.carve(256, F32) for _ in range(2)]
        pn = [self.carve(256, BF16) for _ in range(2)]
        pT = [self.carve(256, BF16) for _ in range(2)]
        junk = omT.bitcast(F32)
        msc = self.carve(4, F32)
        scale = 128.0 ** -0.5
        self.dma(kvw, w["mem_kv_norm"].to_broadcast([128, 2048]), [], ["kvw"])
        ka = self.wslot()
        self.dma(self.W[ka][:], w["w_mkv"][:, 0:512].rearrange("(c p) f -> p c f", p=128), [], [("W", ka)], q="pool")
        kb = self.wslot()
        self.dma(self.W[kb][:], w["w_mkv"][:, 512:1024].rearrange("(c p) f -> p c f", p=128), [], [("W", kb)], q="pool")
        mstg = [self.carve(1024, F32) for _ in range(2)]

        class _M:
            def __getitem__(s_, k):
                return mstg[k]
        for mb in range(2):
            sa, sb_ = 0, 1
            self.dma(mstg[0], self.mem[mb * 128:(mb + 1) * 128, 0:1024], [], [("mstg", 0)])
            self.dma(mstg[1], self.mem[mb * 128:(mb + 1) * 128, 1024:2048], [], [("mstg", 1)])
            self.tt(junk[:, 0:1024], mstg[0], mstg[0], ALU.mult, [("mstg", 0)], ["junk"])
            self.tt(junk[:, 1024:2048], mstg[1], mstg[1], ALU.mult, [("mstg", 1)], ["junk"])
            self.P.op("dve", lambda e: e.reduce_sum(out=msc[:, 0:1], in_=junk, axis=AX.X), ["junk"], ["msc"])
            self.act(msc[:, 0:1], msc[:, 0:1], AF.Ln, ["msc"], ["msc"], bias=self.epsc, scale=1.0 / D)
            self.act(msc[:, 0:1], msc[:, 0:1], AF.Exp, ["msc"], ["msc"], scale=-0.5)
            self.stt(mnb[:, 0:1024], mstg[0], msc[:, 0:1], kvw[:, 0:1024], ALU.mult, ALU.mult,
                     [("mstg", 0), "msc", "kvw"], ["mnb"])
            self.stt(mnb[:, 1024:2048], mstg[1], msc[:, 0:1], kvw[:, 1024:2048], ALU.mult, ALU.mult,
                     [("mstg", 1), "msc", "kvw"], ["mnb"])
            for q8 in range(2):
                pbk = 4 + q8
                psT = self.ps[pbk][:].bitcast(BF16)
                for j in range(8):
                    dc = q8 * 8 + j
                    self.tr(psT[:, j * 128:(j + 1) * 128], mnb[:, dc * 128:(dc + 1) * 128], self.ident_b, ["mnb", "cb"],
                            [("ps", pbk)])
                self.cp(mT3[:, q8 * 8:(q8 + 1) * 8, mb * 128:(mb + 1) * 128], psT.rearrange("p (c m) -> p c m", m=128),
                        [("ps", pbk)], ["mT"], eng=("act" if q8 else "dve"))
        for h in range(4):
            b = h % 2
            for dc in range(16):
                self.mm(self.ps[b][:, 0:256], self.W[ka][:, dc, h * 128:(h + 1) * 128], mT3[:, dc, :], dc == 0, dc == 15,
                        [("W", ka), "mT"], [("ps", b)])
            self.cp(kTm[:, h * 256:(h + 1) * 256], self.ps[b][:, 0:256], [("ps", b)], ["kTm"], eng=("act" if b else "dve"))
            self.bg_step(2)
        for mb in range(2):
            for dc in range(16):
                self.mm(self.ps[2 + mb][:], mT3[:, dc, mb * 128:(mb + 1) * 128], self.W[kb][:, dc, :], dc == 0, dc == 15,
                        [("W", kb), "mT"], [("ps", 2 + mb)])
            self.cp(vm[:, mb * 512:(mb + 1) * 512], self.ps[2 + mb][:], [("ps", 2 + mb)], ["vm"], eng=("act" if mb else "dve"))
        for t0 in (0, TT):
            if t0 == 0:
                self.bg_finish(pre0)
                XT, xk = XTb, "XTb"
                pre1 = self.bg_add(self.prepass_gen(hT_in, TT, 4, self.XT, "XT"))
            else:
                self.bg_finish(pre1)
                XT, xk = self.XT, "XT"
            kq = self.wslot()
            self.dma(self.W[kq][:], w["w_mq"].rearrange("(c p) f -> p c f", p=128), [], [("W", kq)], q="pool")
            for h in range(4):
                b0 = (h % 2) * 2
                for kc in range(16):
                    for t2 in range(2):
                        self.mm(self.ps[b0 + t2][:], self.W[kq][:, kc, h * 128:(h + 1) * 128],
                                XT[:, kc, t2 * 512:(t2 + 1) * 512], kc == 0, kc == 15, [("W", kq), (xk, kc)],
                                [("ps", b0 + t2)])
                self.cp(qTm3[:, h, 0:512], self.ps[b0][:], [("ps", b0)], [("qTm", h)], eng="act")
                self.cp(qTm3[:, h, 512:1024], self.ps[b0 + 1][:], [("ps", b0 + 1)], [("qTm", h)], eng="dve")
                self.bg_step(2)
            ko = self.wslot()
            Wo = self.W[ko][:].rearrange("p c f -> p (c f)").rearrange("p (c m) -> p c m", m=2048)
            self.dma(Wo, w["w_mo"].rearrange("(c p) m -> p c m", p=128), [], [("W", ko)], q="pool")
            def mA(i):
                tb, h = divmod(i, 4)
                i2 = i % 2
                scp = self.ps[i2][:, 0:256]
                self.mm(scp, qTm3[:, h, tb * 128:(tb + 1) * 128], kTm[:, h * 256:(h + 1) * 256], True, True,
                        [("qTm", h), "kTm"], [("ps", i2)])
                s4 = sc2[i2]
                self.P.op("dve", lambda e, s4=s4, scp=scp: e.reduce_max(out=s4[:, 0:1], in_=scp, axis=AX.X),
                          [("ps", i2)], [("s4", i2)])
                self.ts(s4[:, 1:2], s4[:, 0:1], -scale, None, ALU.mult, None, [("s4", i2)], [("s4", i2)])
                self.act(pb_[i2], scp, AF.Exp, [("ps", i2), ("s4", i2)], [("pb", i2)], bias=s4[:, 1:2], scale=scale)
                self.P.op("dve", lambda e, s4=s4, i2=i2: e.reduce_sum(out=s4[:, 2:3], in_=pb_[i2], axis=AX.X),
                          [("pb", i2)], [("s4", i2)])
                self.P.op("dve", lambda e, s4=s4: e.reciprocal(out=s4[:, 3:4], in_=s4[:, 2:3]), [("s4", i2)],
                          [("s4", i2)])
                self.ts(pn[i2], pb_[i2], s4[:, 3:4], None, ALU.mult, None, [("pb", i2), ("s4", i2)], [("pn", i2)])

            def mB(i):
                tb, h = divmod(i, 4)
                i2 = i % 2
                pTb = self.ps[4 + i2][:].bitcast(BF16)
                for mb in range(2):
                    self.tr(pTb[:, mb * 128:(mb + 1) * 128], pn[i2][:, mb * 128:(mb + 1) * 128],
                            self.ident_b, [("pn", i2), "cb"], [("ps", 4 + i2)])
                self.cp(pT[i2], pTb[:, 0:256], [("ps", 4 + i2)], [("pT", i2)], eng="act")
                op_ = self.ps[2 + i2][:, 0:128]
                for mb in range(2):
                    self.mm(op_, vm[:, mb * 512 + h * 128:mb * 512 + (h + 1) * 128], pT[i2][:, mb * 128:(mb + 1) * 128],
                            mb == 0, mb == 1, ["vm", ("pT", i2)], [("ps", 2 + i2)])
                self.cp(omT3[:, h, tb * 128:(tb + 1) * 128], op_, [("ps", 2 + i2)], [("omT", h, tb)],
                        eng=("act" if i2 else "dve"))

            for it in range(-1, 32):
                if it + 1 < 32:
                    mA(it + 1)
                if it >= 0:
                    mB(it)
                self.bg_step(1)
            if t0 and next_pre is not None:
                self.next_pre_gen = self.bg_add(next_pre())
            for dc in range(16):
                b0 = (dc % 2) * 2
                for h in range(4):
                    for t2 in range(2):
                        self.mm(self.ps[b0 + t2][:], Wo[:, h, dc * 128:(dc + 1) * 128], omT3[:, h, t2 * 512:(t2 + 1) * 512],
                                h == 0, h == 3, [("W", ko)] + [("omT", h, tb) for tb in range(t2 * 4, t2 * 4 + 4)],
                                [("ps", b0 + t2)])
                self.post_evac(dc, t0, b0, yraw)
                self.bg_step(3)
            self.add_post(hT_in, hT_out, t0, 5, yraw)

    def build(self, st):
        nc = self.nc
        self.alloc(st)
        self.epsc = EPS
        self.setup()
        hT = [self.dram("hT%d" % i, [D, S], F32) for i in range(4)]
        yraw = self.dram("yraw", [D, S], F32)
        sc = {}
        for n, shp, dt in (("qT", [1024, S], BF16), ("kT", [1024, S], BF16), ("v_tok", [S, 1024], BF16),
                           ("sz_tok", [S, 1024], F32), ("x_tok", [S, 1024], BF16), ("b_tok", [S, 512], BF16),
                           ("bT", [512, S], BF16), ("cT", [512, S], BF16), ("oT_raw", [1024, S], F32),
                           ("rsb", [128, S], F32), ("mixs", [1024, S], BF16)):
            sc[n] = self.dram(n, shp, dt)
        g0 = self.phase0_gen(hT[0])
        for _ in range(8):
            next(g0)
        if self.stages[0] != "ffn1":
            for _ in g0:
                pass
            g0 = None
        w = self.w
        stages = self.stages
        PRE_I = {"ffn1": 0, "mix": 2, "mem": 4, "ffn2": 6}
        cur = 0
        pre0 = None
        for si, stg in enumerate(stages):
            last = si == len(stages) - 1
            nxt = stages[si + 1] if not last else None
            hin, hout = hT[cur], (hT[cur + 1] if cur + 1 < 4 else None)
            if nxt is None:
                next_pre = None
            elif stg == "mix":
                next_pre = (lambda XTb, hout=hout, nxt=nxt: self.prepass_gen(hout, 0, PRE_I[nxt], XTb, "XTb"))
            else:
                next_pre = (lambda hout=hout, nxt=nxt: self.prepass_gen(hout, 0, PRE_I[nxt], self.XT, "XT"))
            if pre0 is None and stg != "mem":
                pre0 = self.bg_add(self.prepass_gen(hin, 0, PRE_I[stg], self.XT, "XT"))
                if g0 is not None:
                    self.bg_add(g0)
                    g0 = None
            self.next_pre_gen = None
            if stg in ("ffn1", "ffn2"):
                self.P.barrier()
                p = "ffn1" if stg == "ffn1" else "ffn2"
                self.ffn(hin, hout, w[p + "_wg"], w[p + "_wu"], w[p + "_wd"], PRE_I[stg], PRE_I[stg] + 1, yraw, pre0,
                         next_pre=next_pre, final=last)
            elif stg == "mix":
                self.P.barrier()
                self.inproj(hin, sc, pre0)
                self.P.barrier()
                self.sbattn(sc)
                self.P.barrier()
                self.ssd(sc)
                self.P.barrier()
                self.wout(hin, hout, sc, yraw, next_pre if next_pre is not None else (lambda XTb: iter(())))
            elif stg == "mem":
                self.P.barrier()
                if pre0 is None:
                    self.carve_reset()
                    XTb0 = self.carve(16 * TT, BF16).rearrange("p (c t) -> p c t", t=TT)
                    pre0 = self.bg_add(self.prepass_gen(hin, 0, 4, XTb0, "XTb"))
                self.memattn(hin, hout, yraw, pre0, next_pre)
            pre0 = self.next_pre_gen
            cur += 1
            if last:
                self.bg_drain()
                self.P.barrier()
                if stg not in ("ffn1", "ffn2"):
                    self.to_out(hT[cur])
        self.bg_drain()
        self.P.barrier()
        self.P.emit(st)

    def to_out(self, hT_src):
        for t0 in (0, TT):
            for dc in range(16):
                s1 = self.stgslot()
                s2 = self.stgslot()
                self.dma(self.stg[s1][:], hT_src[dc * 128:(dc + 1) * 128, t0:t0 + TT], [], [("stg", s1)])
                for q4 in range(2):
                    b = q4
                    for j in range(4):
                        jj = q4 * 4 + j
                        self.tr(self.ps[b][:, j * 128:(j + 1) * 128], self.stg[s1][:, jj * 128:(jj + 1) * 128],
                                self.ident_f, [("stg", s1), "cf"], [("ps", b)])
                    self.cp(self.stg[s2][:, q4 * 512:(q4 + 1) * 512], self.ps[b][:], [("ps", b)], [("stg", s2)],
                            eng=("dve" if q4 == 0 else "act"))
                self.dma(self.out[t0:t0 + TT, dc * 128:(dc + 1) * 128].rearrange("(j p) d -> p j d", p=128),
                         self.stg[s2][:].rearrange("p (j d) -> p j d", d=128), [("stg", s2)], [("out", dc, t0)])
                self.stgrel(s1)
                self.stgrel(s2)


_CACHE = {}


def run(inputs, stages, dbg=(), ncores=8):
    key = (tuple(stages), tuple(dbg))
    if key not in _CACHE:
        kb = K(list(stages), dbg)
        with ExitStack() as st:
            kb.build(st)
        _CACHE[key] = kb
    kb = _CACHE[key]
    consts = make_consts()
    in_maps = []
    for c in range(ncores):
        m = {"x": np.ascontiguousarray(inputs["x"][c]), "mem": np.ascontiguousarray(inputs["mem"][c]),
             "consts": consts, "selc": make_sel()}
        for n in WNAMES:
            a = np.asarray(inputs[n])
            m[n] = np.ascontiguousarray(a.reshape(WSHAPES[n]))
        in_maps.append(m)
    res = run_bass_kernel_spmd(kb.nc, in_maps, core_ids=list(range(ncores)))
    return res


ALL_STAGES = ("ffn1", "mix", "mem", "ffn2")


def kernel(**inputs):
    inputs = {k: np.asarray(v) for k, v in inputs.items()}
    res = run(inputs, ALL_STAGES)
    return np.stack([r["out"] for r in res.results], axis=0).astype(np.float32)
```
